# Optimizing a Trainium2 kernel written in Bass

```python
import math
import jax
import jax.numpy as jnp
from jax import lax
import numpy as np

D_MODEL = 4096
BATCH = 2
SEQ = 8192
DEPTH = 2

GRID_W = 64
CTX_LEN = 256
CHUNK = 128
CONV_W = 5
EPS = 1e-6
F32 = jnp.float32

SSD_HEADS = 32
SSD_HEAD_DIM = 64
SSD_INNER = SSD_HEADS * SSD_HEAD_DIM
SSD_GROUPS = 4
SSD_GROUP_HEADS = SSD_HEADS // SSD_GROUPS
SSD_STATE = 128
SSD_CONV_DIM = SSD_INNER + 2 * SSD_GROUPS * SSD_STATE

RET_HEADS = 16
RET_QK_DIM = 64
RET_V_DIM = 128
RET_QK = RET_HEADS * RET_QK_DIM
RET_V = RET_HEADS * RET_V_DIM
ROPE_BASE = 10000.0

EVEN_SPLITS = (SSD_INNER, SSD_CONV_DIM, 2 * SSD_HEADS, RET_QK, RET_QK, RET_V, RET_V)
EVEN_IN = sum(EVEN_SPLITS)
EVEN_MIX = SSD_INNER + RET_V

GDN_K_HEADS = 16
GDN_V_HEADS = 32
GDN_HEAD_DIM = 128
GDN_K = GDN_K_HEADS * GDN_HEAD_DIM
GDN_V = GDN_V_HEADS * GDN_HEAD_DIM
GDN_CONV_DIM = 2 * GDN_K + GDN_V
ODD_SPLITS = (GDN_CONV_DIM, GDN_V, 2 * GDN_V_HEADS, 2 * GDN_V_HEADS)
ODD_IN = sum(ODD_SPLITS)

D_FF = -(-8 * D_MODEL // (3 * 256)) * 256

kernel_name = "hybrid_ssd_retention_gdn_prefix_backbone"


def _split(t, sizes):
    return jnp.split(t, np.cumsum(sizes)[:-1].tolist(), axis=-1)


def rmsnorm(t, w):
    t32 = t.astype(F32)
    y = t32 * lax.rsqrt(jnp.mean(t32 * t32, axis=-1, keepdims=True) + EPS)
    return (y * w.astype(F32)).astype(t.dtype)


def l2norm(t):
    t32 = t.astype(F32)
    return (t32 * lax.rsqrt(jnp.sum(t32 * t32, axis=-1, keepdims=True) + EPS)).astype(t.dtype)


def head_layernorm(t):
    t32 = t.astype(F32)
    mu = jnp.mean(t32, axis=-1, keepdims=True)
    var = jnp.mean(jnp.square(t32 - mu), axis=-1, keepdims=True)
    return ((t32 - mu) * lax.rsqrt(var + EPS)).astype(t.dtype)


def modulate(h, shift, scale):
    return h * (1.0 + scale) + shift


def dwconv(t, w):
    ch = t.shape[-1]
    return lax.conv_general_dilated(
        t, w[:, None, :].astype(t.dtype), window_strides=(1,),
        padding=[(CONV_W // 2, CONV_W // 2)],
        dimension_numbers=("NWC", "WIO", "NWC"), feature_group_count=ch)


def axial_rotary(t):
    l = t.shape[1]
    rows = l // GRID_W
    row = jnp.repeat(jnp.arange(rows, dtype=F32), GRID_W)
    col = jnp.tile(jnp.arange(GRID_W, dtype=F32), rows)
    nf = RET_QK_DIM // 4
    freqs = ROPE_BASE ** (-jnp.arange(nf, dtype=F32) / nf)
    ang = jnp.concatenate([row[:, None] * freqs, col[:, None] * freqs], axis=-1)
    cos = jnp.cos(ang)[None, :, None, :]
    sin = jnp.sin(ang)[None, :, None, :]
    half = RET_QK_DIM // 2
    t1, t2 = t[..., :half], t[..., half:]
    return jnp.concatenate([t1 * cos - t2 * sin, t2 * cos + t1 * sin], axis=-1).astype(t.dtype)


def decay_scan(q, k, v, log_a, h0):
    b, l, g, dk = q.shape
    r, dv = v.shape[-2:]
    n = l // CHUNK
    qc = q.reshape(b, n, CHUNK, g, dk)
    kc = k.reshape(b, n, CHUNK, g, dk)
    vc = v.reshape(b, n, CHUNK, g, r, dv)
    cs = jnp.cumsum(log_a.astype(F32).reshape(b, n, CHUNK, g, r), axis=2)
    lower = jnp.tril(jnp.ones((CHUNK, CHUNK), bool))[None, None, :, :, None, None]
    decay = jnp.exp(jnp.where(lower, cs[:, :, :, None] - cs[:, :, None, :], -jnp.inf))
    scores = jnp.einsum("bnigk,bnjgk->bnijg", qc, kc)
    y = jnp.einsum("bnijgr,bnjgrv->bnigrv", scores[..., None] * decay, vc)
    v_end = vc * jnp.exp(cs[:, :, -1:] - cs)[..., None]
    states = jnp.einsum("bncgk,bncgrv->bngrkv", kc, v_end)
    chunk_decay = jnp.exp(cs[:, :, -1])

    def carry_chunk(h, inp):
        st, dec = inp
        return h * dec[..., None, None] + st, h

    h_last, h_in = lax.scan(carry_chunk, h0.astype(states.dtype),
                            (jnp.moveaxis(states, 1, 0), jnp.moveaxis(chunk_decay, 1, 0)))
    h_in = jnp.moveaxis(h_in, 0, 1)
    y = y + jnp.einsum("bnigk,bngrkv->bnigrv", qc, h_in) * jnp.exp(cs)[..., None]
    return y.reshape(b, l, g, r, dv).astype(q.dtype), h_last


def delta_scan(q, k, v, g, beta, h0):
    b, l, h, dk = q.shape
    dv = v.shape[-1]
    n = l // CHUNK

    def to_chunks(t):
        return jnp.moveaxis(t.reshape((b, n, CHUNK, h) + t.shape[3:]), 3, 2)

    qc, kc, vc = to_chunks(q), to_chunks(k), to_chunks(v)
    bc = to_chunks(beta.astype(F32))
    cs = jnp.cumsum(to_chunks(g.astype(F32)), axis=-1)
    incl = jnp.tril(jnp.ones((CHUNK, CHUNK), bool))
    strict = jnp.tril(jnp.ones((CHUNK, CHUNK), bool), -1)
    gam = jnp.exp(jnp.where(incl, cs[..., :, None] - cs[..., None, :], -jnp.inf))
    kk = jnp.einsum("bnhik,bnhjk->bnhij", kc, kc)
    a_mat = jnp.where(strict, kk * gam * bc[..., :, None], 0.0) + jnp.eye(CHUNK, dtype=F32)
    rhs = jnp.concatenate([vc * bc[..., None], kc * (bc * jnp.exp(cs))[..., None]], axis=-1)
    sol = lax.linalg.triangular_solve(a_mat, rhs.astype(a_mat.dtype), left_side=True, lower=True,
                                      unit_diagonal=True)
    u, w = sol[..., :dv], sol[..., dv:]
    att = jnp.einsum("bnhik,bnhjk->bnhij", qc, kc) * gam

    def step(s, inp):
        q_i, k_i, u_i, w_i, cs_i, att_i = inp
        v_new = u_i - jnp.einsum("bhck,bhkv->bhcv", w_i, s)
        o = (jnp.einsum("bhck,bhkv->bhcv", q_i * jnp.exp(cs_i)[..., None], s)
             + jnp.einsum("bhij,bhjv->bhiv", att_i, v_new))
        k_end = k_i * jnp.exp(cs_i[..., -1:] - cs_i)[..., None]
        s = s * jnp.exp(cs_i[..., -1])[..., None, None] + jnp.einsum("bhck,bhcv->bhkv", k_end, v_new)
        return s, o

    xs = tuple(jnp.moveaxis(t, 1, 0) for t in (qc, kc, u, w, cs, att))
    s_last, o = lax.scan(step, h0.astype(F32), xs)
    o = jnp.moveaxis(jnp.moveaxis(o, 0, 1), 2, 3).reshape(b, l, h, dv)
    return o.astype(v.dtype), s_last


def two_pass(scan_fn, ctx_args, lat_args, h0, reverse):
    if reverse:
        ctx_args = tuple(jnp.flip(a, axis=1) for a in ctx_args)
        lat_args = tuple(jnp.flip(a, axis=1) for a in lat_args)
    y_c, h_c = scan_fn(*ctx_args, h0)
    y_x, _ = scan_fn(*lat_args, h_c)
    if reverse:
        y_c, y_x = jnp.flip(y_c, axis=1), jnp.flip(y_x, axis=1)
    return y_c, y_x


def even_features(h, w_in, conv_w, conv_b, dt_bias, a_log, rotary):
    b, l, _ = h.shape
    z, xbc, dt, rq, rk, rv, rg = _split(h @ w_in, EVEN_SPLITS)
    xbc = jax.nn.silu(dwconv(xbc, conv_w) + conv_b)
    xs, bm, cm = _split(xbc, (SSD_INNER, SSD_GROUPS * SSD_STATE, SSD_GROUPS * SSD_STATE))
    xs = xs.reshape(b, l, SSD_GROUPS, SSD_GROUP_HEADS, SSD_HEAD_DIM)
    bm = bm.reshape(b, l, SSD_GROUPS, SSD_STATE)
    cm = cm.reshape(b, l, SSD_GROUPS, SSD_STATE)
    dt = jax.nn.softplus(dt.reshape(b, l, 2, SSD_HEADS).astype(F32) + dt_bias.astype(F32))
    la = (-jnp.exp(a_log.astype(F32)) * dt).reshape(b, l, 2, SSD_GROUPS, SSD_GROUP_HEADS)
    dt = dt.reshape(b, l, 2, SSD_GROUPS, SSD_GROUP_HEADS)
    v_dirs = [(xs * dt[:, :, d][..., None]).astype(xs.dtype) for d in range(2)]
    la_dirs = [la[:, :, d] for d in range(2)]
    rq = rq.reshape(b, l, RET_HEADS, RET_QK_DIM)
    rk = rk.reshape(b, l, RET_HEADS, RET_QK_DIM) * RET_QK_DIM ** -0.5
    if rotary:
        rq, rk = axial_rotary(rq), axial_rotary(rk)
    rv = rv.reshape(b, l, RET_HEADS, 1, RET_V_DIM)
    return {"z": z, "xs": xs, "bm": bm, "cm": cm, "v": v_dirs, "la": la_dirs,
            "rq": rq, "rk": rk, "rv": rv, "rg": rg}


def even_out(f, y_ssd, y_ret, d_skip, ssd_norm, w_out):
    b, l = y_ssd.shape[:2]
    y = y_ssd + d_skip.reshape(SSD_GROUPS, SSD_GROUP_HEADS)[..., None] * f["xs"]
    y = y.reshape(b, l, SSD_INNER) * jax.nn.silu(f["z"])
    gsz = SSD_INNER // SSD_GROUPS
    y = rmsnorm(y.reshape(b, l, SSD_GROUPS, gsz), ssd_norm.reshape(SSD_GROUPS, gsz)).reshape(b, l, SSD_INNER)
    r = head_layernorm(y_ret[:, :, :, 0]).reshape(b, l, RET_V) * jax.nn.silu(f["rg"])
    return jnp.concatenate([y, r], axis=-1) @ w_out


def even_mixer(hc, hx, w_in, conv_w, conv_b, dt_bias, a_log, d_skip, ssd_norm, ret_decay, w_out, ctx_out):
    fc = even_features(hc, w_in, conv_w, conv_b, dt_bias, a_log, False)
    fx = even_features(hx, w_in, conv_w, conv_b, dt_bias, a_log, True)
    b = hx.shape[0]
    log_gamma = jax.nn.log_sigmoid(ret_decay.astype(F32))
    ssd_h0 = jnp.zeros((b, SSD_GROUPS, SSD_GROUP_HEADS, SSD_STATE, SSD_HEAD_DIM), F32)
    ret_h0 = jnp.zeros((b, RET_HEADS, 1, RET_QK_DIM, RET_V_DIM), F32)
    ssd_c = ssd_x = ret_c = ret_x = 0.0
    for d in range(2):
        yc, yx = two_pass(decay_scan, (fc["cm"], fc["bm"], fc["v"][d], fc["la"][d]),
                          (fx["cm"], fx["bm"], fx["v"][d], fx["la"][d]), ssd_h0, d == 1)
        ssd_c, ssd_x = ssd_c + yc, ssd_x + yx
        la_c = jnp.broadcast_to(log_gamma[d][:, None], fc["rq"].shape[:3] + (1,))
        la_x = jnp.broadcast_to(log_gamma[d][:, None], fx["rq"].shape[:3] + (1,))
        yc, yx = two_pass(decay_scan, (fc["rq"], fc["rk"], fc["rv"], la_c),
                          (fx["rq"], fx["rk"], fx["rv"], la_x), ret_h0, d == 1)
        ret_c, ret_x = ret_c + yc, ret_x + yx
    out_x = even_out(fx, ssd_x, ret_x, d_skip, ssd_norm, w_out)
    out_c = even_out(fc, ssd_c, ret_c, d_skip, ssd_norm, w_out) if ctx_out else None
    return out_c, out_x


def odd_features(h, w_in, conv_w, dt_bias, a_log):
    b, l, _ = h.shape
    qkv, z, beta, a = _split(h @ w_in, ODD_SPLITS)
    qkv = jax.nn.silu(dwconv(qkv, conv_w))
    q, k, v = _split(qkv, (GDN_K, GDN_K, GDN_V))
    rep = GDN_V_HEADS // GDN_K_HEADS
    q = jnp.repeat(l2norm(q.reshape(b, l, GDN_K_HEADS, GDN_HEAD_DIM)) * GDN_HEAD_DIM ** -0.5, rep, axis=2)
    k = jnp.repeat(l2norm(k.reshape(b, l, GDN_K_HEADS, GDN_HEAD_DIM)), rep, axis=2)
    v = v.reshape(b, l, GDN_V_HEADS, GDN_HEAD_DIM)
    beta = jax.nn.sigmoid(beta.reshape(b, l, 2, GDN_V_HEADS).astype(F32))
    g = -jnp.exp(a_log.astype(F32)) * jax.nn.softplus(a.reshape(b, l, 2, GDN_V_HEADS).astype(F32)
                                                       + dt_bias.astype(F32))
    return {"q": q, "k": k, "v": v, "z": z, "beta": beta, "g": g}


def odd_out(f, o, norm_w, w_out):
    b, l = o.shape[:2]
    y = rmsnorm(o, norm_w) * jax.nn.silu(f["z"].reshape(b, l, GDN_V_HEADS, GDN_HEAD_DIM))
    return y.reshape(b, l, GDN_V) @ w_out


def odd_mixer(hc, hx, w_in, conv_w, dt_bias, a_log, norm_w, w_out, ctx_out):
    fc = odd_features(hc, w_in, conv_w, dt_bias, a_log)
    fx = odd_features(hx, w_in, conv_w, dt_bias, a_log)
    h0 = jnp.zeros((hx.shape[0], GDN_V_HEADS, GDN_HEAD_DIM, GDN_HEAD_DIM), F32)

    def args(f, d):
        return (f["q"], f["k"], f["v"], f["g"][:, :, d], f["beta"][:, :, d])

    o_c = o_x = 0.0
    for d in range(2):
        yc, yx = two_pass(delta_scan, args(fc, d), args(fx, d), h0, d == 1)
        o_c, o_x = o_c + yc, o_x + yx
    out_x = odd_out(fx, o_x, norm_w, w_out)
    out_c = odd_out(fc, o_c, norm_w, w_out) if ctx_out else None
    return out_c, out_x


def swiglu(h, w1, w3, w2):
    return (jax.nn.silu(h @ w1) * (h @ w3)) @ w2


def residual_tail(s, o, mod, post_mix, pre_ffn, post_ffn, w1, w3, w2):
    s = s + mod[2] * rmsnorm(o, post_mix)
    h = modulate(rmsnorm(s, pre_ffn), mod[3], mod[4])
    return s + mod[5] * rmsnorm(swiglu(h, w1, w3, w2), post_ffn)


def setup_inputs(seed: int = 0) -> dict:
    key = jax.random.key(seed)
    keys = iter(jax.random.split(key, 48))

    def normal(shape, scale):
        return jax.random.normal(next(keys), shape, F32) * scale

    def gain(shape):
        return 1.0 + normal(shape, 0.01)

    def dt_bias(shape):
        dt = jnp.exp(jax.random.uniform(next(keys), shape, F32, math.log(1e-3), math.log(1e-1)))
        return dt + jnp.log(-jnp.expm1(-dt))

    def a_log(shape):
        return jnp.log(jax.random.uniform(next(keys), shape, F32, 1.0, 16.0))

    n_ev = (DEPTH + 1) // 2
    n_od = DEPTH // 2
    d = D_MODEL
    ret_logit = jnp.log(2.0 ** (5.0 + jnp.arange(RET_HEADS, dtype=F32)) - 1.0)
    return {
        "x": normal((BATCH, SEQ, d), 1.0),
        "c": normal((BATCH, d), 1.0),
        "ctx": normal((BATCH, CTX_LEN, d), 1.0),
        "c_ctx": normal((d,), 1.0),
        "ada_w": normal((DEPTH, d, 6 * d), 0.5 * d ** -0.5),
        "ada_b": normal((DEPTH, 6 * d), 0.01),
        "norm_mix_pre": gain((DEPTH, d)),
        "norm_mix_post": gain((DEPTH, d)),
        "norm_ffn_pre": gain((DEPTH, d)),
        "norm_ffn_post": gain((DEPTH, d)),
        "ev_w_in": normal((n_ev, d, EVEN_IN), d ** -0.5),
        "ev_conv_w": normal((n_ev, CONV_W, SSD_CONV_DIM), CONV_W ** -0.5),
        "ev_conv_b": normal((n_ev, SSD_CONV_DIM), 0.01),
        "ev_dt_bias": dt_bias((n_ev, 2, SSD_HEADS)),
        "ev_a_log": a_log((n_ev, 2, SSD_HEADS)),
        "ev_d_skip": gain((n_ev, SSD_HEADS)),
        "ev_ssd_norm": gain((n_ev, SSD_INNER)),
        "ev_ret_decay": ret_logit[None, None, :] + normal((n_ev, 2, RET_HEADS), 0.01),
        "ev_w_out": normal((n_ev, EVEN_MIX, d), EVEN_MIX ** -0.5),
        "od_w_in": normal((n_od, d, ODD_IN), d ** -0.5),
        "od_conv_w": normal((n_od, CONV_W, GDN_CONV_DIM), CONV_W ** -0.5),
        "od_dt_bias": dt_bias((n_od, 2, GDN_V_HEADS)),
        "od_a_log": a_log((n_od, 2, GDN_V_HEADS)),
        "od_norm": gain((n_od, GDN_HEAD_DIM)),
        "od_w_out": normal((n_od, GDN_V, d), GDN_V ** -0.5),
        "ffn_w1": normal((DEPTH, d, D_FF), d ** -0.5),
        "ffn_w3": normal((DEPTH, d, D_FF), d ** -0.5),
        "ffn_w2": normal((DEPTH, D_FF, d), D_FF ** -0.5),
    }


def reference(x, c, ctx, c_ctx, ada_w, ada_b, norm_mix_pre, norm_mix_post, norm_ffn_pre, norm_ffn_post,
              ev_w_in, ev_conv_w, ev_conv_b, ev_dt_bias, ev_a_log, ev_d_skip, ev_ssd_norm, ev_ret_decay,
              ev_w_out, od_w_in, od_conv_w, od_dt_bias, od_a_log, od_norm, od_w_out,
              ffn_w1, ffn_w3, ffn_w2):
    cx = ctx
    for i in range(DEPTH):
        last = i == DEPTH - 1
        j = i // 2
        mod_x = jnp.split((jax.nn.silu(c) @ ada_w[i] + ada_b[i])[:, None, :], 6, axis=-1)
        mod_c = jnp.split(jax.nn.silu(c_ctx) @ ada_w[i] + ada_b[i], 6, axis=-1)
        hx = modulate(rmsnorm(x, norm_mix_pre[i]), mod_x[0], mod_x[1])
        hc = modulate(rmsnorm(cx, norm_mix_pre[i]), mod_c[0], mod_c[1])
        if i % 2 == 0:
            oc, ox = even_mixer(hc, hx, ev_w_in[j], ev_conv_w[j], ev_conv_b[j], ev_dt_bias[j], ev_a_log[j],
                                ev_d_skip[j], ev_ssd_norm[j], ev_ret_decay[j], ev_w_out[j], not last)
        else:
            oc, ox = odd_mixer(hc, hx, od_w_in[j], od_conv_w[j], od_dt_bias[j], od_a_log[j],
                               od_norm[j], od_w_out[j], not last)
        x = residual_tail(x, ox, mod_x, norm_mix_post[i], norm_ffn_pre[i], norm_ffn_post[i],
                          ffn_w1[i], ffn_w3[i], ffn_w2[i])
        if not last:
            cx = residual_tail(cx, oc, mod_c, norm_mix_post[i], norm_ffn_pre[i], norm_ffn_post[i],
                               ffn_w1[i], ffn_w3[i], ffn_w2[i])
    return x
```

```python
import math
from contextlib import ExitStack
import numpy as np
import concourse.bass as bass
import concourse.mybir as mybir
from concourse.bass_utils import run_bass_kernel_spmd

F32 = mybir.dt.float32
BF16 = mybir.dt.bfloat16
AF = mybir.ActivationFunctionType
ALU = mybir.AluOpType
AX = mybir.AxisListType

EPOCH = 20000
SEM_LIMIT = 30000
NCORES = 8
XPAIRS = [[0, 4], [1, 5], [2, 6], [3, 7]]
BGROUPS = [[0, 1, 2, 3], [4, 5, 6, 7]]


class Buf:
    def __init__(self, name, handle=None):
        self.name = name
        self.t = handle
        self.w = []
        self.r = {}
        self.dsem = None
        self.dcnt = 0
        self.dold = []
        self.is_dram = False
        self.wd = {}

    def __getitem__(self, idx):
        return self.t[idx]


class View(Buf):
    def __init__(self, name, parent, i):
        self.name = name
        self.par = parent
        self.t = parent.t
        self.i = i
        self.is_dram = False
        self.is_psum = True

    w = property(lambda self: self.par.w, lambda self, v: setattr(self.par, "w", v))
    r = property(lambda self: self.par.r, lambda self, v: setattr(self.par, "r", v))

    def __getitem__(self, idx):
        return self.t[idx[0], self.i, idx[1]]


class Prog:
    ENG = ("pe", "act", "dve", "pool", "sp")

    def __init__(self, nc):
        self.nc = nc
        self.ops = {e: [] for e in self.ENG}
        self.cnt = {e: 0 for e in self.ENG}
        self.esems = {e: [] for e in self.ENG}
        self.known = {e: {} for e in self.ENG}
        self.stack = None
        self.root = None
        self.last = {}
        self.uid = 0
        self.pool = {}
        self.semval = {}
        self.scope_sems = [[]]
        self.cc_sem = None
        self.cc_n = 0

    def new_sem(self, name, scoped=False, eng=None):
        if scoped and self.pool.get(eng):
            sem = self.pool[eng].pop()
        else:
            self.uid += 1
            sem = self.root.enter_context(self.nc.semaphore(f"{name}_{self.uid}"))
            self.semval[id(sem)] = 0
        if scoped:
            self.scope_sems[-1].append((eng, sem))
        return sem

    def sbuf(self, name, shape, dtype):
        self.uid += 1
        t = self.stack.enter_context(self.nc.sbuf_tensor(f"{name}_{self.uid}", list(shape), dtype))
        return Buf(name, t)

    def psum(self, name, shape, dtype=F32):
        self.uid += 1
        t = self.stack.enter_context(self.nc.psum_tensor(f"{name}_{self.uid}", list(shape), dtype))
        b = Buf(name, t)
        b.is_psum = True
        return b

    def psum_views(self, name, n, w=128, dtype=F32):
        self.uid += 1
        t = self.stack.enter_context(self.nc.psum_tensor(f"{name}_{self.uid}", [128, n, w], dtype))
        par = Buf(name, t)
        par.is_psum = True
        return [View(f"{name}{i}", par, i) for i in range(n)]

    def dram(self, name, shape, dtype, kind="Internal"):
        b = Buf(name, self.nc.dram_tensor(name, list(shape), dtype, kind=kind))
        b.is_dram = True
        return b

    def _esem(self, eng, n):
        k = n // EPOCH
        while len(self.esems[eng]) <= k:
            self.esems[eng].append(self.new_sem(f"s_{eng}"))
        return self.esems[eng][k], (n % EPOCH) + 1

    def _deps(self, reads, writes, skip_waw=False):
        deps = {}

        def add(sem, val):
            k = id(sem)
            if k not in deps or deps[k][1] < val:
                deps[k] = (sem, val)

        for b in reads:
            for (s, v) in b.w:
                add(s, v)
            if getattr(b, "is_psum", False):
                for (s, v) in b.r.values():
                    add(s, v)
        for b in writes:
            if not skip_waw:
                for (s, v) in b.w:
                    add(s, v)
            for (s, v) in b.r.values():
                add(s, v)
        return deps

    def _emit_waits(self, eng, deps, skip_ids=()):
        kn = self.known[eng]
        for k, (s, v) in deps.items():
            if k in skip_ids or kn.get(k, 0) >= v:
                continue
            kn[k] = v
            self.ops[eng].append(("wait", s, v))

    def _note(self, sem, val):
        self.last[id(sem)] = (sem, val)

    def op(self, eng, fn, reads=(), writes=()):
        reads = [b for b in reads if b is not None]
        writes = [b for b in writes if b is not None]
        deps = self._deps(reads, writes)
        skip = {id(s) for s in self.esems["pe"]} if eng == "pe" else ()
        self._emit_waits(eng, deps, skip)
        n = self.cnt[eng]
        self.cnt[eng] += 1
        sem, val = self._esem(eng, n)
        self.ops[eng].append(("op", fn, sem, 1))
        rec = (sem, val)
        self._note(sem, val)
        for b in writes:
            b.w = [rec]
            b.r = {}
        for b in reads:
            if b not in writes:
                b.r[id(sem)] = rec

    def dma(self, eng, out_ap, in_ap, dst, src, scratch=False):
        deps = self._deps([src], [dst], skip_waw=(scratch or dst.is_dram))
        self._emit_waits(eng, deps)
        own = dst if not dst.is_dram else (src if not src.is_dram else dst)
        if own.dsem is None:
            own.dsem = {}
        cur = own.dsem.get(eng)
        if cur is None or self.semval[id(cur)] >= SEM_LIMIT:
            if cur is not None:
                own.dold.append((cur, self.semval[id(cur)]))
            own.dsem[eng] = self.new_sem(f"d{eng}_{own.name}", scoped=not own.is_dram, eng=eng)
        sem = own.dsem[eng]
        self.semval[id(sem)] += 16
        val = self.semval[id(sem)]

        def fn(e, out_ap=out_ap, in_ap=in_ap):
            return e.dma_start(out=out_ap, in_=in_ap)

        self.ops[eng].append(("op", fn, sem, 16))
        rec = (sem, val)
        self._note(sem, val)
        if dst.is_dram:
            dst.wd[id(sem)] = rec
            dst.w = list(dst.wd.values())
        else:
            dst.w = list(dst.dold) + [rec]
            dst.r = {}
        if src is not dst:
            src.r[id(sem)] = rec

    def coll(self, kind, groups, out_ap, in_ap, dst, src):
        deps = self._deps([src], [dst], skip_waw=True)
        self._emit_waits("pool", deps)
        if self.cc_sem is None or self.cc_n >= SEM_LIMIT:
            self.cc_sem = self.new_sem("cc")
            self.cc_n = 0
        sem = self.cc_sem
        self.cc_n += 1
        n = self.cc_n

        def fn(e):
            return e.collective_compute(kind, ALU.bypass, replica_groups=groups, ins=[in_ap], outs=[out_ap])

        self.ops["pool"].append(("op", fn, sem, None))
        self.ops["pool"].append(("wait", sem, n))
        self.known["pool"][id(sem)] = n
        rec = (sem, n)
        self._note(sem, n)
        dst.wd[id(sem)] = rec
        dst.w = list(dst.wd.values())
        src.r[id(sem)] = rec

    def wait_all(self, eng, bufs):
        self._emit_waits(eng, self._deps(bufs, []))

    def barrier(self):
        deps = dict(self.last)
        for eng in self.ENG:
            skip = {id(s) for s in self.esems[eng]} if eng == "pe" else ()
            self._emit_waits(eng, dict(deps), skip)

    def emit(self, block):
        engmap = {"pe": "tensor", "act": "scalar", "dve": "vector", "pool": "gpsimd", "sp": "sync"}
        for eng in self.ENG:
            ops = self.ops[eng]
            if not ops:
                continue

            def body(e, ops=ops):
                for o in ops:
                    if o[0] == "wait":
                        e.wait_ge(o[1], o[2])
                    else:
                        ins = o[1](e)
                        if o[3] is None:
                            ins.then_inc(o[2])
                        else:
                            ins.then_inc(o[2], o[3])

            getattr(block, engmap[eng])(body)


class Cfg:
    def __init__(self, D=4096, SEQ=8192, DFF=None):
        self.D = D
        self.SEQ = SEQ
        self.CTX = 256
        self.KC = D // 128
        self.NTL = SEQ // 4
        self.NT = self.NTL + 64
        self.TA = SEQ + 256
        self.DFF = DFF if DFF is not None else -(-8 * D // (3 * 256)) * 256
        self.MODW = 6 * D // 8
        self.GW = math.gcd(self.MODW, 512)
        self.GRID_W = 64
        self.EV_FM = 1280
        self.EV_TM = 1552
        self.OD_FM = 2048
        self.OD_TM = 1056

    def token_tiles(self):
        out = [(0, 64)]
        r = 64
        while r < self.NT:
            n = min(512, self.NT - r)
            out.append((r, n))
            r += n
        return out


def even_cols(g):
    o_z, o_xbc, o_dt = 0, 2048, 2048 + 3072
    o_rq = o_dt + 64
    o_rk = o_rq + 1024
    o_rv = o_rk + 1024
    o_rg = o_rv + 2048
    r = np.arange
    xs = o_xbc + g * 512 + r(512)
    bm = o_xbc + 2048 + g * 128 + r(128)
    cm = o_xbc + 2048 + 512 + g * 128 + r(128)
    rq = o_rq + g * 256 + r(256)
    rk = o_rk + g * 256 + r(256)
    z = o_z + g * 512 + r(512)
    rv = o_rv + g * 512 + r(512)
    rg = o_rg + g * 512 + r(512)
    dt = np.concatenate([o_dt + d * 32 + g * 8 + r(8) for d in range(2)])
    return np.concatenate([xs, bm, cm, rq, rk, z, rv, rg, dt])


def odd_cols(g):
    r = np.arange
    q = g * 512 + r(512)
    k = 2048 + g * 512 + r(512)
    v = 4096 + g * 1024 + r(1024)
    z = 8192 + g * 1024 + r(1024)
    o_b = 8192 + 4096
    beta = np.concatenate([o_b + d * 32 + g * 8 + r(8) for d in range(2)])
    a = np.concatenate([o_b + 64 + d * 32 + g * 8 + r(8) for d in range(2)])
    return np.concatenate([q, k, v, z, beta, a])


EPS = 1e-6


class Ctx:
    pass


def sub(P):
    st = ExitStack()
    return st


class Scope:
    def __init__(self, P):
        self.P = P

    def __enter__(self):
        self.prev = self.P.stack
        self.st = ExitStack()
        self.P.stack = self.st
        self.P.scope_sems.append([])
        return self

    def __exit__(self, *a):
        P = self.P
        P.barrier()
        for eng, sem in P.scope_sems.pop():
            if P.semval[id(sem)] < SEM_LIMIT:
                P.pool.setdefault(eng, []).append(sem)
        self.st.close()
        P.stack = self.prev
        return False


def phase_ada(C):
    P, cfg, T = C.P, C.cfg, C.T
    D, KC, MODW = cfg.D, cfg.KC, cfg.MODW
    GA = 512 if MODW % 512 == 0 else MODW
    kg = min(8, KC)
    with Scope(P):
        sv0 = P.sbuf("sv0", [128, KC, 3], F32)
        sv = P.sbuf("sv", [128, KC, 3], F32)
        for k8 in range(0, KC, 8):
            ke = min(KC, k8 + 8)
            P.dma("sp", sv0[:, k8:ke, :], T["cv3T"].t[k8 * 128:ke * 128, :].rearrange("(kc p) r -> p kc r", p=128), sv0, T["cv3T"])
        P.op("act", lambda e: e.activation(sv[:, :, :], sv0[:, :, :], AF.Silu), [sv0], [sv])
        wts = [P.sbuf(f"adaw{i}", [128, kg, GA], F32) for i in range(2)]
        bias = P.sbuf("adab", [3, 2 * MODW], F32)
        modsb = P.sbuf("modsb", [3, 2 * MODW], F32)
        ps = [P.psum(f"adaps{i}", [3, GA]) for i in range(2)]
        P.dma("sp", bias[:, :], T["ada_b"].t.ap().rearrange("l m -> (l m)").partition_broadcast(3), bias, T["ada_b"])
        it = 0
        for l in range(2):
            for j in range(MODW // GA):
                pt = ps[(l * (MODW // GA) + j) % 2]
                for k0 in range(0, KC, kg):
                    wt = wts[it % 2]
                    it += 1
                    P.dma("sp", wt[:, :, :],
                          T["ada_w"].t[l, k0 * 128:(k0 + kg) * 128, j * GA:(j + 1) * GA].rearrange("(kc p) n -> p kc n", p=128),
                          wt, T["ada_w"])
                    for kk in range(kg):
                        kc = k0 + kk
                        P.op("pe", lambda e, pt=pt, wt=wt, kk=kk, kc=kc: e.matmul(
                            pt[:, :], sv[:, kc, :], wt[:, kk, :], start=(kc == 0), stop=(kc == KC - 1)),
                            [sv, wt], [pt])
                c0 = l * MODW + j * GA
                P.op("dve", lambda e, pt=pt, c0=c0: e.tensor_tensor(modsb[:, c0:c0 + GA], pt[:, :], bias[:, c0:c0 + GA], ALU.add),
                     [pt, bias], [modsb])
        P.dma("pool", T["mod_loc"][:, :], modsb[:, :], T["mod_loc"], modsb)
        P.coll("AllGather", XPAIRS, T["mod_pair"].t.ap().opt(), T["mod_loc"].t.ap().opt(), T["mod_pair"], T["mod_loc"])
        P.coll("AllGather", BGROUPS, T["mod_all"].t.ap().opt(), T["mod_pair"].t.ap().opt(), T["mod_all"], T["mod_pair"])


def phase_vecs(C):
    P, cfg, T = C.P, C.cfg, C.T
    D, MODW = cfg.D, cfg.MODW
    GW = min(512, D)
    CB = min(1024, D)
    with Scope(P):
        sel2 = P.sbuf("sel2", [3, 2], F32)
        P.dma("sp", sel2[:, :], T["sel2"][:, :], sel2, T["sel2"])
        rows = [P.sbuf(f"mrow{i}", [3, CB], F32) for i in range(2)]
        selv = [P.sbuf(f"selv{i}", [2, CB], F32) for i in range(6)]
        nw = [P.sbuf(f"nw{i}", [2, CB], F32) for i in range(4)]
        outv = [P.sbuf(f"outv{i}", [2, CB], F32) for i in range(2)]
        ps = [P.psum(f"vps{i}", [2, GW]) for i in range(2)]
        n = 0
        for l in range(2):
            for cb0 in range(0, D, CB):
                for k in range(4):
                    P.dma("sp", nw[k][:, :], T["norms"].t[l, k, cb0:cb0 + CB].partition_broadcast(2), nw[k], T["norms"])
                for v in range(6):
                    row = rows[v % 2]
                    x = 0
                    while x < CB:
                        Gc = v * D + cb0 + x
                        s_, off = Gc // MODW, Gc % MODW
                        ln = min(CB - x, MODW - off)
                        P.dma("sp", row[:, x:x + ln], T["mod_all"].t[s_ * 3:(s_ + 1) * 3, l * MODW + off:l * MODW + off + ln],
                              row, T["mod_all"])
                        x += ln
                    for j in range(CB // GW):
                        pt = ps[n % 2]
                        n += 1
                        P.op("pe", lambda e, pt=pt, row=row, j=j: e.matmul(pt[:, :], sel2[:, :], row[:, j * GW:(j + 1) * GW],
                                                                        start=True, stop=True), [sel2, row], [pt])
                        P.op("dve", lambda e, pt=pt, v=v, j=j: e.tensor_copy(selv[v][:, j * GW:(j + 1) * GW], pt[:, :]), [pt], [selv[v]])
                combos = [(0, 1, 0, "scale"), (1, 0, None, "copy"), (2, 2, 1, "gate"),
                          (3, 4, 2, "scale"), (4, 3, None, "copy"), (5, 5, 3, "gate")]
                for (oi, mi, wi, kind) in combos:
                    ov = outv[oi % 2]
                    if kind == "scale":
                        P.op("dve", lambda e, ov=ov, mi=mi, wi=wi: e.scalar_tensor_tensor(
                            ov[:, :], selv[mi][:, :], 1.0, nw[wi][:, :], ALU.add, ALU.mult), [selv[mi], nw[wi]], [ov])
                    elif kind == "gate":
                        P.op("dve", lambda e, ov=ov, mi=mi, wi=wi: e.tensor_tensor(ov[:, :], selv[mi][:, :], nw[wi][:, :], ALU.mult),
                             [selv[mi], nw[wi]], [ov])
                    else:
                        P.op("dve", lambda e, ov=ov, mi=mi: e.tensor_copy(ov[:, :], selv[mi][:, :]), [selv[mi]], [ov])
                    P.dma("pool", T["vecs"].t[l, oi, :, cb0:cb0 + CB], ov[:, :], T["vecs"], ov, scratch=True)


def make_ident(C):
    P = C.P
    identf = P.sbuf("identf", [128, 128], F32)
    ident = P.sbuf("ident", [128, 128], BF16)
    P.op("pool", lambda e: e.memset(identf[:, :], 0.0), [], [identf])
    P.op("pool", lambda e: e.affine_select(identf[:, :], identf[:, :], pattern=[[-1, 128]], compare_op=ALU.not_equal,
                                           fill=1.0, base=0, channel_multiplier=1), [identf], [identf])
    P.op("pool", lambda e: e.tensor_copy(ident[:, :], identf[:, :]), [identf], [ident])
    C.ident = ident
    C.identf = identf


def rstd_from_ss(P, rstd, ss, n, inv_d):
    P.op("dve", lambda e: e.tensor_scalar(rstd[:n, :], ss[:n, :], inv_d, EPS, ALU.mult, ALU.add), [ss], [rstd])
    P.op("act", lambda e: e.activation(rstd[:n, :], rstd[:n, :], AF.Sqrt), [rstd], [rstd])
    P.op("dve", lambda e: e.reciprocal(rstd[:n, :], rstd[:n, :]), [rstd], [rstd])


def transpose_rows(C, src, n, nfeat, dstT, pst, col0=0):
    P = C.P
    nk = nfeat // 128
    for k0 in range(0, nk, 4):
        pt = pst[(k0 // 4) % len(pst)]
        kk = min(4, nk - k0)
        for q in range(kk):
            kc = k0 + q
            P.op("pe", lambda e, pt=pt, q=q, kc=kc: e.transpose(pt[:, q, :n], src[:n, kc * 128:(kc + 1) * 128], C.ident[:n, :n]),
                 [src, C.ident], [pt])
        P.op("act", lambda e, pt=pt, k0=k0, kk=kk: e.activation(dstT[:, k0:k0 + kk, col0:col0 + n], pt[:, :kk, :n], AF.Copy),
             [pt], [dstT])


def phase_prenorm(C, l, xin):
    P, cfg, T = C.P, C.cfg, C.T
    D, KC, NT = cfg.D, cfg.KC, cfg.NT
    with Scope(P):
        g1 = P.sbuf("g1", [128, D], F32)
        sh1 = P.sbuf("sh1", [128, D], F32)
        xt = [P.sbuf(f"xt{i}", [128, D], F32) for i in range(2)]
        junk = P.sbuf("junk", [128, D], BF16)
        tt = P.sbuf("tt", [128, D], F32)
        hb = [P.sbuf(f"hb{i}", [128, D], BF16) for i in range(2)]
        hT = [P.sbuf(f"hT{i}", [128, KC, 128], BF16) for i in range(2)]
        ss = P.sbuf("ss", [128, 1], F32)
        rstd = P.sbuf("rstd", [128, 1], F32)
        pst = [P.psum(f"tps{i}", [128, 4, 128], BF16) for i in range(2)]
        tiles = [(0, 64, 1)] + [(r, 128, 0) for r in range(64, NT, 128)]
        curvar = None
        for i, (r0, n, var) in enumerate(tiles):
            if var != curvar:
                P.dma("sp", g1[:, :], T["vecs"].t[l, 0, var, :].partition_broadcast(128), g1, T["vecs"])
                P.dma("sp", sh1[:, :], T["vecs"].t[l, 1, var, :].partition_broadcast(128), sh1, T["vecs"])
                curvar = var
            x = xt[i % 2]
            h = hb[i % 2]
            ht = hT[i % 2]
            P.dma("sp", x[:n, :], xin[r0:r0 + n, :], x, xin)
            P.op("act", lambda e, x=x, n=n: e.activation(junk[:n, :], x[:n, :], AF.Square, accum_out=ss[:n, :]), [x], [junk, ss])
            rstd_from_ss(P, rstd, ss, n, 1.0 / D)
            P.op("dve", lambda e, x=x, n=n: e.scalar_tensor_tensor(tt[:n, :], x[:n, :], rstd[:n, 0:1], g1[:n, :], ALU.mult, ALU.mult),
                 [x, rstd, g1], [tt])
            P.op("pool", lambda e, h=h, n=n: e.tensor_tensor(h[:n, :], tt[:n, :], sh1[:n, :], ALU.add), [tt, sh1], [h])
            transpose_rows(C, h, n, D, ht, pst)
            for k8 in range(0, KC, 8):
                ke = min(KC, k8 + 8)
                P.dma("pool", T["hT_loc"].t[k8 * 128:ke * 128, r0:r0 + n].rearrange("(kc p) t -> p kc t", p=128), ht[:, k8:ke, :n],
                      T["hT_loc"], ht, scratch=True)
        ag_chunks(C, BGROUPS, 4, T["hallT"], T["hT_loc"], 128)


def declare_tensors(C):
    P, cfg = C.P, C.cfg
    D, NT, MODW, DFF = cfg.D, cfg.NT, cfg.MODW, cfg.DFF
    T = {}
    ext = lambda n, s, dt=F32: T.__setitem__(n, P.dram(n, s, dt, kind="ExternalInput"))
    ext("xin", [NT, D])
    ext("cv3T", [D, 3])
    ext("sel2", [3, 2])
    ext("ada_w", [2, D, MODW])
    ext("ada_b", [2, MODW])
    ext("norms", [2, 4, D])
    itn = lambda n, s, dt=F32: T.__setitem__(n, P.dram(n, s, dt))
    itn("mod_loc", [3, 2 * MODW])
    itn("mod_pair", [6, 2 * MODW])
    itn("mod_all", [NCORES * 3, 2 * MODW])
    itn("vecs", [2, 6, 2, D])
    itn("hT_loc", [D, NT], BF16)
    itn("hallT", [4 * D, NT], BF16)
    NIN = max(cfg.EV_FM + cfg.EV_TM, cfg.OD_FM + cfg.OD_TM)
    ext("ev_w_in", [D, cfg.EV_FM + cfg.EV_TM])
    ext("od_w_in", [D, cfg.OD_FM + cfg.OD_TM])
    ext("od_cw", [128, 16, 5])
    ext("od_vec", [160])
    itn("w_in_bf", [D, NIN], BF16)
    itn("fm_pre", [max(cfg.EV_FM, cfg.OD_FM), cfg.TA])
    itn("tm_pre", [cfg.TA, max(cfg.EV_TM, cfg.OD_TM)])
    C.ncons = make_consts(cfg).shape[1]
    ext("consts", [128, C.ncons])
    ext("rot_cos", [128, cfg.SEQ])
    ext("rot_sin", [128, cfg.SEQ])
    ext("ev_cw", [128, 6, 5])
    ext("ev_cb", [128, 6])
    ext("ev_vec", [1088])
    itn("xs_keep", [cfg.TA, 512], BF16)
    itn("yd", [2, cfg.TA, 1024])
    itn("yT_loc", [4 * 1024, cfg.NT], BF16)
    itn("yallT", [16 * 1024, cfg.NT], BF16)
    ext("w_out", [2, 4096 // 4, D])
    ext("ffn_w1", [2, D // 4, DFF])
    ext("ffn_w3", [2, D // 4, DFF])
    ext("ffn_w2", [2, DFF // 4, D])
    for l in range(2):
        for nm, rows, cols in (("wo", 4096, D), ("w1", D, DFF), ("w3", D, DFF), ("w2", DFF, D)):
            itn(f"{nm}_s{l}", [rows // 4, cols], BF16)
            itn(f"{nm}_f{l}", [rows, cols], BF16)
    itn("o_loc", [NT, D])
    itn("s_loc", [NT, D])
    itn("f_loc", [NT, D])
    itn("x1_loc", [NT, D])
    itn("h2T_loc", [D, NT], BF16)
    T["out"] = P.dram("out", [cfg.NTL, D], F32, kind="ExternalOutput")
    C.T = T


def build_program(cfg, stages=("ada", "vecs", "prenorm0"), dumps=()):
    nc = bass.Bass(target_bir_lowering=False)
    C = Ctx()
    C.cfg = cfg
    C.P = P = Prog(nc)
    declare_tensors(C)
    T = C.T
    outs = {}
    dumps = [d if isinstance(d, tuple) else (d, None, None) for d in dumps]
    for name, shp, _ in dumps:
        b = T[name]
        outs[name] = P.dram("dump_" + name, list(shp or b.t.shape), b.t.dtype, kind="ExternalOutput")
    with ExitStack() as root:
        P.root = root
        P.stack = root
        make_ident(C)
        C.rk = nc.sync.partition_id() % 4
        if "ada" in stages:
            phase_ada(C)
        if "vecs" in stages:
            phase_vecs(C)
        if "prenorm0" in stages:
            phase_prenorm(C, 0, T["xin"])
        if "inproj0" in stages:
            cast_weight(C, T["w_in_bf"], T["ev_w_in"], cfg.D, cfg.EV_FM + cfg.EV_TM)
            phase_inproj(C, 0, T["w_in_bf"], cfg.EV_FM, cfg.EV_TM)
        if "mixer0" in stages:
            phase_mixer_even(C)
            phase_even_epilogue(C)
        if "tail0" in stages:
            import os
            cut = int(os.environ.get("TAILCUT", "9"))
            ag_chunks(C, BGROUPS, 4, T["yallT"], T["yT_loc"], 128)
            if cut >= 2:
                prep_weights(C, 0)
            if cut >= 3:
                phase_outproj(C, 0, True)
            if cut >= 4:
                phase_postmix(C, 0, T["xin"], True)
            if cut >= 5:
                phase_ffn(C, 0, True)
            if cut >= 6:
                phase_postffn(C, 0, T["x1_loc"], True)
        if "prenorm1x" in stages:
            phase_prenorm(C, 1, T["xin"])
        if "prenorm1" in stages:
            phase_prenorm(C, 1, T["x1_loc"])
        if "inproj1" in stages:
            cast_weight(C, T["w_in_bf"], T["od_w_in"], cfg.D, cfg.OD_FM + cfg.OD_TM)
            phase_inproj(C, 1, T["w_in_bf"], cfg.OD_FM, cfg.OD_TM)
        if "mixer1" in stages:
            phase_mixer_odd(C)
            phase_odd_epilogue(C)
        if "tail1" in stages:
            ag_chunks(C, BGROUPS, 4, T["yallT"], T["yT_loc"], 128)
            prep_weights(C, 1)
            phase_outproj(C, 1, False)
            phase_postmix(C, 1, T["x1_loc"], False)
            phase_ffn(C, 1, False)
            phase_postffn(C, 1, T["out"], False, dst_row_off=64)
            P.wait_all("pool", [T["out"]])
        for name, shp, sl in dumps:
            P.dma("pool", outs[name].t.ap(), sl(T[name].t) if sl else T[name].t.ap(), outs[name], T[name])
        P.wait_all("pool", list(outs.values()))
        P.barrier()
        with nc.Block() as block:
            P.emit(block)
    return nc, C


class GemmRes:
    def __init__(self, C, kg, nbanks=4, nsets=2, nw=3, tag="g"):
        P = C.P
        self.kg = kg
        self.wp = [P.sbuf(f"{tag}wp{i}", [128, kg, 512], BF16) for i in range(nw)]
        self.ps = [[P.psum(f"{tag}ps{s}_{i}", [128, 512]) for i in range(nbanks)] for s in range(nsets)]
        self.nbanks = nbanks
        self.wi = 0
        self.si = 0


def gemm(C, G, mode, xT, T_, W, K, col0, ncols, evac, wrow0=0):
    P = C.P
    KC = K // 128
    kg = G.kg
    ngw = G.nbanks * 128 if mode == "FM" else 512
    for c0 in range(col0, col0 + ncols, ngw):
        nw = min(ngw, col0 + ncols - c0)
        pset = G.ps[G.si % len(G.ps)]
        G.si += 1
        if mode == "FM":
            subs = [(q, c0 + q * 128, min(128, c0 + nw - (c0 + q * 128))) for q in range((nw + 127) // 128)]
        else:
            subs = [(q, q * 128, min(128, T_ - q * 128)) for q in range((T_ + 127) // 128)]
        for k0 in range(0, KC, kg):
            kk = min(kg, KC - k0)
            wp = G.wp[G.wi % len(G.wp)]
            G.wi += 1
            P.dma("sp", wp[:, :kk, :nw],
                  W.t[wrow0 + k0 * 128:wrow0 + (k0 + kk) * 128, c0:c0 + nw].rearrange("(kc p) n -> p kc n", p=128), wp, W)
            for (q, a0, an) in subs:
                pt = pset[q]
                for j in range(kk):
                    kc = k0 + j
                    if mode == "FM":
                        P.op("pe", lambda e, pt=pt, wp=wp, j=j, kc=kc, a0=a0, an=an, c0=c0: e.matmul(
                            pt[:an, :T_], wp[:, j, a0 - c0:a0 - c0 + an], xT[:, kc, :T_], start=(kc == 0), stop=(kc == KC - 1)),
                            [wp, xT], [pt])
                    else:
                        P.op("pe", lambda e, pt=pt, wp=wp, j=j, kc=kc, a0=a0, an=an, nw=nw: e.matmul(
                            pt[:an, :nw], xT[:, kc, a0:a0 + an], wp[:, j, :nw], start=(kc == 0), stop=(kc == KC - 1)),
                            [wp, xT], [pt])
        for (q, a0, an) in subs:
            if mode == "FM":
                evac(a0, pset[q], an)
            else:
                evac(a0, an, c0, nw, pset[q])


def cast_weight(C, dst, src, rows, ncols, rstep=512, drow0=0):
    P = C.P
    for r0 in range(0, rows, rstep):
        rn = min(rstep, rows - r0)
        P.dma("pool", dst.t[drow0 + r0:drow0 + r0 + rn, :ncols], src.t[r0:r0 + rn, :ncols], dst, src, scratch=True)


def seq_blocks(cfg):
    out = []
    for r in range(4):
        out.append((r, 0, 64, r * 64))
        for t0 in range(0, cfg.NTL, 512):
            n = min(512, cfg.NTL - t0)
            out.append((r, 64 + t0, n, 256 + r * cfg.NTL + t0))
    return out


def phase_inproj(C, l, wbf, nfm, ntm):
    P, cfg, T = C.P, C.cfg, C.T
    D, KC = cfg.D, cfg.KC
    with Scope(P):
        G = GemmRes(C, min(8, KC))
        xts = [P.sbuf(f"ipx{i}", [128, KC, 512], BF16) for i in range(2)]
        stg = [P.sbuf(f"ipst{i}", [128, 512], F32) for i in range(4)]
        si = [0]
        for bi, (r, row0, n, sp0) in enumerate(seq_blocks(cfg)):
            xT = xts[bi % 2]
            hv = T["hallT"].t.ap().rearrange("(kc r p) t -> kc r p t", kc=KC, r=4)
            for k8 in range(0, KC, 8):
                ke = min(KC, k8 + 8)
                P.dma("sp", xT[:, k8:ke, :n], hv[k8:ke, r, :, row0:row0 + n].rearrange("kc p t -> p kc t"), xT, T["hallT"])

            def evac_fm(c0, pt, nf, n=n, sp0=sp0):
                st = stg[si[0] % 4]
                eng = "act" if si[0] % 2 == 0 else "dve"
                si[0] += 1
                if eng == "act":
                    P.op("act", lambda e: e.activation(st[:nf, :n], pt[:nf, :n], AF.Copy), [pt], [st])
                else:
                    P.op("dve", lambda e: e.tensor_copy(st[:nf, :n], pt[:nf, :n]), [pt], [st])
                P.dma("pool", T["fm_pre"].t[c0:c0 + nf, sp0:sp0 + n], st[:nf, :n], T["fm_pre"], st)

            def evac_tm(a0, an, c0, nw, pt, n=n, sp0=sp0):
                st = stg[si[0] % 4]
                eng = "act" if si[0] % 2 == 0 else "dve"
                si[0] += 1
                if eng == "act":
                    P.op("act", lambda e: e.activation(st[:an, :nw], pt[:an, :nw], AF.Copy), [pt], [st])
                else:
                    P.op("dve", lambda e: e.tensor_copy(st[:an, :nw], pt[:an, :nw]), [pt], [st])
                P.dma("pool", T["tm_pre"].t[sp0 + a0:sp0 + a0 + an, c0 - nfm:c0 - nfm + nw], st[:an, :nw], T["tm_pre"], st)

            gemm(C, G, "FM", xT, n, wbf, D, 0, nfm, evac_fm)
            gemm(C, G, "TM", xT, n, wbf, D, nfm, ntm, evac_tm)


def make_in_maps(cfg, inp):
    D = cfg.D
    maps = []
    for cidx in range(NCORES):
        b, g = cidx // 4, cidx % 4
        sidx = g * 2 + b
        m = {}
        m["xin"] = np.concatenate([inp["ctx"][b, g * 64:(g + 1) * 64], inp["x"][b, g * cfg.NTL:(g + 1) * cfg.NTL]], 0)
        m["cv3T"] = np.ascontiguousarray(np.stack([inp["c"][0], inp["c"][1], inp["c_ctx"]], 1))
        sel2 = np.zeros((3, 2), np.float32)
        sel2[b, 0] = 1
        sel2[2, 1] = 1
        m["sel2"] = sel2
        m["ada_w"] = np.ascontiguousarray(inp["ada_w"][:, :, sidx * cfg.MODW:(sidx + 1) * cfg.MODW])
        m["ada_b"] = np.ascontiguousarray(inp["ada_b"][:, sidx * cfg.MODW:(sidx + 1) * cfg.MODW])
        m["norms"] = np.ascontiguousarray(np.stack([inp["norm_mix_pre"], inp["norm_mix_post"], inp["norm_ffn_pre"], inp["norm_ffn_post"]], 1))
        ecols = even_cols(g)
        m["ev_w_in"] = np.ascontiguousarray(inp["ev_w_in"][0][:, ecols])
        ocols = odd_cols(g)
        m["od_w_in"] = np.ascontiguousarray(inp["od_w_in"][0][:, ocols])
        m["od_cw"] = np.ascontiguousarray(inp["od_conv_w"][0][:, ocols[:2048]].T.reshape(16, 128, 5).transpose(1, 0, 2))
        vh = slice(g * 8, (g + 1) * 8)
        m["od_vec"] = np.concatenate([inp["od_dt_bias"][0][:, vh].reshape(-1), inp["od_a_log"][0][:, vh].reshape(-1),
                                      inp["od_norm"][0]]).astype(np.float32)
        m["consts"] = make_consts(cfg)
        m["rot_cos"], m["rot_sin"] = rotary_tables(cfg)
        ccols = ecols[:768] - 2048
        m["ev_cw"] = np.ascontiguousarray(inp["ev_conv_w"][0][:, ccols].T.reshape(6, 128, 5).transpose(1, 0, 2))
        m["ev_cb"] = np.ascontiguousarray(inp["ev_conv_b"][0][ccols].reshape(6, 128).T)
        hs = slice(g * 8, (g + 1) * 8)
        rs = slice(g * 4, (g + 1) * 4)
        perm_e = np.concatenate([np.concatenate([gq * 512 + np.arange(512), 2048 + gq * 512 + np.arange(512)]) for gq in range(4)])

        def qshard(W):
            K_, N_ = W.shape
            rn = pick_rn(K_ // 4, N_)
            idx = np.concatenate([(c * 4 + g) * rn + np.arange(rn) for c in range(K_ // 4 // rn)])
            return W[idx]

        m["w_out"] = np.ascontiguousarray(np.stack([qshard(inp["ev_w_out"][0][perm_e]), qshard(inp["od_w_out"][0])]))
        m["ffn_w1"] = np.ascontiguousarray(np.stack([qshard(inp["ffn_w1"][l_]) for l_ in range(2)]))
        m["ffn_w3"] = np.ascontiguousarray(np.stack([qshard(inp["ffn_w3"][l_]) for l_ in range(2)]))
        m["ffn_w2"] = np.ascontiguousarray(np.stack([qshard(inp["ffn_w2"][l_]) for l_ in range(2)]))
        m["ev_vec"] = np.concatenate([
            inp["ev_dt_bias"][0][:, hs].reshape(-1), inp["ev_a_log"][0][:, hs].reshape(-1),
            inp["ev_ret_decay"][0][:, rs].reshape(-1), np.zeros(24, np.float32),
            np.repeat(inp["ev_d_skip"][0][hs], 64), inp["ev_ssd_norm"][0][g * 512:(g + 1) * 512]]).astype(np.float32)
        maps.append(m)
    return maps


NEG = -30000.0
CO = {}


def make_consts(cfg):
    i = np.arange(128)[None, :].astype(np.float32)
    j = np.arange(128)[:, None].astype(np.float32)
    parts = []

    def add(name, arr):
        CO[name] = (sum(p.shape[1] for p in parts), arr.shape[1])
        parts.append(arr.astype(np.float32))

    add("mf", (i >= j) * 1.0)
    add("mr", (i <= j) * 1.0)
    add("nf", np.where(i >= j, 0.0, NEG))
    add("nr", np.where(i <= j, 0.0, NEG))
    add("nfs", np.where(i > j, 0.0, NEG))
    add("nrs", np.where(i < j, 0.0, NEG))
    add("df", np.maximum(i - j, 0.0))
    add("dr", np.maximum(j - i, 0.0))
    p = np.arange(128, dtype=np.float32)[:, None]
    add("pos", np.concatenate([p + 1, 128 - p, 127 - p, p], 1))
    add("ones", np.ones((128, 128)))
    sel8 = np.zeros((128, 8 * 128))
    for h in range(8):
        sel8[h, h * 128:(h + 1) * 128] = 1.0
    add("sel8", sel8)
    m = np.arange(128)
    perm = np.where((m % 64) < 32, m + 32, m - 32)
    pm = np.zeros((128, 128))
    pm[perm, m] = 1.0
    add("pm", pm)
    add("ident", np.eye(128))
    bd = ((np.arange(128)[:, None] // 32) == (np.arange(128)[None, :] // 32)) * 1.0
    add("bd", bd)
    add("off", 1.0 - bd)
    return np.concatenate(parts, 1)


def rotary_tables(cfg):
    L = cfg.SEQ
    rows = L // cfg.GRID_W
    row = np.repeat(np.arange(rows, dtype=np.float32), cfg.GRID_W)
    col = np.tile(np.arange(cfg.GRID_W, dtype=np.float32), rows)
    nf = 16
    freqs = (10000.0 ** (-np.arange(nf, dtype=np.float32) / nf)).astype(np.float32)
    ang = np.concatenate([row[:, None] * freqs, col[:, None] * freqs], -1).astype(np.float32)
    cos = np.cos(ang).astype(np.float32).T
    sin = np.sin(ang).astype(np.float32).T
    cos2 = np.concatenate([cos, cos, cos, cos], 0)
    sins = np.concatenate([-sin, sin, -sin, sin], 0)
    return np.ascontiguousarray(cos2), np.ascontiguousarray(sins)


def make_cslice(cons):
    def cslice(C, name, rows=128, lo=0, n=None):
        o, w = CO[name]
        n = w if n is None else n
        return cons[:rows, o + lo:o + lo + n]
    return cslice


def chunk_list(cfg, d):
    nctx = 2
    nlat = cfg.SEQ // 128
    ctx = [(k * 128, True, k == 0, k == nctx - 1, 0) for k in range(nctx)]
    lat = [(256 + k * 128, False, k == 0, k == nlat - 1, k * 128) for k in range(nlat)]
    if d == 0:
        return ctx + lat
    return ctx[::-1] + lat[::-1]


def act_copy(P, out_ap, in_ap, reads, writes, scale=None):
    if scale is None:
        P.op("act", lambda e: e.activation(out_ap, in_ap, AF.Copy), reads, writes)
    else:
        P.op("act", lambda e: e.activation(out_ap, in_ap, AF.Copy, scale=scale), reads, writes)


def phase_mixer_even(C):
    P, cfg, T = C.P, C.cfg, C.T
    TA = cfg.TA
    fm, tm = T["fm_pre"], T["tm_pre"]
    with Scope(P):
        cons = P.sbuf("cons", [128, C.ncons], F32)
        cslice = make_cslice(cons)
        P.dma("sp", cons[:, :], T["consts"].t.ap(), cons, T["consts"])
        cw = P.sbuf("cw", [128, 6, 5], F32)
        cb = P.sbuf("cb", [128, 6], F32)
        P.dma("sp", cw[:, :, :], T["ev_cw"].t.ap(), cw, T["ev_cw"])
        P.dma("sp", cb[:, :], T["ev_cb"].t.ap(), cb, T["ev_cb"])
        vec = P.sbuf("evvec", [128, 48], F32)
        P.dma("sp", vec[:, :40], T["ev_vec"].t[0:40].partition_broadcast(128), vec, T["ev_vec"])
        negA = P.sbuf("negA", [128, 16], F32)
        P.op("act", lambda e: e.activation(negA[:, :], vec[:, 16:32], AF.Exp), [vec], [negA])
        P.op("dve", lambda e: e.tensor_scalar(negA[:, :], negA[:, :], -1.0, None, ALU.mult), [negA], [negA])
        ee = P.sbuf("ee", [128, 8], F32)
        lg = P.sbuf("lg", [128, 8], F32)
        P.op("act", lambda e: e.activation(ee[:, :], vec[:, 32:40], AF.Exp, scale=-1.0), [vec], [ee])
        P.op("dve", lambda e: e.tensor_scalar(lg[:, :], ee[:, :], -0.25, 1.0 / 3.0, ALU.mult, ALU.add), [ee], [lg])
        P.op("dve", lambda e: e.tensor_tensor(lg[:, :], lg[:, :], ee[:, :], ALU.mult), [lg, ee], [lg])
        P.op("dve", lambda e: e.tensor_scalar(lg[:, :], lg[:, :], -1.0, 0.5, ALU.mult, ALU.add), [lg], [lg])
        P.op("dve", lambda e: e.tensor_tensor(lg[:, :], lg[:, :], ee[:, :], ALU.mult), [lg, ee], [lg])
        P.op("dve", lambda e: e.tensor_scalar(lg[:, :], lg[:, :], -1.0, 1.0, ALU.mult, ALU.add), [lg], [lg])
        P.op("dve", lambda e: e.tensor_tensor(lg[:, :], lg[:, :], ee[:, :], ALU.mult), [lg, ee], [lg])
        P.op("dve", lambda e: e.tensor_scalar(lg[:, :], lg[:, :], -1.0, None, ALU.mult), [lg], [lg])
        decc = P.sbuf("decc", [128, 8, 128], F32)
        gi = P.sbuf("gi", [128, 8], F32)
        ge = P.sbuf("ge", [128, 8], F32)
        g128 = P.sbuf("g128", [128, 8], F32)
        P.op("act", lambda e: e.activation(g128[:, :], lg[:, :], AF.Exp, scale=128.0), [lg], [g128])
        for d in range(2):
            for r in range(4):
                k = d * 4 + r
                dist = cslice(C, "df" if d == 0 else "dr")
                msk = cslice(C, "mf" if d == 0 else "mr")
                P.op("act", lambda e, k=k, dist=dist: e.activation(decc[:, k, :], dist, AF.Exp, scale=lg[:, k:k + 1]), [cons, lg], [decc])
                P.op("dve", lambda e, k=k, msk=msk: e.scalar_tensor_tensor(decc[:, k, :], decc[:, k, :], 0.125, msk, ALU.mult, ALU.mult),
                     [decc, cons], [decc])
                pi = cslice(C, "pos", lo=(0 if d == 0 else 1), n=1)
                pe_ = cslice(C, "pos", lo=(2 if d == 0 else 3), n=1)
                P.op("act", lambda e, k=k, pi=pi: e.activation(gi[:, k:k + 1], pi, AF.Exp, scale=lg[:, k:k + 1]), [cons, lg], [gi])
                P.op("act", lambda e, k=k, pe_=pe_: e.activation(ge[:, k:k + 1], pe_, AF.Exp, scale=lg[:, k:k + 1]), [cons, lg], [ge])
        P.op("dve", lambda e: e.tensor_scalar(gi[:, :], gi[:, :], 0.125, None, ALU.mult), [gi], [gi])

        NB = 2
        win = [P.sbuf(f"win{i}", [128, 6, 132], F32) for i in range(NB)]
        rqk = [P.sbuf(f"rqk{i}", [128, 4, 128], F32) for i in range(NB)]
        cosT = [P.sbuf(f"cosT{i}", [128, 128], F32) for i in range(NB)]
        sinT = [P.sbuf(f"sinT{i}", [128, 128], F32) for i in range(NB)]
        rvdt = [P.sbuf(f"rvdt{i}", [128, 528], F32) for i in range(NB)]
        acc = [P.sbuf(f"cacc{i}", [128, 128], F32) for i in range(2)]
        cvT = P.sbuf("cvT", [128, 6, 128], BF16)
        xs_tok = P.sbuf("xs_tok", [128, 512], BF16)
        bm_tok = P.sbuf("bm_tok", [128, 128], BF16)
        qkT = P.sbuf("qkT", [128, 4, 128], BF16)
        rk_tok = P.sbuf("rk_tok", [128, 256], BF16)
        rv_bf = P.sbuf("rv_bf", [128, 512], BF16)
        t1 = [P.sbuf(f"rt1_{i}", [128, 128], F32) for i in range(2)]
        t2 = [P.sbuf(f"rt2_{i}", [128, 128], F32) for i in range(2)]
        x16 = P.sbuf("x16", [128, 16], F32)
        dt16 = P.sbuf("dt16", [128, 16], F32)
        la16 = P.sbuf("la16", [128, 16], F32)
        csT = P.sbuf("csT", [8, 128], F32)
        cscol = P.sbuf("cscol", [128, 8], F32)
        clb = P.sbuf("clb", [128, 8], F32)
        ecs = P.sbuf("ecs", [128, 8], F32)
        decb = P.sbuf("decb", [128, 8], F32)
        wgt = P.sbuf("wgt", [128, 8], F32)
        GT = P.sbuf("GT", [128, 128], F32)
        tmpE = [P.sbuf(f"tmpE{i}", [128, 128], F32) for i in range(2)]
        EE = [P.sbuf(f"EE{i}", [128, 128], F32) for i in range(2)]
        AT = [P.sbuf(f"AT{i}", [128, 128], BF16) for i in range(2)]
        yi_sb = [P.sbuf(f"yi_sb{i}", [128, 128], F32) for i in range(2)]
        xw = [P.sbuf(f"xw{i}", [128, 128], BF16) for i in range(2)]
        y_sb = [P.sbuf(f"y_sb{i}", [128, 1024], F32) for i in range(2)]
        S_f = P.sbuf("S_f", [128, 8, 64], F32)
        S_bf = P.sbuf("S_bf", [128, 8, 64], BF16)
        Sr_f = P.sbuf("Sr_f", [128, 2, 128], F32)
        Sr_bf = P.sbuf("Sr_bf", [128, 2, 128], BF16)
        tp4 = P.psum("tp4", [128, 4, 128], BF16)
        tp1 = P.psum("tp1", [128, 2, 128], BF16)
        bk3 = P.psum_views("bk3", 4)
        sw_ps = bk3[0:2]
        GT_ps = bk3[2]
        csT_ps = bk3[3]
        bk4 = P.psum_views("bk4", 4)
        csb_ps = bk4[0:2]
        small_a, small_b = bk4[2], bk4[3]
        bk5 = P.psum_views("bk5", 4)
        y_ps = bk5[0:2]
        yi_ps = bk5[2:4]
        bk6 = P.psum_views("bk6", 4)
        S_ps = bk6[0:2]
        GTr_ps = bk6[2:4]

        cic = [0]

        def run_dir(d):
            ci = cic[0]
            Md = cslice(C, "mf" if d == 0 else "mr")
            negm = cslice(C, "nf" if d == 0 else "nr")
            P.op("pool", lambda e: e.memset(S_f[:, :, :], 0.0), [], [S_f])
            P.op("pool", lambda e: e.memset(S_bf[:, :, :], 0.0), [], [S_bf])
            P.op("pool", lambda e: e.memset(Sr_f[:, :, :], 0.0), [], [Sr_f])
            P.op("pool", lambda e: e.memset(Sr_bf[:, :, :], 0.0), [], [Sr_bf])
            for (s, is_ctx, first, last, lp0) in chunk_list(cfg, d):
                w_ = win[ci % NB]
                q_ = rqk[ci % NB]
                co_, si_ = cosT[ci % NB], sinT[ci % NB]
                rd_ = rvdt[ci % NB]
                ys = y_sb[ci % 2]
                ci += 1
                lo = 0 if not first else 2
                hi = 132 if not last else 130
                if first:
                    P.op("pool", lambda e, w_=w_: e.memset(w_[:, :, 0:2], 0.0), [], [w_])
                if last:
                    P.op("pool", lambda e, w_=w_: e.memset(w_[:, :, 130:132], 0.0), [], [w_])
                P.dma("sp", w_[:, :, lo:hi], fm.t[0:768, s - 2 + lo:s - 2 + hi].rearrange("(ft p) t -> p ft t", p=128), w_, fm)
                P.dma("sp", q_[:, :, :], fm.t[768:1280, s:s + 128].rearrange("(ft p) t -> p ft t", p=128), q_, fm)
                if not is_ctx:
                    P.dma("sp", co_[:, :], T["rot_cos"].t[:, lp0:lp0 + 128], co_, T["rot_cos"])
                    P.dma("sp", si_[:, :], T["rot_sin"].t[:, lp0:lp0 + 128], si_, T["rot_sin"])
                P.dma("sp", rd_[:, 0:512], tm.t[s:s + 128, 512:1024], rd_, tm)
                P.dma("sp", rd_[:, 512:528], tm.t[s:s + 128, 1536:1552], rd_, tm)
                for ft in range(6):
                    a_ = acc[ft % 2]
                    P.op("dve", lambda e, a_=a_, ft=ft, w_=w_: e.tensor_scalar(a_[:, :], w_[:, ft, 0:128], cw[:, ft, 0:1], None, ALU.mult),
                         [w_, cw], [a_])
                    for k in range(1, 5):
                        P.op("dve", lambda e, a_=a_, ft=ft, w_=w_, k=k: e.scalar_tensor_tensor(
                            a_[:, :], w_[:, ft, k:k + 128], cw[:, ft, k:k + 1], a_[:, :], ALU.mult, ALU.add), [w_, cw, a_], [a_])
                    P.op("act", lambda e, a_=a_, ft=ft: e.activation(cvT[:, ft, :], a_[:, :], AF.Silu, bias=cb[:, ft:ft + 1]), [a_, cb], [cvT])
                for q in range(4):
                    P.op("pe", lambda e, q=q: e.transpose(tp4[:, q, :], cvT[:, q, :], C.ident[:, :]), [cvT, C.ident], [tp4])
                P.op("pe", lambda e: e.transpose(tp1[:, 0, :], cvT[:, 4, :], C.ident[:, :]), [cvT, C.ident], [tp1])
                act_copy(P, xs_tok[:, :], tp4[:, :, :], [tp4], [xs_tok])
                P.op("dve", lambda e: e.tensor_copy(bm_tok[:, :], tp1[:, 0, :]), [tp1], [bm_tok])
                if d == 0:
                    P.dma("pool", T["xs_keep"].t[s:s + 128, :], xs_tok[:, :], T["xs_keep"], xs_tok)
                for t in range(4):
                    if is_ctx:
                        P.op("pool", lambda e, t=t, q_=q_: e.tensor_copy(qkT[:, t, :], q_[:, t, :]), [q_], [qkT])
                    else:
                        sp_ = sw_ps[t % 2]
                        a1, a2 = t1[t % 2], t2[t % 2]
                        P.op("pe", lambda e, sp_=sp_, t=t, q_=q_: e.matmul(sp_[:, :], cslice(C, "pm"), q_[:, t, :], start=True, stop=True),
                             [cons, q_], [sp_])
                        P.op("pool", lambda e, a1=a1, t=t, q_=q_, co_=co_: e.tensor_tensor(a1[:, :], q_[:, t, :], co_[:, :], ALU.mult), [q_, co_], [a1])
                        P.op("dve", lambda e, a2=a2, sp_=sp_, si_=si_: e.tensor_tensor(a2[:, :], sp_[:, :], si_[:, :], ALU.mult), [sp_, si_], [a2])
                        P.op("pool", lambda e, a1=a1, a2=a2, t=t: e.tensor_tensor(qkT[:, t, :], a1[:, :], a2[:, :], ALU.add), [a1, a2], [qkT])
                for t in range(2):
                    P.op("pe", lambda e, t=t: e.transpose(tp1[:, t, :], qkT[:, 2 + t, :], C.ident[:, :]), [qkT, C.ident], [tp1])
                act_copy(P, rk_tok[:, :], tp1[:, :, :], [tp1], [rk_tok])
                P.op("pool", lambda e, rd_=rd_: e.tensor_copy(rv_bf[:, :], rd_[:, 0:512]), [rd_], [rv_bf])
                P.op("dve", lambda e, rd_=rd_: e.tensor_tensor(x16[:, :], rd_[:, 512:528], vec[:, 0:16], ALU.add), [rd_, vec], [x16])
                P.op("act", lambda e: e.activation(x16[:, :], x16[:, :], AF.Exp), [x16], [x16])
                P.op("act", lambda e: e.activation(dt16[:, :], x16[:, :], AF.Ln, bias=1.0), [x16], [dt16])
                P.op("dve", lambda e: e.tensor_tensor(la16[:, :], dt16[:, :], negA[:, :], ALU.mult), [dt16, negA], [la16])
                lad = la16[:, d * 8:(d + 1) * 8]
                P.op("pe", lambda e, lad=lad: e.matmul(csT_ps[0:8, :], lad, Md, start=True, stop=True), [la16, cons], [csT_ps])
                P.op("pe", lambda e, lad=lad: e.matmul(small_a[:, 0:8], Md, lad, start=True, stop=True), [la16, cons], [small_a])
                P.op("pe", lambda e, lad=lad: e.matmul(small_b[:, 0:8], cslice(C, "ones"), lad, start=True, stop=True), [la16, cons], [small_b])
                act_copy(P, csT[:, :], csT_ps[0:8, :], [csT_ps], [csT])
                P.op("dve", lambda e: e.tensor_copy(cscol[:, :], small_a[:, 0:8]), [small_a], [cscol])
                P.op("dve", lambda e: e.tensor_copy(clb[:, :], small_b[:, 0:8]), [small_b], [clb])
                P.op("act", lambda e: e.activation(ecs[:, :], cscol[:, :], AF.Exp), [cscol], [ecs])
                P.op("act", lambda e: e.activation(decb[:, :], clb[:, :], AF.Exp), [clb], [decb])
                P.op("dve", lambda e: e.tensor_tensor(wgt[:, :], clb[:, :], cscol[:, :], ALU.subtract), [clb, cscol], [wgt])
                P.op("act", lambda e: e.activation(wgt[:, :], wgt[:, :], AF.Exp), [wgt], [wgt])
                P.op("dve", lambda e: e.tensor_tensor(wgt[:, :], wgt[:, :], dt16[:, d * 8:(d + 1) * 8], ALU.mult), [wgt, dt16], [wgt])
                P.op("pe", lambda e: e.matmul(GT_ps[:, :], cvT[:, 4, :], cvT[:, 5, :], start=True, stop=True), [cvT], [GT_ps])
                act_copy(P, GT[:, :], GT_ps[:, :], [GT_ps], [GT])
                for h in range(8):
                    cp, te, ee_, at = csb_ps[h % 2], tmpE[h % 2], EE[h % 2], AT[h % 2]
                    yp, yip, sp2, yis, xw_ = y_ps[h % 2], yi_ps[h % 2], S_ps[h % 2], yi_sb[h % 2], xw[h % 2]
                    P.op("pe", lambda e, cp=cp, h=h: e.matmul(cp[:, :], cslice(C, "sel8", rows=8, lo=h * 128, n=128), csT[:, :], start=True, stop=True),
                         [cons, csT], [cp])
                    P.op("dve", lambda e, cp=cp, te=te, h=h: e.scalar_tensor_tensor(te[:, :], cp[:, :], cscol[:, h:h + 1], negm, ALU.subtract, ALU.add),
                         [cp, cscol, cons], [te])
                    P.op("act", lambda e, te=te, ee_=ee_: e.activation(ee_[:, :], te[:, :], AF.Exp), [te], [ee_])
                    P.op("dve", lambda e, at=at, ee_=ee_, h=h: e.scalar_tensor_tensor(at[:, :], GT[:, :], dt16[:, d * 8 + h:d * 8 + h + 1], ee_[:, :], ALU.mult, ALU.mult),
                         [GT, dt16, ee_], [at])
                    P.op("pe", lambda e, yp=yp, at=at, h=h: e.matmul(yp[:, 0:64], at[:, :], xs_tok[:, h * 64:(h + 1) * 64], start=True, stop=True),
                         [at, xs_tok], [yp])
                    P.op("pe", lambda e, yip=yip, h=h: e.matmul(yip[:, 0:64], cvT[:, 5, :], S_bf[:, h, :], start=True, stop=True), [cvT, S_bf], [yip])
                    act_copy(P, yis[:, 0:64], yip[:, 0:64], [yip, ecs], [yis], scale=ecs[:, h:h + 1])
                    P.op("dve", lambda e, ys=ys, yp=yp, yis=yis, h=h: e.tensor_tensor(ys[:, h * 64:(h + 1) * 64], yp[:, 0:64], yis[:, 0:64], ALU.add),
                         [yp, yis], [ys])
                    P.op("pool", lambda e, xw_=xw_, h=h: e.tensor_scalar(xw_[:, 0:64], xs_tok[:, h * 64:(h + 1) * 64], wgt[:, h:h + 1], None, ALU.mult),
                         [xs_tok, wgt], [xw_])
                    P.op("pe", lambda e, sp2=sp2, xw_=xw_: e.matmul(sp2[:, 0:64], bm_tok[:, :], xw_[:, 0:64], start=True, stop=True), [bm_tok, xw_], [sp2])
                    P.op("dve", lambda e, sp2=sp2, h=h: e.scalar_tensor_tensor(S_f[:, h, :], S_f[:, h, :], decb[:, h:h + 1], sp2[:, 0:64], ALU.mult, ALU.add),
                         [S_f, decb, sp2], [S_f])
                    act_copy(P, S_bf[:, h, :], S_f[:, h, :], [S_f], [S_bf])
                for r in range(4):
                    t, po = r // 2, (r % 2) * 64
                    k = d * 4 + r
                    gp, at = GTr_ps[r % 2], AT[r % 2]
                    yp, yip, sp2, yis, xw_ = y_ps[r % 2], yi_ps[r % 2], S_ps[r % 2], yi_sb[r % 2], xw[r % 2]
                    P.op("pe", lambda e, gp=gp, t=t, po=po: e.matmul(gp[:, :], qkT[po:po + 64, 2 + t, :], qkT[po:po + 64, t, :], start=True, stop=True),
                         [qkT], [gp])
                    P.op("dve", lambda e, at=at, gp=gp, k=k: e.tensor_tensor(at[:, :], gp[:, :], decc[:, k, :], ALU.mult), [gp, decc], [at])
                    P.op("pe", lambda e, yp=yp, at=at, r=r: e.matmul(yp[:, :], at[:, :], rv_bf[:, r * 128:(r + 1) * 128], start=True, stop=True),
                         [at, rv_bf], [yp])
                    P.op("pe", lambda e, yip=yip, t=t, po=po: e.matmul(yip[:, :], qkT[po:po + 64, t, :], Sr_bf[po:po + 64, t, :], start=True, stop=True),
                         [qkT, Sr_bf], [yip])
                    act_copy(P, yis[:, :], yip[:, :], [yip, gi], [yis], scale=gi[:, k:k + 1])
                    P.op("dve", lambda e, ys=ys, yp=yp, yis=yis, r=r: e.tensor_tensor(ys[:, 512 + r * 128:512 + (r + 1) * 128], yp[:, :], yis[:, :], ALU.add),
                         [yp, yis], [ys])
                    P.op("pool", lambda e, xw_=xw_, r=r, k=k: e.tensor_scalar(xw_[:, :], rv_bf[:, r * 128:(r + 1) * 128], ge[:, k:k + 1], None, ALU.mult),
                         [rv_bf, ge], [xw_])
                    P.op("pe", lambda e, sp2=sp2, xw_=xw_, t=t: e.matmul(sp2[:, :], rk_tok[:, t * 128:(t + 1) * 128], xw_[:, :], start=True, stop=True),
                         [rk_tok, xw_], [sp2])
                    P.op("dve", lambda e, sp2=sp2, t=t, po=po, k=k: e.scalar_tensor_tensor(
                        Sr_f[po:po + 64, t, :], Sr_f[po:po + 64, t, :], g128[po:po + 64, k:k + 1], sp2[po:po + 64, :], ALU.mult, ALU.add),
                        [Sr_f, g128, sp2], [Sr_f])
                    act_copy(P, Sr_bf[po:po + 64, t, :], Sr_f[po:po + 64, t, :], [Sr_f], [Sr_bf])
                P.dma("pool", T["yd"].t[d, s:s + 128, :], ys[:, :], T["yd"], ys)
            cic[0] = ci

        run_dir(0)
        run_dir(1)


def store_yT(C, yt, s):
    P, cfg, T = C.P, C.cfg, C.T
    if s < 256:
        for half in range(2):
            r = (s // 64) + half
            P.dma("pool", T["yT_loc"].t[r * 1024:(r + 1) * 1024, 0:64].rearrange("(kc p) t -> p kc t", p=128), yt[:, :, half * 64:(half + 1) * 64],
                  T["yT_loc"], yt, scratch=True)
    else:
        lp = s - 256
        r, row0 = lp // cfg.NTL, 64 + lp % cfg.NTL
        P.dma("pool", T["yT_loc"].t[r * 1024:(r + 1) * 1024, row0:row0 + 128].rearrange("(kc p) t -> p kc t", p=128), yt[:, :, :], T["yT_loc"], yt, scratch=True)


def phase_even_epilogue(C):
    P, cfg, T = C.P, C.cfg, C.T
    with Scope(P):
        dskb = P.sbuf("dskb", [128, 512], F32)
        ssdn = P.sbuf("ssdn", [128, 512], F32)
        P.dma("sp", dskb[:, :], T["ev_vec"].t[64:576].partition_broadcast(128), dskb, T["ev_vec"])
        P.dma("sp", ssdn[:, :], T["ev_vec"].t[576:1088].partition_broadcast(128), ssdn, T["ev_vec"])
        NB = 2
        yf = [P.sbuf(f"yf{i}", [128, 1024], F32) for i in range(NB)]
        yr = [P.sbuf(f"yr{i}", [128, 1024], F32) for i in range(NB)]
        xk = [P.sbuf(f"xk{i}", [128, 512], BF16) for i in range(NB)]
        zg = [P.sbuf(f"zg{i}", [128, 1536], F32) for i in range(NB)]
        ysum = P.sbuf("ysum", [128, 1024], F32)
        t512 = P.sbuf("t512", [128, 512], F32)
        junk = P.sbuf("ejunk", [128, 512], F32)
        sz = P.sbuf("sz", [128, 1024], F32)
        ymix = P.sbuf("ymix", [128, 1024], BF16)
        ss = P.sbuf("ess", [128, 8], F32)
        rstd = P.sbuf("erstd", [128, 8], F32)
        mean = P.sbuf("emean", [128, 8], F32)
        yT = [P.sbuf(f"eyT{i}", [128, 8, 128], BF16) for i in range(2)]
        pst = [P.psum(f"etp{i}", [128, 4, 128], BF16) for i in range(2)]
        nch = cfg.TA // 128
        for ci in range(nch):
            s = ci * 128
            a, b, x, z = yf[ci % NB], yr[ci % NB], xk[ci % NB], zg[ci % NB]
            P.dma("sp", a[:, :], T["yd"].t[0, s:s + 128, :], a, T["yd"])
            P.dma("sp", b[:, :], T["yd"].t[1, s:s + 128, :], b, T["yd"])
            P.dma("sp", x[:, :], T["xs_keep"].t[s:s + 128, :], x, T["xs_keep"])
            P.dma("sp", z[:, 0:512], T["tm_pre"].t[s:s + 128, 0:512], z, T["tm_pre"])
            P.dma("sp", z[:, 512:1024], T["tm_pre"].t[s:s + 128, 1024:1536], z, T["tm_pre"])
            P.op("dve", lambda e, a=a, b=b: e.tensor_tensor(ysum[:, :], a[:, :], b[:, :], ALU.add), [a, b], [ysum])
            P.op("pool", lambda e, x=x: e.tensor_tensor(t512[:, :], x[:, :], dskb[:, :], ALU.mult), [x, dskb], [t512])
            P.op("dve", lambda e: e.tensor_tensor(ysum[:, 0:512], ysum[:, 0:512], t512[:, :], ALU.add), [ysum, t512], [ysum])
            P.op("act", lambda e, z=z: e.activation(sz[:, :], z[:, 0:1024], AF.Silu), [z], [sz])
            P.op("dve", lambda e: e.tensor_tensor(ysum[:, 0:512], ysum[:, 0:512], sz[:, 0:512], ALU.mult), [ysum, sz], [ysum])
            P.op("act", lambda e: e.activation(junk[:, :], ysum[:, 0:512], AF.Square, accum_out=ss[:, 0:1]), [ysum], [junk, ss])
            P.op("dve", lambda e: e.tensor_scalar(rstd[:, 0:1], ss[:, 0:1], 1.0 / 512.0, EPS, ALU.mult, ALU.add), [ss], [rstd])
            P.op("act", lambda e: e.activation(rstd[:, 0:1], rstd[:, 0:1], AF.Sqrt), [rstd], [rstd])
            P.op("dve", lambda e: e.reciprocal(rstd[:, 0:1], rstd[:, 0:1]), [rstd], [rstd])
            P.op("dve", lambda e: e.scalar_tensor_tensor(ymix[:, 0:512], ysum[:, 0:512], rstd[:, 0:1], ssdn[:, :], ALU.mult, ALU.mult),
                 [ysum, rstd, ssdn], [ymix])
            for r in range(4):
                sl = slice(512 + r * 128, 512 + (r + 1) * 128)
                P.op("act", lambda e, sl=sl, r=r: e.activation(junk[:, 0:128], ysum[:, sl], AF.Copy, accum_out=mean[:, r:r + 1]), [ysum], [junk, mean])
                P.op("dve", lambda e, r=r: e.tensor_scalar(mean[:, r:r + 1], mean[:, r:r + 1], 1.0 / 128.0, None, ALU.mult), [mean], [mean])
                P.op("dve", lambda e, sl=sl, r=r: e.tensor_scalar(ysum[:, sl], ysum[:, sl], mean[:, r:r + 1], None, ALU.subtract), [ysum, mean], [ysum])
                P.op("act", lambda e, sl=sl, r=r: e.activation(junk[:, 0:128], ysum[:, sl], AF.Square, accum_out=ss[:, 1 + r:2 + r]), [ysum], [junk, ss])
                P.op("dve", lambda e, r=r: e.tensor_scalar(rstd[:, 1 + r:2 + r], ss[:, 1 + r:2 + r], 1.0 / 128.0, EPS, ALU.mult, ALU.add), [ss], [rstd])
                P.op("act", lambda e, r=r: e.activation(rstd[:, 1 + r:2 + r], rstd[:, 1 + r:2 + r], AF.Sqrt), [rstd], [rstd])
                P.op("dve", lambda e, r=r: e.reciprocal(rstd[:, 1 + r:2 + r], rstd[:, 1 + r:2 + r]), [rstd], [rstd])
                P.op("dve", lambda e, sl=sl, r=r: e.scalar_tensor_tensor(ymix[:, sl], ysum[:, sl], rstd[:, 1 + r:2 + r], sz[:, sl], ALU.mult, ALU.mult),
                     [ysum, rstd, sz], [ymix])
            yt = yT[ci % 2]
            transpose_rows(C, ymix, 128, 1024, yt, pst)
            store_yT(C, yt, s)


MAXCC = 1 << 20


def pick_rn(rows, cols, esize=2):
    best = 1
    for rn in range(1, rows + 1):
        if rows % rn == 0 and rn * cols * esize <= MAXCC:
            best = rn
    return best


def ag_chunks(C, groups, nr, dst, src, rn):
    P = C.P
    R = src.t.shape[0]
    for c in range(R // rn):
        P.coll("AllGather", groups, dst.t[c * nr * rn:(c + 1) * nr * rn, :].opt(), src.t[c * rn:(c + 1) * rn, :].opt(), dst, src)


def prep_weights(C, l):
    P, cfg, T = C.P, C.cfg, C.T
    for nm, src in (("wo", "w_out"), ("w1", "ffn_w1"), ("w3", "ffn_w3"), ("w2", "ffn_w2")):
        sh, fu = T[f"{nm}_s{l}"], T[f"{nm}_f{l}"]
        rows, cols = sh.t.shape
        for r0 in range(0, rows, 256):
            rn = min(256, rows - r0)
            P.dma("pool", sh.t[r0:r0 + rn, :], T[src].t[l, r0:r0 + rn, :], sh, T[src], scratch=True)
        ag_chunks(C, BGROUPS, 4, fu, sh, pick_rn(rows, cols))


def phase_outproj(C, l, do_ctx):
    P, cfg, T = C.P, C.cfg, C.T
    D = cfg.D
    rk = C.rk
    with Scope(P):
        G = GemmRes(C, 8)
        xts = [P.sbuf(f"opx{i}", [128, 32, 512], BF16) for i in range(2)]
        stg = [P.sbuf(f"opst{i}", [128, 512], F32) for i in range(4)]
        si = [0]
        for bi, (r0, n) in enumerate(cfg.token_tiles()):
            if r0 == 0 and not do_ctx:
                continue
            xT = xts[bi % 2]
            for gq in range(4):
                yv = T["yallT"].t.ap().rearrange("(d kk g p) t -> d kk g p t", d=4, kk=8, g=4)
                P.dma("sp", xT[:, gq * 8:(gq + 1) * 8, :n], yv[rk, :, gq, :, r0:r0 + n].rearrange("kk p t -> p kk t"), xT, T["yallT"])

            def evac(a0, an, c0, nw, pt, r0=r0):
                st = stg[si[0] % 4]
                if si[0] % 2 == 0:
                    act_copy(P, st[:an, :nw], pt[:an, :nw], [pt], [st])
                else:
                    P.op("dve", lambda e: e.tensor_copy(st[:an, :nw], pt[:an, :nw]), [pt], [st])
                si[0] += 1
                P.dma("pool", T["o_loc"].t[r0 + a0:r0 + a0 + an, c0:c0 + nw], st[:an, :nw], T["o_loc"], st)

            gemm(C, G, "TM", xT, n, T[f"wo_f{l}"], 4096, 0, D, evac)


def row_tiles(cfg, do_ctx):
    t = [(0, 64, 1)] if do_ctx else []
    return t + [(r, 128, 0) for r in range(64, cfg.NT, 128)]


def phase_postmix(C, l, xin, do_ctx):
    P, cfg, T = C.P, C.cfg, C.T
    D, KC = cfg.D, cfg.KC
    with Scope(P):
        gm = P.sbuf("gm", [128, D], F32)
        g2 = P.sbuf("g2", [128, D], F32)
        sh2 = P.sbuf("sh2", [128, D], F32)
        ot = [P.sbuf(f"ot{i}", [128, D], F32) for i in range(2)]
        xt = [P.sbuf(f"pxt{i}", [128, D], F32) for i in range(2)]
        junk = P.sbuf("pjunk", [128, D], BF16)
        hb = [P.sbuf(f"phb{i}", [128, D], BF16) for i in range(2)]
        hT = [P.sbuf(f"phT{i}", [128, KC, 128], BF16) for i in range(2)]
        ss = P.sbuf("pss", [128, 1], F32)
        rstd = P.sbuf("prstd", [128, 1], F32)
        pst = [P.psum(f"ptp{i}", [128, 4, 128], BF16) for i in range(2)]
        cur = None
        for i, (r0, n, var) in enumerate(row_tiles(cfg, do_ctx)):
            if var != cur:
                for (tile_, vi) in ((gm, 2), (g2, 3), (sh2, 4)):
                    P.dma("sp", tile_[:, :], T["vecs"].t[l, vi, var, :].partition_broadcast(128), tile_, T["vecs"])
                cur = var
            o, x, h, ht = ot[i % 2], xt[i % 2], hb[i % 2], hT[i % 2]
            P.dma("sp", o[:n, :], T["o_loc"].t[r0:r0 + n, :], o, T["o_loc"])
            P.dma("sp", x[:n, :], xin.t[r0:r0 + n, :], x, xin)
            P.op("act", lambda e, o=o, n=n: e.activation(junk[:n, :], o[:n, :], AF.Square, accum_out=ss[:n, :]), [o], [junk, ss])
            rstd_from_ss(P, rstd, ss, n, 1.0 / D)
            P.op("dve", lambda e, o=o, n=n: e.scalar_tensor_tensor(o[:n, :], o[:n, :], rstd[:n, 0:1], gm[:n, :], ALU.mult, ALU.mult), [o, rstd, gm], [o])
            P.op("pool", lambda e, o=o, x=x, n=n: e.tensor_tensor(x[:n, :], x[:n, :], o[:n, :], ALU.add), [x, o], [x])
            P.dma("pool", T["s_loc"].t[r0:r0 + n, :], x[:n, :], T["s_loc"], x, scratch=True)
            P.op("act", lambda e, x=x, n=n: e.activation(junk[:n, :], x[:n, :], AF.Square, accum_out=ss[:n, :]), [x], [junk, ss])
            rstd_from_ss(P, rstd, ss, n, 1.0 / D)
            P.op("dve", lambda e, o=o, x=x, n=n: e.scalar_tensor_tensor(o[:n, :], x[:n, :], rstd[:n, 0:1], g2[:n, :], ALU.mult, ALU.mult), [x, rstd, g2], [o])
            P.op("pool", lambda e, o=o, h=h, n=n: e.tensor_tensor(h[:n, :], o[:n, :], sh2[:n, :], ALU.add), [o, sh2], [h])
            transpose_rows(C, h, n, D, ht, pst)
            for k8 in range(0, KC, 8):
                ke = min(KC, k8 + 8)
                P.dma("pool", T["h2T_loc"].t[k8 * 128:ke * 128, r0:r0 + n].rearrange("(kc p) t -> p kc t", p=128), ht[:, k8:ke, :n],
                      T["h2T_loc"], ht, scratch=True)


def phase_ffn(C, l, do_ctx):
    P, cfg, T = C.P, C.cfg, C.T
    D, KC, DFF = cfg.D, cfg.KC, cfg.DFF
    FC = DFF // 128
    with Scope(P):
        G = GemmRes(C, 8)
        xts = [P.sbuf(f"fx{i}", [128, KC, 512], BF16) for i in range(1)]
        aT = P.sbuf("aT", [128, FC, 512], BF16)
        su = [P.sbuf(f"su{i}", [128, 512], F32) for i in range(4)]
        stg = [P.sbuf(f"fst{i}", [128, 512], F32) for i in range(4)]
        si = [0]
        for bi, (r0, n) in enumerate(cfg.token_tiles()):
            if r0 == 0 and not do_ctx:
                continue
            xT = xts[0]
            for k8 in range(0, KC, 8):
                ke = min(KC, k8 + 8)
                P.dma("sp", xT[:, k8:ke, :n], T["h2T_loc"].t[k8 * 128:ke * 128, r0:r0 + n].rearrange("(kc p) t -> p kc t", p=128), xT, T["h2T_loc"])
            for c0 in range(0, DFF, 512):
                nw = min(512, DFF - c0)

                def ev1(a0, pt, nf, c0=c0, n=n):
                    q = (a0 - c0) // 128
                    P.op("act", lambda e: e.activation(su[q][:nf, :n], pt[:nf, :n], AF.Silu), [pt], [su[q]])

                def ev3(a0, pt, nf, c0=c0, n=n):
                    q = (a0 - c0) // 128
                    P.op("dve", lambda e: e.tensor_tensor(aT[:nf, a0 // 128, :n], su[q][:nf, :n], pt[:nf, :n], ALU.mult), [su[q], pt], [aT])

                gemm(C, G, "FM", xT, n, T[f"w1_f{l}"], D, c0, nw, ev1)
                gemm(C, G, "FM", xT, n, T[f"w3_f{l}"], D, c0, nw, ev3)

            def evac(a0, an, c0, nw, pt, r0=r0):
                st = stg[si[0] % 4]
                if si[0] % 2 == 0:
                    act_copy(P, st[:an, :nw], pt[:an, :nw], [pt], [st])
                else:
                    P.op("dve", lambda e: e.tensor_copy(st[:an, :nw], pt[:an, :nw]), [pt], [st])
                si[0] += 1
                P.dma("pool", T["f_loc"].t[r0 + a0:r0 + a0 + an, c0:c0 + nw], st[:an, :nw], T["f_loc"], st)

            gemm(C, G, "TM", aT, n, T[f"w2_f{l}"], DFF, 0, D, evac)


def phase_postffn(C, l, dst, do_ctx, dst_row_off=0):
    P, cfg, T = C.P, C.cfg, C.T
    D = cfg.D
    with Scope(P):
        gf = P.sbuf("gf", [128, D], F32)
        ft = [P.sbuf(f"fft{i}", [128, D], F32) for i in range(2)]
        st_ = [P.sbuf(f"fst_{i}", [128, D], F32) for i in range(2)]
        junk = P.sbuf("fjunk", [128, D], BF16)
        ss = P.sbuf("fss", [128, 1], F32)
        rstd = P.sbuf("frstd", [128, 1], F32)
        cur = None
        for i, (r0, n, var) in enumerate(row_tiles(cfg, do_ctx)):
            if var != cur:
                P.dma("sp", gf[:, :], T["vecs"].t[l, 5, var, :].partition_broadcast(128), gf, T["vecs"])
                cur = var
            f, s_ = ft[i % 2], st_[i % 2]
            P.dma("sp", f[:n, :], T["f_loc"].t[r0:r0 + n, :], f, T["f_loc"])
            P.dma("sp", s_[:n, :], T["s_loc"].t[r0:r0 + n, :], s_, T["s_loc"])
            P.op("act", lambda e, f=f, n=n: e.activation(junk[:n, :], f[:n, :], AF.Square, accum_out=ss[:n, :]), [f], [junk, ss])
            rstd_from_ss(P, rstd, ss, n, 1.0 / D)
            P.op("dve", lambda e, f=f, n=n: e.scalar_tensor_tensor(f[:n, :], f[:n, :], rstd[:n, 0:1], gf[:n, :], ALU.mult, ALU.mult), [f, rstd, gf], [f])
            P.op("pool", lambda e, f=f, s_=s_, n=n: e.tensor_tensor(s_[:n, :], s_[:n, :], f[:n, :], ALU.add), [s_, f], [s_])
            P.dma("pool", dst.t[r0 - dst_row_off:r0 - dst_row_off + n, :], s_[:n, :], dst, s_, scratch=True)


def phase_mixer_odd(C):
    P, cfg, T = C.P, C.cfg, C.T
    fm, tm = T["fm_pre"], T["tm_pre"]
    import os
    CUT = float(os.environ.get("ODDCUT", "9"))
    with Scope(P):
        cons = P.sbuf("cons", [128, C.ncons], F32)
        cslice = make_cslice(cons)
        P.dma("sp", cons[:, :], T["consts"].t.ap(), cons, T["consts"])
        cw = P.sbuf("ocw", [128, 16, 5], F32)
        P.dma("sp", cw[:, :, :], T["od_cw"].t.ap(), cw, T["od_cw"])
        vec = P.sbuf("odvec", [128, 32], F32)
        P.dma("sp", vec[:, :], T["od_vec"].t[0:32].partition_broadcast(128), vec, T["od_vec"])
        negA = P.sbuf("onegA", [128, 16], F32)
        P.op("act", lambda e: e.activation(negA[:, :], vec[:, 16:32], AF.Exp), [vec], [negA])
        P.op("dve", lambda e: e.tensor_scalar(negA[:, :], negA[:, :], -1.0, None, ALU.mult), [negA], [negA])
        identb = P.sbuf("identb", [128, 128], BF16)
        P.op("pool", lambda e: e.tensor_copy(identb[:, :], cslice(C, "ident")), [cons], [identb])

        NB = 2
        win = [P.sbuf(f"owin{i}", [128, 16, 132], F32) for i in range(NB)]
        ba = [P.sbuf(f"oba{i}", [128, 32], F32) for i in range(NB)]
        acc = [P.sbuf(f"oacc{i}", [128, 128], F32) for i in range(2)]
        cv = P.sbuf("ocv", [128, 8, 128], F32)
        vT = P.sbuf("ovT", [128, 8, 128], BF16)
        sq = [P.sbuf(f"osq{i}", [128, 128], F32) for i in range(2)]
        rn = [P.sbuf(f"orn{i}", [128, 128], F32) for i in range(2)]
        qkT = P.sbuf("oqkT", [128, 8, 128], BF16)
        k_tok = P.sbuf("ok_tok", [128, 4, 128], BF16)
        v_tok = P.sbuf("ov_tok", [128, 8, 128], BF16)
        x16 = P.sbuf("ox16", [128, 32], F32)
        g16 = P.sbuf("og16", [128, 16], F32)
        be16 = P.sbuf("obe16", [128, 16], F32)
        lnb16 = P.sbuf("olnb16", [128, 16], F32)
        csT = P.sbuf("ocsT", [8, 128], F32)
        lbT = P.sbuf("olbT", [8, 128], F32)
        cscol = P.sbuf("ocscol", [128, 8], F32)
        clb = P.sbuf("oclb", [128, 8], F32)
        ecs = P.sbuf("oecs", [128, 8], F32)
        decb = P.sbuf("odecb", [128, 8], F32)
        wend = P.sbuf("owend", [128, 8], F32)
        wkb = P.sbuf("owkb", [128, 8], F32)
        tmp = [P.sbuf(f"otmp{i}", [128, 128], F32) for i in range(3)]
        Ei = [P.sbuf(f"oEi{i}", [128, 128], F32) for i in range(3)]
        attT = P.sbuf("oattT", [128, 128], BF16)
        Mk = [P.sbuf(f"oM{i}", [128, 128], F32) for i in range(2)]
        Nk = [P.sbuf(f"oN{i}", [128, 128], F32) for i in range(2)]
        Uk = [P.sbuf(f"oU{i}", [128, 128], F32) for i in range(2)]
        kn32 = P.sbuf("okn32", [128, 4, 128], F32)
        u_sb = P.sbuf("ou_sb", [128, 128], F32)
        Esb = P.sbuf("oEsb", [128, 128], F32)
        Fsb = P.sbuf("oFsb", [128, 128], F32)
        Gsb = P.sbuf("oGsb", [128, 128], F32)
        F2sb = P.sbuf("oF2sb", [128, 128], F32)
        W1 = P.sbuf("oW1", [128, 128], F32)
        Tbd = P.sbuf("oTbd", [128, 128], F32)
        wT = P.sbuf("owT", [128, 128], BF16)
        ecsb = P.sbuf("oecsb", [128, 128], F32)
        qeT = P.sbuf("oqeT", [128, 128], BF16)
        kb = P.sbuf("okb", [128, 128], F32)
        kend = P.sbuf("okend", [128, 128], BF16)
        vb = P.sbuf("ovb", [128, 128], F32)
        vnew = P.sbuf("ovnew", [128, 128], BF16)
        y_sb = [P.sbuf(f"oy_sb{i}", [128, 1024], F32) for i in range(2)]
        S_f = P.sbuf("oS_f", [128, 8, 128], F32)
        S_bf = P.sbuf("oS_bf", [128, 8, 128], BF16)
        tpb = P.psum_views("otpb", 4, 128, BF16)
        bkA = P.psum_views("obkA", 4)
        bkB = P.psum_views("obkB", 4)
        bkC = P.psum_views("obkC", 4)
        bkD = P.psum_views("obkD", 4)
        bkE = P.psum_views("obkE", 4)
        ssq_ps = bkA[0:2]
        QK_ps, KK_ps = bkA[2], bkA[3]
        small_a, small_b, csb_ps = bkB[1], bkB[2], bkB[3]
        csT_ps = bkE[2]
        lbb_ps, M_ps, N_ps, U_ps = bkC[0], bkC[1], bkC[2], bkC[3]
        u_ps, wT_ps, vn_ps, o_ps = bkD[0], bkD[1], bkD[2], bkD[3]
        S_ps, lbT_ps = bkE[0], bkE[1]

        cic = [0]

        def run_dir(d):
            ci = cic[0]
            Md = cslice(C, "mf" if d == 0 else "mr")
            n_incl = cslice(C, "nf" if d == 0 else "nr")
            n_N0 = cslice(C, "nfs" if d == 0 else "nrs")
            n_M0 = cslice(C, "nrs" if d == 0 else "nfs")
            P.op("pool", lambda e: e.memset(S_f[:, :, :], 0.0), [], [S_f])
            P.op("pool", lambda e: e.memset(S_bf[:, :, :], 0.0), [], [S_bf])
            for (s, is_ctx, first, last, lp0) in chunk_list(cfg, d):
                w_ = win[ci % NB]
                ba_ = ba[ci % NB]
                ys = y_sb[ci % 2]
                ci += 1
                lo = 0 if not first else 2
                hi = 132 if not last else 130
                if first:
                    P.op("pool", lambda e, w_=w_: e.memset(w_[:, :, 0:2], 0.0), [], [w_])
                if last:
                    P.op("pool", lambda e, w_=w_: e.memset(w_[:, :, 130:132], 0.0), [], [w_])
                for q4 in range(4):
                    P.dma("sp", w_[:, q4 * 4:(q4 + 1) * 4, lo:hi],
                          fm.t[q4 * 512:(q4 + 1) * 512, s - 2 + lo:s - 2 + hi].rearrange("(ft p) t -> p ft t", p=128), w_, fm)
                P.dma("sp", ba_[:, :], tm.t[s:s + 128, 1024:1056], ba_, tm)
                for ft in range(16):
                    a_ = acc[ft % 2]
                    P.op("dve", lambda e, a_=a_, ft=ft, w_=w_: e.tensor_scalar(a_[:, :], w_[:, ft, 0:128], cw[:, ft, 0:1], None, ALU.mult), [w_, cw], [a_])
                    for k in range(1, 5):
                        P.op("dve", lambda e, a_=a_, ft=ft, w_=w_, k=k: e.scalar_tensor_tensor(
                            a_[:, :], w_[:, ft, k:k + 128], cw[:, ft, k:k + 1], a_[:, :], ALU.mult, ALU.add), [w_, cw, a_], [a_])
                    if ft < 8:
                        P.op("act", lambda e, a_=a_, ft=ft: e.activation(cv[:, ft, :], a_[:, :], AF.Silu), [a_], [cv])
                    else:
                        P.op("act", lambda e, a_=a_, ft=ft: e.activation(vT[:, ft - 8, :], a_[:, :], AF.Silu), [a_], [vT])
                for t in range(8 if CUT >= 2 else 0):
                    sq_, rn_, sp_ = sq[t % 2], rn[t % 2], ssq_ps[t % 2]
                    P.op("act", lambda e, sq_=sq_, t=t: e.activation(sq_[:, :], cv[:, t, :], AF.Square), [cv], [sq_])
                    P.op("pe", lambda e, sp_=sp_, sq_=sq_: e.matmul(sp_[:, :], cslice(C, "ones"), sq_[:, :], start=True, stop=True), [cons, sq_], [sp_])
                    P.op("dve", lambda e, rn_=rn_, sp_=sp_: e.tensor_scalar(rn_[:, :], sp_[:, :], EPS, None, ALU.add), [sp_], [rn_])
                    P.op("act", lambda e, rn_=rn_: e.activation(rn_[:, :], rn_[:, :], AF.Sqrt), [rn_], [rn_])
                    P.op("dve", lambda e, rn_=rn_: e.reciprocal(rn_[:, :], rn_[:, :]), [rn_], [rn_])
                    sc = (128.0 ** -0.5) if t < 4 else 1.0
                    P.op("dve", lambda e, rn_=rn_, t=t, sc=sc: e.scalar_tensor_tensor(qkT[:, t, :], cv[:, t, :], sc, rn_[:, :], ALU.mult, ALU.mult),
                         [cv, rn_], [qkT])
                    if t >= 4:
                        P.op("pool", lambda e, rn_=rn_, t=t: e.tensor_tensor(kn32[:, t - 4, :], cv[:, t, :], rn_[:, :], ALU.mult), [cv, rn_], [kn32])
                if CUT < 3:
                    continue
                for t in range(4):
                    P.op("pe", lambda e, t=t: e.transpose(tpb[t][:, :], qkT[:, 4 + t, :], identb[:, :]), [qkT, identb], [tpb[t]])
                    P.op("dve", lambda e, t=t: e.tensor_copy(k_tok[:, t, :], tpb[t][:, :]), [tpb[t]], [k_tok])
                for t in range(8):
                    P.op("pe", lambda e, t=t: e.transpose(tpb[t % 4][:, :], vT[:, t, :], identb[:, :]), [vT, identb], [tpb[t % 4]])
                    act_copy(P, v_tok[:, t, :], tpb[t % 4][:, :], [tpb[t % 4]], [v_tok])
                if CUT < 3.1:
                    continue
                P.op("dve", lambda e, ba_=ba_: e.tensor_tensor(x16[:, 16:32], ba_[:, 16:32], vec[:, 0:16], ALU.add), [ba_, vec], [x16])
                P.op("act", lambda e: e.activation(x16[:, 16:32], x16[:, 16:32], AF.Exp), [x16], [x16])
                P.op("act", lambda e: e.activation(g16[:, :], x16[:, 16:32], AF.Ln, bias=1.0), [x16], [g16])
                P.op("dve", lambda e: e.tensor_tensor(g16[:, :], g16[:, :], negA[:, :], ALU.mult), [g16, negA], [g16])
                P.op("act", lambda e, ba_=ba_: e.activation(x16[:, 0:16], ba_[:, 0:16], AF.Exp, scale=-1.0), [ba_], [x16])
                P.op("act", lambda e: e.activation(lnb16[:, :], x16[:, 0:16], AF.Ln, bias=1.0), [x16], [lnb16])
                P.op("dve", lambda e: e.tensor_scalar(lnb16[:, :], lnb16[:, :], -1.0, None, ALU.mult), [lnb16], [lnb16])
                P.op("act", lambda e: e.activation(be16[:, :], lnb16[:, :], AF.Exp), [lnb16], [be16])
                gd = g16[:, d * 8:(d + 1) * 8]
                lnbd = lnb16[:, d * 8:(d + 1) * 8]
                if CUT < 3.3:
                    continue
                P.op("pe", lambda e: e.matmul(csT_ps[0:8, :], gd, Md, start=True, stop=True), [g16, cons], [csT_ps])
                if CUT < 3.6:
                    continue
                P.op("pe", lambda e: e.matmul(lbT_ps[0:8, :], gd, Md, start=True, stop=False), [g16, cons], [lbT_ps])
                P.op("pe", lambda e: e.matmul(lbT_ps[0:8, :], lnbd, cslice(C, "ident"), start=False, stop=True), [lnb16, cons], [lbT_ps])
                if CUT < 3.8:
                    continue
                P.op("pe", lambda e: e.matmul(small_a[:, 0:8], Md, gd, start=True, stop=True), [g16, cons], [small_a])
                P.op("pe", lambda e: e.matmul(small_b[:, 0:8], cslice(C, "ones"), gd, start=True, stop=True), [g16, cons], [small_b])
                if CUT < 3.85:
                    continue
                act_copy(P, csT[:, :], csT_ps[0:8, :], [csT_ps], [csT])
                act_copy(P, lbT[:, :], lbT_ps[0:8, :], [lbT_ps], [lbT])
                if CUT < 3.9:
                    continue
                P.op("dve", lambda e: e.tensor_copy(cscol[:, :], small_a[:, 0:8]), [small_a], [cscol])
                P.op("dve", lambda e: e.tensor_copy(clb[:, :], small_b[:, 0:8]), [small_b], [clb])
                if CUT < 3.95:
                    continue
                P.op("act", lambda e: e.activation(ecs[:, :], cscol[:, :], AF.Exp), [cscol], [ecs])
                P.op("act", lambda e: e.activation(decb[:, :], clb[:, :], AF.Exp), [clb], [decb])
                P.op("dve", lambda e: e.tensor_tensor(wend[:, :], clb[:, :], cscol[:, :], ALU.subtract), [clb, cscol], [wend])
                P.op("act", lambda e: e.activation(wend[:, :], wend[:, :], AF.Exp), [wend], [wend])
                P.op("dve", lambda e: e.tensor_tensor(wkb[:, :], ecs[:, :], be16[:, d * 8:(d + 1) * 8], ALU.mult), [ecs, be16], [wkb])
                for h in range(8 if CUT >= 5 else 0):
                    kh = h // 2
                    if h % 2 == 0:
                        P.op("pe", lambda e, kh=kh: e.matmul(QK_ps[:, :], qkT[:, 4 + kh, :], qkT[:, kh, :], start=True, stop=True), [qkT], [QK_ps])
                        P.op("pe", lambda e, kh=kh: e.matmul(KK_ps[:, :], kn32[:, kh, :], kn32[:, kh, :], start=True, stop=True), [kn32], [KK_ps])
                    selh = cslice(C, "sel8", rows=8, lo=h * 128, n=128)
                    P.op("pe", lambda e, selh=selh: e.matmul(csb_ps[:, :], selh, csT[:, :], start=True, stop=True), [cons, csT], [csb_ps])
                    P.op("pe", lambda e, selh=selh: e.matmul(lbb_ps[:, :], selh, lbT[:, :], start=True, stop=True), [cons, lbT], [lbb_ps])
                    csc = cscol[:, h:h + 1]
                    P.op("dve", lambda e, csc=csc: e.scalar_tensor_tensor(tmp[0][:, :], csb_ps[:, :], csc, n_incl, ALU.subtract, ALU.add), [csb_ps, cscol, cons], [tmp[0]])
                    P.op("act", lambda e: e.activation(Ei[0][:, :], tmp[0][:, :], AF.Exp), [tmp[0]], [Ei[0]])
                    P.op("dve", lambda e: e.tensor_tensor(attT[:, :], QK_ps[:, :], Ei[0][:, :], ALU.mult), [QK_ps, Ei[0]], [attT])
                    P.op("dve", lambda e, csc=csc: e.scalar_tensor_tensor(tmp[1][:, :], lbb_ps[:, :], csc, n_N0, ALU.subtract, ALU.add), [lbb_ps, cscol, cons], [tmp[1]])
                    P.op("act", lambda e: e.activation(Ei[1][:, :], tmp[1][:, :], AF.Exp), [tmp[1]], [Ei[1]])
                    P.op("dve", lambda e: e.tensor_tensor(Nk[0][:, :], KK_ps[:, :], Ei[1][:, :], ALU.mult), [KK_ps, Ei[1]], [Nk[0]])
                    P.op("dve", lambda e: e.scalar_tensor_tensor(tmp[2][:, :], csb_ps[:, :], -1.0, n_M0, ALU.mult, ALU.add), [csb_ps, cons], [tmp[2]])
                    P.op("act", lambda e, csc=csc: e.activation(Ei[2][:, :], tmp[2][:, :], AF.Exp, bias=csc), [tmp[2], cscol], [Ei[2]])
                    P.op("dve", lambda e, h=h: e.scalar_tensor_tensor(Mk[0][:, :], KK_ps[:, :], be16[:, d * 8 + h:d * 8 + h + 1], Ei[2][:, :], ALU.mult, ALU.mult),
                         [KK_ps, be16, Ei[2]], [Mk[0]])
                    P.op("act", lambda e: e.activation(ecsb[:, :], csb_ps[:, :], AF.Exp), [csb_ps], [ecsb])
                    P.op("pool", lambda e, kh=kh: e.tensor_tensor(qeT[:, :], qkT[:, kh, :], ecsb[:, :], ALU.mult), [qkT, ecsb], [qeT])
                    P.op("dve", lambda e: e.tensor_tensor(Esb[:, :], Mk[0][:, :], cslice(C, "off"), ALU.mult), [Mk[0], cons], [Esb])
                    P.op("dve", lambda e: e.tensor_tensor(Mk[0][:, :], Mk[0][:, :], cslice(C, "bd"), ALU.mult), [Mk[0], cons], [Mk[0]])
                    P.op("pool", lambda e: e.tensor_tensor(Nk[0][:, :], Nk[0][:, :], cslice(C, "bd"), ALU.mult), [Nk[0], cons], [Nk[0]])
                    P.op("pool", lambda e: e.tensor_tensor(Uk[0][:, :], cslice(C, "ident"), Nk[0][:, :], ALU.subtract), [cons, Nk[0]], [Uk[0]])
                    cm_, cn_, cu_ = 0, 0, 0
                    for lv in range(1, 6):
                        Mo, No, Uo = Mk[cm_], Nk[cn_], Uk[cu_]
                        Mn, Nn, Un = Mk[1 - cm_], Nk[1 - cn_], Uk[1 - cu_]
                        P.op("pe", lambda e, Mo=Mo, No=No: e.matmul(M_ps[:, :], No[:, :], Mo[:, :], start=True, stop=True), [Mo, No], [M_ps])
                        if lv < 5:
                            P.op("pe", lambda e, Mo=Mo, No=No: e.matmul(N_ps[:, :], Mo[:, :], No[:, :], start=True, stop=True), [Mo, No], [N_ps])
                        act_copy(P, Mn[:, :], M_ps[:, :], [M_ps], [Mn])
                        if lv < 5:
                            P.op("dve", lambda e, Nn=Nn: e.tensor_copy(Nn[:, :], N_ps[:, :]), [N_ps], [Nn])
                        P.op("pe", lambda e, Mn=Mn, Uo=Uo: e.matmul(U_ps[:, :], Mn[:, :], Uo[:, :], start=True, stop=True), [Mn, Uo], [U_ps])
                        P.op("dve", lambda e, Un=Un, Uo=Uo: e.tensor_tensor(Un[:, :], U_ps[:, :], Uo[:, :], ALU.add), [U_ps, Uo], [Un])
                        cm_, cn_, cu_ = 1 - cm_, 1 - cn_, 1 - cu_
                    Ubd = Uk[cu_]
                    Ufin = Uk[1 - cu_]
                    P.op("pe", lambda e, Ubd=Ubd: e.matmul(M_ps[:, :], Ubd[:, :], Esb[:, :], start=True, stop=True), [Ubd, Esb], [M_ps])
                    P.op("pe", lambda e, Ubd=Ubd: e.matmul(N_ps[:, :], Esb[:, :], Ubd[:, :], start=True, stop=True), [Ubd, Esb], [N_ps])
                    act_copy(P, Fsb[:, :], M_ps[:, :], [M_ps], [Fsb])
                    P.op("dve", lambda e: e.tensor_copy(Gsb[:, :], N_ps[:, :]), [N_ps], [Gsb])
                    P.op("pe", lambda e: e.matmul(M_ps[:, :], Fsb[:, :], Gsb[:, :], start=True, stop=True), [Fsb, Gsb], [M_ps])
                    P.op("pe", lambda e: e.matmul(N_ps[:, :], Gsb[:, :], Fsb[:, :], start=True, stop=True), [Fsb, Gsb], [N_ps])
                    act_copy(P, F2sb[:, :], N_ps[:, :], [N_ps], [F2sb])
                    P.op("pool", lambda e: e.tensor_tensor(W1[:, :], cslice(C, "ident"), Gsb[:, :], ALU.subtract), [cons, Gsb], [W1])
                    P.op("dve", lambda e: e.tensor_tensor(W1[:, :], W1[:, :], M_ps[:, :], ALU.add), [W1, M_ps], [W1])
                    P.op("pe", lambda e: e.matmul(U_ps[:, :], F2sb[:, :], Gsb[:, :], start=True, stop=True), [F2sb, Gsb], [U_ps])
                    P.op("dve", lambda e: e.tensor_tensor(W1[:, :], W1[:, :], U_ps[:, :], ALU.subtract), [W1, U_ps], [W1])
                    P.op("pe", lambda e, Ubd=Ubd: e.matmul(M_ps[:, :], Ubd[:, :], cslice(C, "ident"), start=True, stop=True), [Ubd, cons], [M_ps])
                    act_copy(P, Tbd[:, :], M_ps[:, :], [M_ps], [Tbd])
                    P.op("pe", lambda e: e.matmul(U_ps[:, :], Tbd[:, :], W1[:, :], start=True, stop=True), [Tbd, W1], [U_ps])
                    P.op("dve", lambda e, Ufin=Ufin: e.tensor_copy(Ufin[:, :], U_ps[:, :]), [U_ps], [Ufin])
                    cu_ = 1 - cu_
                    U = Uk[cu_]
                    if CUT < 7:
                        continue
                    P.op("pool", lambda e, h=h: e.tensor_scalar(vb[:, :], v_tok[:, h, :], be16[:, d * 8 + h:d * 8 + h + 1], None, ALU.mult), [v_tok, be16], [vb])
                    P.op("pool", lambda e, h=h, kh=kh: e.tensor_scalar(kb[:, :], k_tok[:, kh, :], wkb[:, h:h + 1], None, ALU.mult), [k_tok, wkb], [kb])
                    P.op("pool", lambda e, h=h, kh=kh: e.tensor_scalar(kend[:, :], k_tok[:, kh, :], wend[:, h:h + 1], None, ALU.mult), [k_tok, wend], [kend])
                    P.op("pe", lambda e, U=U: e.matmul(u_ps[:, :], U[:, :], vb[:, :], start=True, stop=True), [U, vb], [u_ps])
                    P.op("pe", lambda e, U=U: e.matmul(wT_ps[:, :], kb[:, :], U[:, :], start=True, stop=True), [U, kb], [wT_ps])
                    act_copy(P, u_sb[:, :], u_ps[:, :], [u_ps], [u_sb])
                    act_copy(P, wT[:, :], wT_ps[:, :], [wT_ps], [wT])
                    P.op("pe", lambda e, h=h: e.matmul(vn_ps[:, :], wT[:, :], S_bf[:, h, :], start=True, stop=True), [wT, S_bf], [vn_ps])
                    P.op("dve", lambda e: e.tensor_tensor(vnew[:, :], u_sb[:, :], vn_ps[:, :], ALU.subtract), [u_sb, vn_ps], [vnew])
                    P.op("pe", lambda e, h=h: e.matmul(o_ps[:, :], qeT[:, :], S_bf[:, h, :], start=True, stop=False), [qeT, S_bf], [o_ps])
                    P.op("pe", lambda e: e.matmul(o_ps[:, :], attT[:, :], vnew[:, :], start=False, stop=True), [attT, vnew], [o_ps])
                    act_copy(P, ys[:, h * 128:(h + 1) * 128], o_ps[:, :], [o_ps], [ys])
                    P.op("pe", lambda e: e.matmul(S_ps[:, :], kend[:, :], vnew[:, :], start=True, stop=True), [kend, vnew], [S_ps])
                    P.op("dve", lambda e, h=h: e.scalar_tensor_tensor(S_f[:, h, :], S_f[:, h, :], decb[:, h:h + 1], S_ps[:, :], ALU.mult, ALU.add),
                         [S_f, decb, S_ps], [S_f])
                    act_copy(P, S_bf[:, h, :], S_f[:, h, :], [S_f], [S_bf])
                P.dma("pool", T["yd"].t[d, s:s + 128, :], ys[:, :], T["yd"], ys)
            cic[0] = ci

        run_dir(0)
        run_dir(1)


def phase_odd_epilogue(C):
    P, cfg, T = C.P, C.cfg, C.T
    with Scope(P):
        nw = P.sbuf("onw", [128, 128], F32)
        P.dma("sp", nw[:, :], T["od_vec"].t[32:160].partition_broadcast(128), nw, T["od_vec"])
        NB = 2
        yf = [P.sbuf(f"oyf{i}", [128, 1024], F32) for i in range(NB)]
        yr = [P.sbuf(f"oyr{i}", [128, 1024], F32) for i in range(NB)]
        zg = [P.sbuf(f"ozg{i}", [128, 1024], F32) for i in range(NB)]
        sz = P.sbuf("osz", [128, 1024], F32)
        junk = P.sbuf("ojunk", [128, 128], F32)
        ymix = P.sbuf("oymix", [128, 1024], BF16)
        ss = P.sbuf("oss", [128, 8], F32)
        rstd = P.sbuf("orstd", [128, 8], F32)
        yT = [P.sbuf(f"oyT{i}", [128, 8, 128], BF16) for i in range(2)]
        pst = [P.psum(f"oetp{i}", [128, 4, 128], BF16) for i in range(2)]
        for ci in range(cfg.TA // 128):
            s = ci * 128
            a, b, z = yf[ci % NB], yr[ci % NB], zg[ci % NB]
            P.dma("sp", a[:, :], T["yd"].t[0, s:s + 128, :], a, T["yd"])
            P.dma("sp", b[:, :], T["yd"].t[1, s:s + 128, :], b, T["yd"])
            P.dma("sp", z[:, :], T["tm_pre"].t[s:s + 128, 0:1024], z, T["tm_pre"])
            P.op("dve", lambda e, a=a, b=b: e.tensor_tensor(a[:, :], a[:, :], b[:, :], ALU.add), [a, b], [a])
            P.op("act", lambda e, z=z: e.activation(sz[:, :], z[:, :], AF.Silu), [z], [sz])
            for h in range(8):
                sl = slice(h * 128, (h + 1) * 128)
                P.op("act", lambda e, a=a, sl=sl, h=h: e.activation(junk[:, :], a[:, sl], AF.Square, accum_out=ss[:, h:h + 1]), [a], [junk, ss])
            P.op("dve", lambda e: e.tensor_scalar(rstd[:, :], ss[:, :], 1.0 / 128.0, EPS, ALU.mult, ALU.add), [ss], [rstd])
            P.op("act", lambda e: e.activation(rstd[:, :], rstd[:, :], AF.Sqrt), [rstd], [rstd])
            P.op("dve", lambda e: e.reciprocal(rstd[:, :], rstd[:, :]), [rstd], [rstd])
            for h in range(8):
                sl = slice(h * 128, (h + 1) * 128)
                P.op("dve", lambda e, a=a, sl=sl, h=h: e.scalar_tensor_tensor(a[:, sl], a[:, sl], rstd[:, h:h + 1], nw[:, :], ALU.mult, ALU.mult), [a, rstd, nw], [a])
            P.op("pool", lambda e, a=a: e.tensor_tensor(ymix[:, :], a[:, :], sz[:, :], ALU.mult), [a, sz], [ymix])
            yt = yT[ci % 2]
            transpose_rows(C, ymix, 128, 1024, yt, pst)
            store_yT(C, yt, s)


ALL_STAGES = ("ada", "vecs", "prenorm0", "inproj0", "mixer0", "tail0", "prenorm1", "inproj1", "mixer1", "tail1")


def kernel(**inputs):
    cfg = Cfg()
    inp = {k: np.asarray(v) for k, v in inputs.items()}
    nc, C = build_program(cfg, stages=ALL_STAGES, dumps=())
    in_maps = make_in_maps(cfg, inp)
    res = run_bass_kernel_spmd(nc, in_maps, core_ids=list(range(NCORES)))
    out = np.stack([np.concatenate([np.asarray(res.results[b * 4 + g]["out"]).reshape(cfg.NTL, cfg.D) for g in range(4)], 0)
                    for b in range(2)])
    return np.ascontiguousarray(out.astype(np.float32))
```

```python
import math
from contextlib import ExitStack
import numpy as np
import concourse.bass as bass
import concourse.mybir as mybir
from concourse.bass_utils import run_bass_kernel_spmd

F32 = mybir.dt.float32
BF16 = mybir.dt.bfloat16
AF = mybir.ActivationFunctionType
ALU = mybir.AluOpType
AX = mybir.AxisListType

EPOCH = 20000
SEM_LIMIT = 30000
NCORES = 8
XPAIRS = [[0, 4], [1, 5], [2, 6], [3, 7]]
BGROUPS = [[0, 1, 2, 3], [4, 5, 6, 7]]


class Buf:
    def __init__(self, name, handle=None):
        self.name = name
        self.t = handle
        self.w = []
        self.r = {}
        self.dsem = None
        self.dcnt = 0
        self.dold = []
        self.is_dram = False
        self.wd = {}

    def __getitem__(self, idx):
        return self.t[idx]


class View(Buf):
    def __init__(self, name, parent, i):
        self.name = name
        self.par = parent
        self.t = parent.t
        self.i = i
        self.is_dram = False
        self.is_psum = True

    w = property(lambda self: self.par.w, lambda self, v: setattr(self.par, "w", v))
    r = property(lambda self: self.par.r, lambda self, v: setattr(self.par, "r", v))

    def __getitem__(self, idx):
        return self.t[idx[0], self.i, idx[1]]


class Prog:
    ENG = ("pe", "act", "dve", "pool", "sp")

    def __init__(self, nc):
        self.nc = nc
        self.ops = {e: [] for e in self.ENG}
        self.cnt = {e: 0 for e in self.ENG}
        self.esems = {e: [] for e in self.ENG}
        self.known = {e: {} for e in self.ENG}
        self.stack = None
        self.root = None
        self.last = {}
        self.uid = 0
        self.pool = {}
        self.semval = {}
        self.scope_sems = [[]]
        self.cc_sem = None
        self.cc_n = 0

    def new_sem(self, name, scoped=False, eng=None):
        if scoped and self.pool.get(eng):
            sem = self.pool[eng].pop()
        else:
            self.uid += 1
            sem = self.root.enter_context(self.nc.semaphore(f"{name}_{self.uid}"))
            self.semval[id(sem)] = 0
        if scoped:
            self.scope_sems[-1].append((eng, sem))
        return sem

    def sbuf(self, name, shape, dtype):
        self.uid += 1
        t = self.stack.enter_context(self.nc.sbuf_tensor(f"{name}_{self.uid}", list(shape), dtype))
        return Buf(name, t)

    def psum(self, name, shape, dtype=F32):
        self.uid += 1
        t = self.stack.enter_context(self.nc.psum_tensor(f"{name}_{self.uid}", list(shape), dtype))
        b = Buf(name, t)
        b.is_psum = True
        return b

    def psum_views(self, name, n, w=128, dtype=F32):
        self.uid += 1
        t = self.stack.enter_context(self.nc.psum_tensor(f"{name}_{self.uid}", [128, n, w], dtype))
        par = Buf(name, t)
        par.is_psum = True
        return [View(f"{name}{i}", par, i) for i in range(n)]

    def dram(self, name, shape, dtype, kind="Internal"):
        b = Buf(name, self.nc.dram_tensor(name, list(shape), dtype, kind=kind))
        b.is_dram = True
        return b

    def _esem(self, eng, n):
        k = n // EPOCH
        while len(self.esems[eng]) <= k:
            self.esems[eng].append(self.new_sem(f"s_{eng}"))
        return self.esems[eng][k], (n % EPOCH) + 1

    def _deps(self, reads, writes, skip_waw=False):
        deps = {}

        def add(sem, val):
            k = id(sem)
            if k not in deps or deps[k][1] < val:
                deps[k] = (sem, val)

        for b in reads:
            for (s, v) in b.w:
                add(s, v)
            if getattr(b, "is_psum", False):
                for (s, v) in b.r.values():
                    add(s, v)
        for b in writes:
            if not skip_waw:
                for (s, v) in b.w:
                    add(s, v)
            for (s, v) in b.r.values():
                add(s, v)
        return deps

    def _emit_waits(self, eng, deps, skip_ids=()):
        kn = self.known[eng]
        for k, (s, v) in deps.items():
            if k in skip_ids or kn.get(k, 0) >= v:
                continue
            kn[k] = v
            self.ops[eng].append(("wait", s, v))

    def _note(self, sem, val):
        self.last[id(sem)] = (sem, val)

    def op(self, eng, fn, reads=(), writes=()):
        reads = [b for b in reads if b is not None]
        writes = [b for b in writes if b is not None]
        if eng == "pe":
            deps = self._deps(reads, writes)
            skip = {id(s) for s in self.esems["pe"]}
        else:
            deps = self._deps(reads, [])
            own = {id(s) for s in self.esems[eng]}
            for k, sv in self._deps([], writes).items():
                if k in own:
                    continue
                if k not in deps or deps[k][1] < sv[1]:
                    deps[k] = sv
            skip = ()
        self._emit_waits(eng, deps, skip)
        n = self.cnt[eng]
        self.cnt[eng] += 1
        sem, val = self._esem(eng, n)
        self.ops[eng].append(("op", fn, sem, 1))
        rec = (sem, val)
        self._note(sem, val)
        for b in writes:
            b.w = [rec]
            b.r = {}
        for b in reads:
            if b not in writes:
                b.r[id(sem)] = rec

    def dma(self, eng, out_ap, in_ap, dst, src, scratch=False):
        deps = self._deps([src], [dst], skip_waw=(scratch or dst.is_dram))
        self._emit_waits(eng, deps)
        own = dst if not dst.is_dram else (src if not src.is_dram else dst)
        if own.dsem is None:
            own.dsem = {}
        cur = own.dsem.get(eng)
        if cur is None or self.semval[id(cur)] >= SEM_LIMIT:
            if cur is not None:
                own.dold.append((cur, self.semval[id(cur)]))
            own.dsem[eng] = self.new_sem(f"d{eng}_{own.name}", scoped=not own.is_dram, eng=eng)
        sem = own.dsem[eng]
        self.semval[id(sem)] += 16
        val = self.semval[id(sem)]

        def fn(e, out_ap=out_ap, in_ap=in_ap):
            return e.dma_start(out=out_ap, in_=in_ap)

        self.ops[eng].append(("op", fn, sem, 16))
        rec = (sem, val)
        self._note(sem, val)
        if dst.is_dram:
            dst.wd[id(sem)] = rec
            dst.w = list(dst.wd.values())
        else:
            dst.w = list(dst.dold) + [rec]
            dst.r = {}
        if src is not dst:
            src.r[id(sem)] = rec

    def coll(self, kind, groups, out_ap, in_ap, dst, src):
        deps = self._deps([src], [dst], skip_waw=True)
        self._emit_waits("pool", deps)
        if self.cc_sem is None or self.cc_n >= SEM_LIMIT:
            self.cc_sem = self.new_sem("cc")
            self.cc_n = 0
        sem = self.cc_sem
        self.cc_n += 1
        n = self.cc_n

        def fn(e):
            return e.collective_compute(kind, ALU.bypass, replica_groups=groups, ins=[in_ap], outs=[out_ap])

        self.ops["pool"].append(("op", fn, sem, None))
        self.ops["pool"].append(("wait", sem, n))
        self.known["pool"][id(sem)] = n
        rec = (sem, n)
        self._note(sem, n)
        dst.wd[id(sem)] = rec
        dst.w = list(dst.wd.values())
        src.r[id(sem)] = rec

    def wait_all(self, eng, bufs):
        self._emit_waits(eng, self._deps(bufs, []))

    def barrier(self):
        deps = dict(self.last)
        for eng in self.ENG:
            skip = {id(s) for s in self.esems[eng]} if eng == "pe" else ()
            self._emit_waits(eng, dict(deps), skip)

    def emit(self, block):
        engmap = {"pe": "tensor", "act": "scalar", "dve": "vector", "pool": "gpsimd", "sp": "sync"}
        for eng in self.ENG:
            ops = self.ops[eng]
            if not ops:
                continue

            def body(e, ops=ops):
                for o in ops:
                    if o[0] == "wait":
                        e.wait_ge(o[1], o[2])
                    else:
                        ins = o[1](e)
                        if o[3] is None:
                            ins.then_inc(o[2])
                        else:
                            ins.then_inc(o[2], o[3])

            getattr(block, engmap[eng])(body)


class Cfg:
    def __init__(self, D=4096, SEQ=8192, DFF=None):
        self.D = D
        self.SEQ = SEQ
        self.CTX = 256
        self.KC = D // 128
        self.NTL = SEQ // 4
        self.NT = self.NTL + 64
        self.TA = SEQ + 256
        self.DFF = DFF if DFF is not None else -(-8 * D // (3 * 256)) * 256
        self.MODW = 6 * D // 8
        self.GW = math.gcd(self.MODW, 512)
        self.GRID_W = 64
        self.EV_FM = 1280
        self.EV_TM = 1552
        self.OD_FM = 2048
        self.OD_TM = 1056

    def token_tiles(self):
        out = [(0, 64)]
        r = 64
        while r < self.NT:
            n = min(512, self.NT - r)
            out.append((r, n))
            r += n
        return out


def even_cols(g):
    o_z, o_xbc, o_dt = 0, 2048, 2048 + 3072
    o_rq = o_dt + 64
    o_rk = o_rq + 1024
    o_rv = o_rk + 1024
    o_rg = o_rv + 2048
    r = np.arange
    xs = o_xbc + g * 512 + r(512)
    bm = o_xbc + 2048 + g * 128 + r(128)
    cm = o_xbc + 2048 + 512 + g * 128 + r(128)
    rq = o_rq + g * 256 + r(256)
    rk = o_rk + g * 256 + r(256)
    z = o_z + g * 512 + r(512)
    rv = o_rv + g * 512 + r(512)
    rg = o_rg + g * 512 + r(512)
    dt = np.concatenate([o_dt + d * 32 + g * 8 + r(8) for d in range(2)])
    return np.concatenate([xs, bm, cm, rq, rk, z, rv, rg, dt])


def odd_cols(g):
    r = np.arange
    q = g * 512 + r(512)
    k = 2048 + g * 512 + r(512)
    v = 4096 + g * 1024 + r(1024)
    z = 8192 + g * 1024 + r(1024)
    o_b = 8192 + 4096
    beta = np.concatenate([o_b + d * 32 + g * 8 + r(8) for d in range(2)])
    a = np.concatenate([o_b + 64 + d * 32 + g * 8 + r(8) for d in range(2)])
    return np.concatenate([q, k, v, z, beta, a])


EPS = 1e-6


class Ctx:
    pass


def sub(P):
    st = ExitStack()
    return st


class Scope:
    def __init__(self, P):
        self.P = P

    def __enter__(self):
        self.prev = self.P.stack
        self.st = ExitStack()
        self.P.stack = self.st
        self.P.scope_sems.append([])
        return self

    def __exit__(self, *a):
        P = self.P
        P.barrier()
        for eng, sem in P.scope_sems.pop():
            if P.semval[id(sem)] < SEM_LIMIT:
                P.pool.setdefault(eng, []).append(sem)
        self.st.close()
        P.stack = self.prev
        return False


def phase_ada(C):
    P, cfg, T = C.P, C.cfg, C.T
    D, KC, MODW = cfg.D, cfg.KC, cfg.MODW
    GA = 512 if MODW % 512 == 0 else MODW
    kg = min(8, KC)
    with Scope(P):
        sv0 = P.sbuf("sv0", [128, KC, 3], F32)
        sv = P.sbuf("sv", [128, KC, 3], F32)
        for k8 in range(0, KC, 8):
            ke = min(KC, k8 + 8)
            P.dma("sp", sv0[:, k8:ke, :], T["cv3T"].t[k8 * 128:ke * 128, :].rearrange("(kc p) r -> p kc r", p=128), sv0, T["cv3T"])
        P.op("act", lambda e: e.activation(sv[:, :, :], sv0[:, :, :], AF.Silu), [sv0], [sv])
        wts = [P.sbuf(f"adaw{i}", [128, kg, GA], F32) for i in range(2)]
        bias = P.sbuf("adab", [3, 2 * MODW], F32)
        modsb = P.sbuf("modsb", [3, 2 * MODW], F32)
        ps = [P.psum(f"adaps{i}", [3, GA]) for i in range(2)]
        P.dma("sp", bias[:, :], T["ada_b"].t.ap().rearrange("l m -> (l m)").partition_broadcast(3), bias, T["ada_b"])
        it = 0
        for l in range(2):
            for j in range(MODW // GA):
                pt = ps[(l * (MODW // GA) + j) % 2]
                for k0 in range(0, KC, kg):
                    wt = wts[it % 2]
                    it += 1
                    P.dma("sp", wt[:, :, :],
                          T["ada_w"].t[l, k0 * 128:(k0 + kg) * 128, j * GA:(j + 1) * GA].rearrange("(kc p) n -> p kc n", p=128),
                          wt, T["ada_w"])
                    for kk in range(kg):
                        kc = k0 + kk
                        P.op("pe", lambda e, pt=pt, wt=wt, kk=kk, kc=kc: e.matmul(
                            pt[:, :], sv[:, kc, :], wt[:, kk, :], start=(kc == 0), stop=(kc == KC - 1)),
                            [sv, wt], [pt])
                c0 = l * MODW + j * GA
                P.op("dve", lambda e, pt=pt, c0=c0: e.tensor_tensor(modsb[:, c0:c0 + GA], pt[:, :], bias[:, c0:c0 + GA], ALU.add),
                     [pt, bias], [modsb])
        P.dma("pool", T["mod_loc"][:, :], modsb[:, :], T["mod_loc"], modsb)
        P.coll("AllGather", XPAIRS, T["mod_pair"].t.ap().opt(), T["mod_loc"].t.ap().opt(), T["mod_pair"], T["mod_loc"])
        P.coll("AllGather", BGROUPS, T["mod_all"].t.ap().opt(), T["mod_pair"].t.ap().opt(), T["mod_all"], T["mod_pair"])


def phase_vecs(C):
    P, cfg, T = C.P, C.cfg, C.T
    D, MODW = cfg.D, cfg.MODW
    GW = min(512, D)
    CB = min(1024, D)
    with Scope(P):
        sel2 = P.sbuf("sel2", [3, 2], F32)
        P.dma("sp", sel2[:, :], T["sel2"][:, :], sel2, T["sel2"])
        rows = [P.sbuf(f"mrow{i}", [3, CB], F32) for i in range(2)]
        selv = [P.sbuf(f"selv{i}", [2, CB], F32) for i in range(6)]
        nw = [P.sbuf(f"nw{i}", [2, CB], F32) for i in range(4)]
        outv = [P.sbuf(f"outv{i}", [2, CB], F32) for i in range(2)]
        ps = [P.psum(f"vps{i}", [2, GW]) for i in range(2)]
        n = 0
        for l in range(2):
            for cb0 in range(0, D, CB):
                for k in range(4):
                    P.dma("sp", nw[k][:, :], T["norms"].t[l, k, cb0:cb0 + CB].partition_broadcast(2), nw[k], T["norms"])
                for v in range(6):
                    row = rows[v % 2]
                    x = 0
                    while x < CB:
                        Gc = v * D + cb0 + x
                        s_, off = Gc // MODW, Gc % MODW
                        ln = min(CB - x, MODW - off)
                        P.dma("sp", row[:, x:x + ln], T["mod_all"].t[s_ * 3:(s_ + 1) * 3, l * MODW + off:l * MODW + off + ln],
                              row, T["mod_all"])
                        x += ln
                    for j in range(CB // GW):
                        pt = ps[n % 2]
                        n += 1
                        P.op("pe", lambda e, pt=pt, row=row, j=j: e.matmul(pt[:, :], sel2[:, :], row[:, j * GW:(j + 1) * GW],
                                                                        start=True, stop=True), [sel2, row], [pt])
                        P.op("dve", lambda e, pt=pt, v=v, j=j: e.tensor_copy(selv[v][:, j * GW:(j + 1) * GW], pt[:, :]), [pt], [selv[v]])
                combos = [(0, 1, 0, "scale"), (1, 0, None, "copy"), (2, 2, 1, "gate"),
                          (3, 4, 2, "scale"), (4, 3, None, "copy"), (5, 5, 3, "gate")]
                for (oi, mi, wi, kind) in combos:
                    ov = outv[oi % 2]
                    if kind == "scale":
                        P.op("dve", lambda e, ov=ov, mi=mi, wi=wi: e.scalar_tensor_tensor(
                            ov[:, :], selv[mi][:, :], 1.0, nw[wi][:, :], ALU.add, ALU.mult), [selv[mi], nw[wi]], [ov])
                    elif kind == "gate":
                        P.op("dve", lambda e, ov=ov, mi=mi, wi=wi: e.tensor_tensor(ov[:, :], selv[mi][:, :], nw[wi][:, :], ALU.mult),
                             [selv[mi], nw[wi]], [ov])
                    else:
                        P.op("dve", lambda e, ov=ov, mi=mi: e.tensor_copy(ov[:, :], selv[mi][:, :]), [selv[mi]], [ov])
                    P.dma("pool", T["vecs"].t[l, oi, :, cb0:cb0 + CB], ov[:, :], T["vecs"], ov, scratch=True)


def make_ident(C):
    P = C.P
    identf = P.sbuf("identf", [128, 128], F32)
    ident = P.sbuf("ident", [128, 128], BF16)
    P.op("pool", lambda e: e.memset(identf[:, :], 0.0), [], [identf])
    P.op("pool", lambda e: e.affine_select(identf[:, :], identf[:, :], pattern=[[-1, 128]], compare_op=ALU.not_equal,
                                           fill=1.0, base=0, channel_multiplier=1), [identf], [identf])
    P.op("pool", lambda e: e.tensor_copy(ident[:, :], identf[:, :]), [identf], [ident])
    C.ident = ident
    C.identf = identf


def rstd_from_ss(P, rstd, ss, n, inv_d):
    P.op("dve", lambda e: e.tensor_scalar(rstd[:n, :], ss[:n, :], inv_d, EPS, ALU.mult, ALU.add), [ss], [rstd])
    P.op("act", lambda e: e.activation(rstd[:n, :], rstd[:n, :], AF.Sqrt), [rstd], [rstd])
    P.op("dve", lambda e: e.reciprocal(rstd[:n, :], rstd[:n, :]), [rstd], [rstd])


def transpose_rows(C, src, n, nfeat, dstT, pst, col0=0):
    P = C.P
    nk = nfeat // 128
    for k0 in range(0, nk, 4):
        pt = pst[(k0 // 4) % len(pst)]
        kk = min(4, nk - k0)
        for q in range(kk):
            kc = k0 + q
            P.op("pe", lambda e, pt=pt, q=q, kc=kc: e.transpose(pt[:, q, :n], src[:n, kc * 128:(kc + 1) * 128], C.ident[:n, :n]),
                 [src, C.ident], [pt])
        P.op("act", lambda e, pt=pt, k0=k0, kk=kk: e.activation(dstT[:, k0:k0 + kk, col0:col0 + n], pt[:, :kk, :n], AF.Copy),
             [pt], [dstT])


def phase_prenorm(C, l, xin):
    P, cfg, T = C.P, C.cfg, C.T
    D, KC, NT = cfg.D, cfg.KC, cfg.NT
    with Scope(P):
        g1 = P.sbuf("g1", [128, D], F32)
        sh1 = P.sbuf("sh1", [128, D], F32)
        xt = [P.sbuf(f"xt{i}", [128, D], F32) for i in range(2)]
        junk = P.sbuf("junk", [128, D], BF16)
        tt = P.sbuf("tt", [128, D], F32)
        hb = [P.sbuf(f"hb{i}", [128, D], BF16) for i in range(2)]
        hT = [P.sbuf(f"hT{i}", [128, KC, 128], BF16) for i in range(2)]
        ss = P.sbuf("ss", [128, 1], F32)
        rstd = P.sbuf("rstd", [128, 1], F32)
        pst = [P.psum(f"tps{i}", [128, 4, 128], BF16) for i in range(2)]
        tiles = [(0, 64, 1)] + [(r, 128, 0) for r in range(64, NT, 128)]
        curvar = None
        for i, (r0, n, var) in enumerate(tiles):
            if var != curvar:
                P.dma("sp", g1[:, :], T["vecs"].t[l, 0, var, :].partition_broadcast(128), g1, T["vecs"])
                P.dma("sp", sh1[:, :], T["vecs"].t[l, 1, var, :].partition_broadcast(128), sh1, T["vecs"])
                curvar = var
            x = xt[i % 2]
            h = hb[i % 2]
            ht = hT[i % 2]
            P.dma("sp", x[:n, :], xin[r0:r0 + n, :], x, xin)
            P.op("act", lambda e, x=x, n=n: e.activation(junk[:n, :], x[:n, :], AF.Square, accum_out=ss[:n, :]), [x], [junk, ss])
            rstd_from_ss(P, rstd, ss, n, 1.0 / D)
            P.op("dve", lambda e, x=x, n=n: e.scalar_tensor_tensor(tt[:n, :], x[:n, :], rstd[:n, 0:1], g1[:n, :], ALU.mult, ALU.mult),
                 [x, rstd, g1], [tt])
            P.op("pool", lambda e, h=h, n=n: e.tensor_tensor(h[:n, :], tt[:n, :], sh1[:n, :], ALU.add), [tt, sh1], [h])
            transpose_rows(C, h, n, D, ht, pst)
            for k8 in range(0, KC, 8):
                ke = min(KC, k8 + 8)
                P.dma("pool", T["hT_loc"].t[k8 * 128:ke * 128, r0:r0 + n].rearrange("(kc p) t -> p kc t", p=128), ht[:, k8:ke, :n],
                      T["hT_loc"], ht, scratch=True)
        ag_chunks(C, BGROUPS, 4, T["hallT"], T["hT_loc"], 128)


def declare_tensors(C):
    P, cfg = C.P, C.cfg
    D, NT, MODW, DFF = cfg.D, cfg.NT, cfg.MODW, cfg.DFF
    T = {}
    ext = lambda n, s, dt=F32: T.__setitem__(n, P.dram(n, s, dt, kind="ExternalInput"))
    ext("xin", [NT, D])
    ext("cv3T", [D, 3])
    ext("sel2", [3, 2])
    ext("ada_w", [2, D, MODW])
    ext("ada_b", [2, MODW])
    ext("norms", [2, 4, D])
    itn = lambda n, s, dt=F32: T.__setitem__(n, P.dram(n, s, dt))
    itn("mod_loc", [3, 2 * MODW])
    itn("mod_pair", [6, 2 * MODW])
    itn("mod_all", [NCORES * 3, 2 * MODW])
    itn("vecs", [2, 6, 2, D])
    itn("hT_loc", [D, NT], BF16)
    itn("hallT", [4 * D, NT], BF16)
    NIN = max(cfg.EV_FM + cfg.EV_TM, cfg.OD_FM + cfg.OD_TM)
    ext("ev_w_in", [D, cfg.EV_FM + cfg.EV_TM])
    ext("od_w_in", [D, cfg.OD_FM + cfg.OD_TM])
    ext("od_cw", [128, 16, 5])
    ext("od_vec", [160])
    itn("w_in_bf", [D, NIN], BF16)
    itn("fm_pre", [max(cfg.EV_FM, cfg.OD_FM), cfg.TA])
    itn("tm_pre", [cfg.TA, max(cfg.EV_TM, cfg.OD_TM)])
    C.ncons = make_consts(cfg).shape[1]
    ext("consts", [128, C.ncons])
    ext("rot_cos", [128, cfg.SEQ])
    ext("rot_sin", [128, cfg.SEQ])
    ext("ev_cw", [128, 6, 5])
    ext("ev_cb", [128, 6])
    ext("ev_vec", [1088])
    itn("xs_keep", [cfg.TA, 512], BF16)
    itn("yd", [2, cfg.TA, 1024])
    itn("yT_loc", [4 * 1024, cfg.NT], BF16)
    itn("yallT", [16 * 1024, cfg.NT], BF16)
    ext("w_out", [2, 4096 // 4, D])
    ext("ffn_w1", [2, D // 4, DFF])
    ext("ffn_w3", [2, D // 4, DFF])
    ext("ffn_w2", [2, DFF // 4, D])
    for l in range(2):
        for nm, rows, cols in (("wo", 4096, D), ("w1", D, DFF), ("w3", D, DFF), ("w2", DFF, D)):
            itn(f"{nm}_s{l}", [rows // 4, cols], BF16)
            itn(f"{nm}_f{l}", [rows, cols], BF16)
    itn("o_loc", [NT, D])
    itn("s_loc", [NT, D])
    itn("f_loc", [NT, D])
    itn("x1_loc", [NT, D])
    itn("h2T_loc", [D, NT], BF16)
    T["out"] = P.dram("out", [cfg.NTL, D], F32, kind="ExternalOutput")
    C.T = T


def build_program(cfg, stages=("ada", "vecs", "prenorm0"), dumps=()):
    nc = bass.Bass(target_bir_lowering=False)
    C = Ctx()
    C.cfg = cfg
    C.P = P = Prog(nc)
    declare_tensors(C)
    T = C.T
    outs = {}
    dumps = [d if isinstance(d, tuple) else (d, None, None) for d in dumps]
    for name, shp, _ in dumps:
        b = T[name]
        outs[name] = P.dram("dump_" + name, list(shp or b.t.shape), b.t.dtype, kind="ExternalOutput")
    with ExitStack() as root:
        P.root = root
        P.stack = root
        make_ident(C)
        C.rk = nc.sync.partition_id() % 4
        if "ada" in stages:
            phase_ada(C)
        if "vecs" in stages:
            phase_vecs(C)
        if "prenorm0" in stages:
            phase_prenorm(C, 0, T["xin"])
        if "inproj0" in stages:
            cast_weight(C, T["w_in_bf"], T["ev_w_in"], cfg.D, cfg.EV_FM + cfg.EV_TM)
            phase_inproj(C, 0, T["w_in_bf"], cfg.EV_FM, cfg.EV_TM)
        if "mixer0" in stages:
            phase_mixer_even(C)
            phase_even_epilogue(C)
        if "tail0" in stages:
            import os
            cut = int(os.environ.get("TAILCUT", "9"))
            ag_chunks(C, BGROUPS, 4, T["yallT"], T["yT_loc"], 128)
            if cut >= 2:
                prep_weights(C, 0)
            if cut >= 3:
                phase_outproj(C, 0, True)
            if cut >= 4:
                phase_postmix(C, 0, T["xin"], True)
            if cut >= 5:
                phase_ffn(C, 0, True)
            if cut >= 6:
                phase_postffn(C, 0, T["x1_loc"], True)
        if "prenorm1x" in stages:
            phase_prenorm(C, 1, T["xin"])
        if "prenorm1" in stages:
            phase_prenorm(C, 1, T["x1_loc"])
        if "inproj1" in stages:
            cast_weight(C, T["w_in_bf"], T["od_w_in"], cfg.D, cfg.OD_FM + cfg.OD_TM)
            phase_inproj(C, 1, T["w_in_bf"], cfg.OD_FM, cfg.OD_TM)
        if "mixer1" in stages:
            phase_mixer_odd(C)
            phase_odd_epilogue(C)
        if "tail1" in stages:
            ag_chunks(C, BGROUPS, 4, T["yallT"], T["yT_loc"], 128)
            prep_weights(C, 1)
            phase_outproj(C, 1, False)
            phase_postmix(C, 1, T["x1_loc"], False)
            phase_ffn(C, 1, False)
            phase_postffn(C, 1, T["out"], False, dst_row_off=64)
            P.wait_all("pool", [T["out"]])
        for name, shp, sl in dumps:
            P.dma("pool", outs[name].t.ap(), sl(T[name].t) if sl else T[name].t.ap(), outs[name], T[name])
        P.wait_all("pool", list(outs.values()))
        P.barrier()
        with nc.Block() as block:
            P.emit(block)
    return nc, C


class GemmRes:
    def __init__(self, C, kg, nbanks=4, nsets=2, nw=3, tag="g"):
        P = C.P
        self.kg = kg
        self.wp = [P.sbuf(f"{tag}wp{i}", [128, kg, 512], BF16) for i in range(nw)]
        self.ps = [[P.psum(f"{tag}ps{s}_{i}", [128, 512]) for i in range(nbanks)] for s in range(nsets)]
        self.nbanks = nbanks
        self.wi = 0
        self.si = 0


def gemm(C, G, mode, xT, T_, W, K, col0, ncols, evac, wrow0=0):
    P = C.P
    KC = K // 128
    kg = G.kg
    ngw = G.nbanks * 128 if mode == "FM" else 512
    for c0 in range(col0, col0 + ncols, ngw):
        nw = min(ngw, col0 + ncols - c0)
        pset = G.ps[G.si % len(G.ps)]
        G.si += 1
        if mode == "FM":
            subs = [(q, c0 + q * 128, min(128, c0 + nw - (c0 + q * 128))) for q in range((nw + 127) // 128)]
        else:
            subs = [(q, q * 128, min(128, T_ - q * 128)) for q in range((T_ + 127) // 128)]
        for k0 in range(0, KC, kg):
            kk = min(kg, KC - k0)
            wp = G.wp[G.wi % len(G.wp)]
            G.wi += 1
            P.dma("sp", wp[:, :kk, :nw],
                  W.t[wrow0 + k0 * 128:wrow0 + (k0 + kk) * 128, c0:c0 + nw].rearrange("(kc p) n -> p kc n", p=128), wp, W)
            for (q, a0, an) in subs:
                pt = pset[q]
                for j in range(kk):
                    kc = k0 + j
                    if mode == "FM":
                        P.op("pe", lambda e, pt=pt, wp=wp, j=j, kc=kc, a0=a0, an=an, c0=c0: e.matmul(
                            pt[:an, :T_], wp[:, j, a0 - c0:a0 - c0 + an], xT[:, kc, :T_], start=(kc == 0), stop=(kc == KC - 1)),
                            [wp, xT], [pt])
                    else:
                        P.op("pe", lambda e, pt=pt, wp=wp, j=j, kc=kc, a0=a0, an=an, nw=nw: e.matmul(
                            pt[:an, :nw], xT[:, kc, a0:a0 + an], wp[:, j, :nw], start=(kc == 0), stop=(kc == KC - 1)),
                            [wp, xT], [pt])
        for (q, a0, an) in subs:
            if mode == "FM":
                evac(a0, pset[q], an)
            else:
                evac(a0, an, c0, nw, pset[q])


def cast_weight(C, dst, src, rows, ncols, rstep=512, drow0=0):
    P = C.P
    for r0 in range(0, rows, rstep):
        rn = min(rstep, rows - r0)
        P.dma("pool", dst.t[drow0 + r0:drow0 + r0 + rn, :ncols], src.t[r0:r0 + rn, :ncols], dst, src, scratch=True)


def seq_blocks(cfg):
    out = []
    for r in range(4):
        out.append((r, 0, 64, r * 64))
        for t0 in range(0, cfg.NTL, 512):
            n = min(512, cfg.NTL - t0)
            out.append((r, 64 + t0, n, 256 + r * cfg.NTL + t0))
    return out


def phase_inproj(C, l, wbf, nfm, ntm):
    P, cfg, T = C.P, C.cfg, C.T
    D, KC = cfg.D, cfg.KC
    with Scope(P):
        G = GemmRes(C, min(8, KC))
        xts = [P.sbuf(f"ipx{i}", [128, KC, 512], BF16) for i in range(2)]
        stg = [P.sbuf(f"ipst{i}", [128, 512], F32) for i in range(4)]
        si = [0]
        for bi, (r, row0, n, sp0) in enumerate(seq_blocks(cfg)):
            xT = xts[bi % 2]
            hv = T["hallT"].t.ap().rearrange("(kc r p) t -> kc r p t", kc=KC, r=4)
            for k8 in range(0, KC, 8):
                ke = min(KC, k8 + 8)
                P.dma("sp", xT[:, k8:ke, :n], hv[k8:ke, r, :, row0:row0 + n].rearrange("kc p t -> p kc t"), xT, T["hallT"])

            def evac_fm(c0, pt, nf, n=n, sp0=sp0):
                st = stg[si[0] % 4]
                eng = "act" if si[0] % 2 == 0 else "dve"
                si[0] += 1
                if eng == "act":
                    P.op("act", lambda e: e.activation(st[:nf, :n], pt[:nf, :n], AF.Copy), [pt], [st])
                else:
                    P.op("dve", lambda e: e.tensor_copy(st[:nf, :n], pt[:nf, :n]), [pt], [st])
                P.dma("pool", T["fm_pre"].t[c0:c0 + nf, sp0:sp0 + n], st[:nf, :n], T["fm_pre"], st)

            def evac_tm(a0, an, c0, nw, pt, n=n, sp0=sp0):
                st = stg[si[0] % 4]
                eng = "act" if si[0] % 2 == 0 else "dve"
                si[0] += 1
                if eng == "act":
                    P.op("act", lambda e: e.activation(st[:an, :nw], pt[:an, :nw], AF.Copy), [pt], [st])
                else:
                    P.op("dve", lambda e: e.tensor_copy(st[:an, :nw], pt[:an, :nw]), [pt], [st])
                P.dma("pool", T["tm_pre"].t[sp0 + a0:sp0 + a0 + an, c0 - nfm:c0 - nfm + nw], st[:an, :nw], T["tm_pre"], st)

            gemm(C, G, "FM", xT, n, wbf, D, 0, nfm, evac_fm)
            gemm(C, G, "TM", xT, n, wbf, D, nfm, ntm, evac_tm)


def make_in_maps(cfg, inp):
    D = cfg.D
    maps = []
    for cidx in range(NCORES):
        b, g = cidx // 4, cidx % 4
        sidx = g * 2 + b
        m = {}
        m["xin"] = np.concatenate([inp["ctx"][b, g * 64:(g + 1) * 64], inp["x"][b, g * cfg.NTL:(g + 1) * cfg.NTL]], 0)
        m["cv3T"] = np.ascontiguousarray(np.stack([inp["c"][0], inp["c"][1], inp["c_ctx"]], 1))
        sel2 = np.zeros((3, 2), np.float32)
        sel2[b, 0] = 1
        sel2[2, 1] = 1
        m["sel2"] = sel2
        m["ada_w"] = np.ascontiguousarray(inp["ada_w"][:, :, sidx * cfg.MODW:(sidx + 1) * cfg.MODW])
        m["ada_b"] = np.ascontiguousarray(inp["ada_b"][:, sidx * cfg.MODW:(sidx + 1) * cfg.MODW])
        m["norms"] = np.ascontiguousarray(np.stack([inp["norm_mix_pre"], inp["norm_mix_post"], inp["norm_ffn_pre"], inp["norm_ffn_post"]], 1))
        ecols = even_cols(g)
        m["ev_w_in"] = np.ascontiguousarray(inp["ev_w_in"][0][:, ecols])
        ocols = odd_cols(g)
        m["od_w_in"] = np.ascontiguousarray(inp["od_w_in"][0][:, ocols])
        m["od_cw"] = np.ascontiguousarray(inp["od_conv_w"][0][:, ocols[:2048]].T.reshape(16, 128, 5).transpose(1, 0, 2))
        vh = slice(g * 8, (g + 1) * 8)
        m["od_vec"] = np.concatenate([inp["od_dt_bias"][0][:, vh].reshape(-1), inp["od_a_log"][0][:, vh].reshape(-1),
                                      inp["od_norm"][0]]).astype(np.float32)
        m["consts"] = make_consts(cfg)
        m["rot_cos"], m["rot_sin"] = rotary_tables(cfg)
        ccols = ecols[:768] - 2048
        m["ev_cw"] = np.ascontiguousarray(inp["ev_conv_w"][0][:, ccols].T.reshape(6, 128, 5).transpose(1, 0, 2))
        m["ev_cb"] = np.ascontiguousarray(inp["ev_conv_b"][0][ccols].reshape(6, 128).T)
        hs = slice(g * 8, (g + 1) * 8)
        rs = slice(g * 4, (g + 1) * 4)
        perm_e = np.concatenate([np.concatenate([gq * 512 + np.arange(512), 2048 + gq * 512 + np.arange(512)]) for gq in range(4)])

        def qshard(W):
            K_, N_ = W.shape
            rn = pick_rn(K_ // 4, N_)
            idx = np.concatenate([(c * 4 + g) * rn + np.arange(rn) for c in range(K_ // 4 // rn)])
            return W[idx]

        m["w_out"] = np.ascontiguousarray(np.stack([qshard(inp["ev_w_out"][0][perm_e]), qshard(inp["od_w_out"][0])]))
        m["ffn_w1"] = np.ascontiguousarray(np.stack([qshard(inp["ffn_w1"][l_]) for l_ in range(2)]))
        m["ffn_w3"] = np.ascontiguousarray(np.stack([qshard(inp["ffn_w3"][l_]) for l_ in range(2)]))
        m["ffn_w2"] = np.ascontiguousarray(np.stack([qshard(inp["ffn_w2"][l_]) for l_ in range(2)]))
        m["ev_vec"] = np.concatenate([
            inp["ev_dt_bias"][0][:, hs].reshape(-1), inp["ev_a_log"][0][:, hs].reshape(-1),
            inp["ev_ret_decay"][0][:, rs].reshape(-1), np.zeros(24, np.float32),
            np.repeat(inp["ev_d_skip"][0][hs], 64), inp["ev_ssd_norm"][0][g * 512:(g + 1) * 512]]).astype(np.float32)
        maps.append(m)
    return maps


NEG = -30000.0
CO = {}


def make_consts(cfg):
    i = np.arange(128)[None, :].astype(np.float32)
    j = np.arange(128)[:, None].astype(np.float32)
    parts = []

    def add(name, arr):
        CO[name] = (sum(p.shape[1] for p in parts), arr.shape[1])
        parts.append(arr.astype(np.float32))

    add("mf", (i >= j) * 1.0)
    add("mr", (i <= j) * 1.0)
    add("nf", np.where(i >= j, 0.0, NEG))
    add("nr", np.where(i <= j, 0.0, NEG))
    add("nfs", np.where(i > j, 0.0, NEG))
    add("nrs", np.where(i < j, 0.0, NEG))
    add("df", np.maximum(i - j, 0.0))
    add("dr", np.maximum(j - i, 0.0))
    p = np.arange(128, dtype=np.float32)[:, None]
    add("pos", np.concatenate([p + 1, 128 - p, 127 - p, p], 1))
    add("ones", np.ones((128, 128)))
    sel8 = np.zeros((128, 8 * 128))
    for h in range(8):
        sel8[h, h * 128:(h + 1) * 128] = 1.0
    add("sel8", sel8)
    m = np.arange(128)
    perm = np.where((m % 64) < 32, m + 32, m - 32)
    pm = np.zeros((128, 128))
    pm[perm, m] = 1.0
    add("pm", pm)
    add("ident", np.eye(128))
    bd = ((np.arange(128)[:, None] // 32) == (np.arange(128)[None, :] // 32)) * 1.0
    add("bd", bd)
    add("off", 1.0 - bd)
    return np.concatenate(parts, 1)


def rotary_tables(cfg):
    L = cfg.SEQ
    rows = L // cfg.GRID_W
    row = np.repeat(np.arange(rows, dtype=np.float32), cfg.GRID_W)
    col = np.tile(np.arange(cfg.GRID_W, dtype=np.float32), rows)
    nf = 16
    freqs = (10000.0 ** (-np.arange(nf, dtype=np.float32) / nf)).astype(np.float32)
    ang = np.concatenate([row[:, None] * freqs, col[:, None] * freqs], -1).astype(np.float32)
    cos = np.cos(ang).astype(np.float32).T
    sin = np.sin(ang).astype(np.float32).T
    cos2 = np.concatenate([cos, cos, cos, cos], 0)
    sins = np.concatenate([-sin, sin, -sin, sin], 0)
    return np.ascontiguousarray(cos2), np.ascontiguousarray(sins)


def make_cslice(cons):
    def cslice(C, name, rows=128, lo=0, n=None):
        o, w = CO[name]
        n = w if n is None else n
        return cons[:rows, o + lo:o + lo + n]
    return cslice


def chunk_list(cfg, d):
    nctx = 2
    nlat = cfg.SEQ // 128
    ctx = [(k * 128, True, k == 0, k == nctx - 1, 0) for k in range(nctx)]
    lat = [(256 + k * 128, False, k == 0, k == nlat - 1, k * 128) for k in range(nlat)]
    if d == 0:
        return ctx + lat
    return ctx[::-1] + lat[::-1]


def act_copy(P, out_ap, in_ap, reads, writes, scale=None):
    if scale is None:
        P.op("act", lambda e: e.activation(out_ap, in_ap, AF.Copy), reads, writes)
    else:
        P.op("act", lambda e: e.activation(out_ap, in_ap, AF.Copy, scale=scale), reads, writes)


def phase_mixer_even(C):
    P, cfg, T = C.P, C.cfg, C.T
    TA = cfg.TA
    fm, tm = T["fm_pre"], T["tm_pre"]
    with Scope(P):
        cons = P.sbuf("cons", [128, C.ncons], F32)
        cslice = make_cslice(cons)
        P.dma("sp", cons[:, :], T["consts"].t.ap(), cons, T["consts"])
        cw = P.sbuf("cw", [128, 6, 5], F32)
        cb = P.sbuf("cb", [128, 6], F32)
        P.dma("sp", cw[:, :, :], T["ev_cw"].t.ap(), cw, T["ev_cw"])
        P.dma("sp", cb[:, :], T["ev_cb"].t.ap(), cb, T["ev_cb"])
        vec = P.sbuf("evvec", [128, 48], F32)
        P.dma("sp", vec[:, :40], T["ev_vec"].t[0:40].partition_broadcast(128), vec, T["ev_vec"])
        negA = P.sbuf("negA", [128, 16], F32)
        P.op("act", lambda e: e.activation(negA[:, :], vec[:, 16:32], AF.Exp), [vec], [negA])
        P.op("dve", lambda e: e.tensor_scalar(negA[:, :], negA[:, :], -1.0, None, ALU.mult), [negA], [negA])
        ee = P.sbuf("ee", [128, 8], F32)
        lg = P.sbuf("lg", [128, 8], F32)
        P.op("act", lambda e: e.activation(ee[:, :], vec[:, 32:40], AF.Exp, scale=-1.0), [vec], [ee])
        P.op("dve", lambda e: e.tensor_scalar(lg[:, :], ee[:, :], -0.25, 1.0 / 3.0, ALU.mult, ALU.add), [ee], [lg])
        P.op("dve", lambda e: e.tensor_tensor(lg[:, :], lg[:, :], ee[:, :], ALU.mult), [lg, ee], [lg])
        P.op("dve", lambda e: e.tensor_scalar(lg[:, :], lg[:, :], -1.0, 0.5, ALU.mult, ALU.add), [lg], [lg])
        P.op("dve", lambda e: e.tensor_tensor(lg[:, :], lg[:, :], ee[:, :], ALU.mult), [lg, ee], [lg])
        P.op("dve", lambda e: e.tensor_scalar(lg[:, :], lg[:, :], -1.0, 1.0, ALU.mult, ALU.add), [lg], [lg])
        P.op("dve", lambda e: e.tensor_tensor(lg[:, :], lg[:, :], ee[:, :], ALU.mult), [lg, ee], [lg])
        P.op("dve", lambda e: e.tensor_scalar(lg[:, :], lg[:, :], -1.0, None, ALU.mult), [lg], [lg])
        decc = P.sbuf("decc", [128, 8, 128], F32)
        gi = P.sbuf("gi", [128, 8], F32)
        ge = P.sbuf("ge", [128, 8], F32)
        g128 = P.sbuf("g128", [128, 8], F32)
        P.op("act", lambda e: e.activation(g128[:, :], lg[:, :], AF.Exp, scale=128.0), [lg], [g128])
        for d in range(2):
            for r in range(4):
                k = d * 4 + r
                dist = cslice(C, "df" if d == 0 else "dr")
                msk = cslice(C, "mf" if d == 0 else "mr")
                P.op("act", lambda e, k=k, dist=dist: e.activation(decc[:, k, :], dist, AF.Exp, scale=lg[:, k:k + 1]), [cons, lg], [decc])
                P.op("dve", lambda e, k=k, msk=msk: e.scalar_tensor_tensor(decc[:, k, :], decc[:, k, :], 0.125, msk, ALU.mult, ALU.mult),
                     [decc, cons], [decc])
                pi = cslice(C, "pos", lo=(0 if d == 0 else 1), n=1)
                pe_ = cslice(C, "pos", lo=(2 if d == 0 else 3), n=1)
                P.op("act", lambda e, k=k, pi=pi: e.activation(gi[:, k:k + 1], pi, AF.Exp, scale=lg[:, k:k + 1]), [cons, lg], [gi])
                P.op("act", lambda e, k=k, pe_=pe_: e.activation(ge[:, k:k + 1], pe_, AF.Exp, scale=lg[:, k:k + 1]), [cons, lg], [ge])
        P.op("dve", lambda e: e.tensor_scalar(gi[:, :], gi[:, :], 0.125, None, ALU.mult), [gi], [gi])

        NB = 2
        win = [P.sbuf(f"win{i}", [128, 6, 132], F32) for i in range(NB)]
        rqk = [P.sbuf(f"rqk{i}", [128, 4, 128], F32) for i in range(NB)]
        cosT = [P.sbuf(f"cosT{i}", [128, 128], F32) for i in range(NB)]
        sinT = [P.sbuf(f"sinT{i}", [128, 128], F32) for i in range(NB)]
        rvdt = [P.sbuf(f"rvdt{i}", [128, 528], F32) for i in range(NB)]
        acc = [P.sbuf(f"cacc{i}", [128, 128], F32) for i in range(2)]
        cvT = P.sbuf("cvT", [128, 6, 128], BF16)
        xs_tok = P.sbuf("xs_tok", [128, 512], BF16)
        bm_tok = P.sbuf("bm_tok", [128, 128], BF16)
        qkT = P.sbuf("qkT", [128, 4, 128], BF16)
        rk_tok = P.sbuf("rk_tok", [128, 256], BF16)
        rv_bf = P.sbuf("rv_bf", [128, 512], BF16)
        t1 = [P.sbuf(f"rt1_{i}", [128, 128], F32) for i in range(2)]
        t2 = [P.sbuf(f"rt2_{i}", [128, 128], F32) for i in range(2)]
        x16 = P.sbuf("x16", [128, 16], F32)
        dt16 = P.sbuf("dt16", [128, 16], F32)
        la16 = P.sbuf("la16", [128, 16], F32)
        csT = P.sbuf("csT", [8, 128], F32)
        cscol = P.sbuf("cscol", [128, 8], F32)
        clb = P.sbuf("clb", [128, 8], F32)
        ecs = P.sbuf("ecs", [128, 8], F32)
        decb = P.sbuf("decb", [128, 8], F32)
        wgt = P.sbuf("wgt", [128, 8], F32)
        GT = P.sbuf("GT", [128, 128], F32)
        tmpE = [P.sbuf(f"tmpE{i}", [128, 128], F32) for i in range(2)]
        EE = [P.sbuf(f"EE{i}", [128, 128], F32) for i in range(2)]
        AT = [P.sbuf(f"AT{i}", [128, 128], BF16) for i in range(2)]
        yi_sb = [P.sbuf(f"yi_sb{i}", [128, 128], F32) for i in range(2)]
        xw = [P.sbuf(f"xw{i}", [128, 128], BF16) for i in range(2)]
        y_sb = [P.sbuf(f"y_sb{i}", [128, 1024], F32) for i in range(2)]
        S_f = P.sbuf("S_f", [128, 8, 64], F32)
        S_bf = P.sbuf("S_bf", [128, 8, 64], BF16)
        Sr_f = P.sbuf("Sr_f", [128, 2, 128], F32)
        Sr_bf = P.sbuf("Sr_bf", [128, 2, 128], BF16)
        tp4 = P.psum("tp4", [128, 4, 128], BF16)
        tp1 = P.psum("tp1", [128, 2, 128], BF16)
        bk3 = P.psum_views("bk3", 4)
        sw_ps = bk3[0:2]
        GT_ps = bk3[2]
        csT_ps = bk3[3]
        bk4 = P.psum_views("bk4", 4)
        csb_ps = bk4[0:2]
        small_a, small_b = bk4[2], bk4[3]
        bk5 = P.psum_views("bk5", 4)
        y_ps = bk5[0:2]
        yi_ps = bk5[2:4]
        bk6 = P.psum_views("bk6", 4)
        S_ps = bk6[0:2]
        GTr_ps = bk6[2:4]

        cic = [0]

        def run_dir(d):
            ci = cic[0]
            Md = cslice(C, "mf" if d == 0 else "mr")
            negm = cslice(C, "nf" if d == 0 else "nr")
            P.op("pool", lambda e: e.memset(S_f[:, :, :], 0.0), [], [S_f])
            P.op("pool", lambda e: e.memset(S_bf[:, :, :], 0.0), [], [S_bf])
            P.op("pool", lambda e: e.memset(Sr_f[:, :, :], 0.0), [], [Sr_f])
            P.op("pool", lambda e: e.memset(Sr_bf[:, :, :], 0.0), [], [Sr_bf])
            for (s, is_ctx, first, last, lp0) in chunk_list(cfg, d):
                w_ = win[ci % NB]
                q_ = rqk[ci % NB]
                co_, si_ = cosT[ci % NB], sinT[ci % NB]
                rd_ = rvdt[ci % NB]
                ys = y_sb[ci % 2]
                ci += 1
                lo = 0 if not first else 2
                hi = 132 if not last else 130
                if first:
                    P.op("pool", lambda e, w_=w_: e.memset(w_[:, :, 0:2], 0.0), [], [w_])
                if last:
                    P.op("pool", lambda e, w_=w_: e.memset(w_[:, :, 130:132], 0.0), [], [w_])
                P.dma("sp", w_[:, :, lo:hi], fm.t[0:768, s - 2 + lo:s - 2 + hi].rearrange("(ft p) t -> p ft t", p=128), w_, fm)
                P.dma("sp", q_[:, :, :], fm.t[768:1280, s:s + 128].rearrange("(ft p) t -> p ft t", p=128), q_, fm)
                if not is_ctx:
                    P.dma("sp", co_[:, :], T["rot_cos"].t[:, lp0:lp0 + 128], co_, T["rot_cos"])
                    P.dma("sp", si_[:, :], T["rot_sin"].t[:, lp0:lp0 + 128], si_, T["rot_sin"])
                P.dma("sp", rd_[:, 0:512], tm.t[s:s + 128, 512:1024], rd_, tm)
                P.dma("sp", rd_[:, 512:528], tm.t[s:s + 128, 1536:1552], rd_, tm)
                for ft in range(6):
                    a_ = acc[ft % 2]
                    P.op("dve", lambda e, a_=a_, ft=ft, w_=w_: e.tensor_scalar(a_[:, :], w_[:, ft, 0:128], cw[:, ft, 0:1], None, ALU.mult),
                         [w_, cw], [a_])
                    for k in range(1, 5):
                        P.op("dve", lambda e, a_=a_, ft=ft, w_=w_, k=k: e.scalar_tensor_tensor(
                            a_[:, :], w_[:, ft, k:k + 128], cw[:, ft, k:k + 1], a_[:, :], ALU.mult, ALU.add), [w_, cw, a_], [a_])
                    P.op("act", lambda e, a_=a_, ft=ft: e.activation(cvT[:, ft, :], a_[:, :], AF.Silu, bias=cb[:, ft:ft + 1]), [a_, cb], [cvT])
                for q in range(4):
                    P.op("pe", lambda e, q=q: e.transpose(tp4[:, q, :], cvT[:, q, :], C.ident[:, :]), [cvT, C.ident], [tp4])
                P.op("pe", lambda e: e.transpose(tp1[:, 0, :], cvT[:, 4, :], C.ident[:, :]), [cvT, C.ident], [tp1])
                act_copy(P, xs_tok[:, :], tp4[:, :, :], [tp4], [xs_tok])
                P.op("dve", lambda e: e.tensor_copy(bm_tok[:, :], tp1[:, 0, :]), [tp1], [bm_tok])
                if d == 0:
                    P.dma("pool", T["xs_keep"].t[s:s + 128, :], xs_tok[:, :], T["xs_keep"], xs_tok)
                for t in range(4):
                    if is_ctx:
                        P.op("pool", lambda e, t=t, q_=q_: e.tensor_copy(qkT[:, t, :], q_[:, t, :]), [q_], [qkT])
                    else:
                        sp_ = sw_ps[t % 2]
                        a1, a2 = t1[t % 2], t2[t % 2]
                        P.op("pe", lambda e, sp_=sp_, t=t, q_=q_: e.matmul(sp_[:, :], cslice(C, "pm"), q_[:, t, :], start=True, stop=True),
                             [cons, q_], [sp_])
                        P.op("pool", lambda e, a1=a1, t=t, q_=q_, co_=co_: e.tensor_tensor(a1[:, :], q_[:, t, :], co_[:, :], ALU.mult), [q_, co_], [a1])
                        P.op("dve", lambda e, a2=a2, sp_=sp_, si_=si_: e.tensor_tensor(a2[:, :], sp_[:, :], si_[:, :], ALU.mult), [sp_, si_], [a2])
                        P.op("pool", lambda e, a1=a1, a2=a2, t=t: e.tensor_tensor(qkT[:, t, :], a1[:, :], a2[:, :], ALU.add), [a1, a2], [qkT])
                for t in range(2):
                    P.op("pe", lambda e, t=t: e.transpose(tp1[:, t, :], qkT[:, 2 + t, :], C.ident[:, :]), [qkT, C.ident], [tp1])
                act_copy(P, rk_tok[:, :], tp1[:, :, :], [tp1], [rk_tok])
                P.op("pool", lambda e, rd_=rd_: e.tensor_copy(rv_bf[:, :], rd_[:, 0:512]), [rd_], [rv_bf])
                P.op("dve", lambda e, rd_=rd_: e.tensor_tensor(x16[:, :], rd_[:, 512:528], vec[:, 0:16], ALU.add), [rd_, vec], [x16])
                P.op("act", lambda e: e.activation(x16[:, :], x16[:, :], AF.Exp), [x16], [x16])
                P.op("act", lambda e: e.activation(dt16[:, :], x16[:, :], AF.Ln, bias=1.0), [x16], [dt16])
                P.op("dve", lambda e: e.tensor_tensor(la16[:, :], dt16[:, :], negA[:, :], ALU.mult), [dt16, negA], [la16])
                lad = la16[:, d * 8:(d + 1) * 8]
                P.op("pe", lambda e, lad=lad: e.matmul(csT_ps[0:8, :], lad, Md, start=True, stop=True), [la16, cons], [csT_ps])
                P.op("pe", lambda e, lad=lad: e.matmul(small_a[:, 0:8], Md, lad, start=True, stop=True), [la16, cons], [small_a])
                P.op("pe", lambda e, lad=lad: e.matmul(small_b[:, 0:8], cslice(C, "ones"), lad, start=True, stop=True), [la16, cons], [small_b])
                act_copy(P, csT[:, :], csT_ps[0:8, :], [csT_ps], [csT])
                P.op("dve", lambda e: e.tensor_copy(cscol[:, :], small_a[:, 0:8]), [small_a], [cscol])
                P.op("dve", lambda e: e.tensor_copy(clb[:, :], small_b[:, 0:8]), [small_b], [clb])
                P.op("act", lambda e: e.activation(ecs[:, :], cscol[:, :], AF.Exp), [cscol], [ecs])
                P.op("act", lambda e: e.activation(decb[:, :], clb[:, :], AF.Exp), [clb], [decb])
                P.op("dve", lambda e: e.tensor_tensor(wgt[:, :], clb[:, :], cscol[:, :], ALU.subtract), [clb, cscol], [wgt])
                P.op("act", lambda e: e.activation(wgt[:, :], wgt[:, :], AF.Exp), [wgt], [wgt])
                P.op("dve", lambda e: e.tensor_tensor(wgt[:, :], wgt[:, :], dt16[:, d * 8:(d + 1) * 8], ALU.mult), [wgt, dt16], [wgt])
                P.op("pe", lambda e: e.matmul(GT_ps[:, :], cvT[:, 4, :], cvT[:, 5, :], start=True, stop=True), [cvT], [GT_ps])
                act_copy(P, GT[:, :], GT_ps[:, :], [GT_ps], [GT])
                for h in range(8):
                    cp, te, ee_, at = csb_ps[h % 2], tmpE[h % 2], EE[h % 2], AT[h % 2]
                    yp, yip, sp2, yis, xw_ = y_ps[h % 2], yi_ps[h % 2], S_ps[h % 2], yi_sb[h % 2], xw[h % 2]
                    P.op("pe", lambda e, cp=cp, h=h: e.matmul(cp[:, :], cslice(C, "sel8", rows=8, lo=h * 128, n=128), csT[:, :], start=True, stop=True),
                         [cons, csT], [cp])
                    P.op("dve", lambda e, cp=cp, te=te, h=h: e.scalar_tensor_tensor(te[:, :], cp[:, :], cscol[:, h:h + 1], negm, ALU.subtract, ALU.add),
                         [cp, cscol, cons], [te])
                    P.op("act", lambda e, te=te, ee_=ee_: e.activation(ee_[:, :], te[:, :], AF.Exp), [te], [ee_])
                    P.op("dve", lambda e, at=at, ee_=ee_, h=h: e.scalar_tensor_tensor(at[:, :], GT[:, :], dt16[:, d * 8 + h:d * 8 + h + 1], ee_[:, :], ALU.mult, ALU.mult),
                         [GT, dt16, ee_], [at])
                    P.op("pe", lambda e, yp=yp, at=at, h=h: e.matmul(yp[:, 0:64], at[:, :], xs_tok[:, h * 64:(h + 1) * 64], start=True, stop=True),
                         [at, xs_tok], [yp])
                    P.op("pe", lambda e, yip=yip, h=h: e.matmul(yip[:, 0:64], cvT[:, 5, :], S_bf[:, h, :], start=True, stop=True), [cvT, S_bf], [yip])
                    act_copy(P, yis[:, 0:64], yip[:, 0:64], [yip, ecs], [yis], scale=ecs[:, h:h + 1])
                    P.op("dve", lambda e, ys=ys, yp=yp, yis=yis, h=h: e.tensor_tensor(ys[:, h * 64:(h + 1) * 64], yp[:, 0:64], yis[:, 0:64], ALU.add),
                         [yp, yis], [ys])
                    P.op("pool", lambda e, xw_=xw_, h=h: e.tensor_scalar(xw_[:, 0:64], xs_tok[:, h * 64:(h + 1) * 64], wgt[:, h:h + 1], None, ALU.mult),
                         [xs_tok, wgt], [xw_])
                    P.op("pe", lambda e, sp2=sp2, xw_=xw_: e.matmul(sp2[:, 0:64], bm_tok[:, :], xw_[:, 0:64], start=True, stop=True), [bm_tok, xw_], [sp2])
                    P.op("dve", lambda e, sp2=sp2, h=h: e.scalar_tensor_tensor(S_f[:, h, :], S_f[:, h, :], decb[:, h:h + 1], sp2[:, 0:64], ALU.mult, ALU.add),
                         [S_f, decb, sp2], [S_f])
                    act_copy(P, S_bf[:, h, :], S_f[:, h, :], [S_f], [S_bf])
                for r in range(4):
                    t, po = r // 2, (r % 2) * 64
                    k = d * 4 + r
                    gp, at = GTr_ps[r % 2], AT[r % 2]
                    yp, yip, sp2, yis, xw_ = y_ps[r % 2], yi_ps[r % 2], S_ps[r % 2], yi_sb[r % 2], xw[r % 2]
                    P.op("pe", lambda e, gp=gp, t=t, po=po: e.matmul(gp[:, :], qkT[po:po + 64, 2 + t, :], qkT[po:po + 64, t, :], start=True, stop=True),
                         [qkT], [gp])
                    P.op("dve", lambda e, at=at, gp=gp, k=k: e.tensor_tensor(at[:, :], gp[:, :], decc[:, k, :], ALU.mult), [gp, decc], [at])
                    P.op("pe", lambda e, yp=yp, at=at, r=r: e.matmul(yp[:, :], at[:, :], rv_bf[:, r * 128:(r + 1) * 128], start=True, stop=True),
                         [at, rv_bf], [yp])
                    P.op("pe", lambda e, yip=yip, t=t, po=po: e.matmul(yip[:, :], qkT[po:po + 64, t, :], Sr_bf[po:po + 64, t, :], start=True, stop=True),
                         [qkT, Sr_bf], [yip])
                    act_copy(P, yis[:, :], yip[:, :], [yip, gi], [yis], scale=gi[:, k:k + 1])
                    P.op("dve", lambda e, ys=ys, yp=yp, yis=yis, r=r: e.tensor_tensor(ys[:, 512 + r * 128:512 + (r + 1) * 128], yp[:, :], yis[:, :], ALU.add),
                         [yp, yis], [ys])
                    P.op("pool", lambda e, xw_=xw_, r=r, k=k: e.tensor_scalar(xw_[:, :], rv_bf[:, r * 128:(r + 1) * 128], ge[:, k:k + 1], None, ALU.mult),
                         [rv_bf, ge], [xw_])
                    P.op("pe", lambda e, sp2=sp2, xw_=xw_, t=t: e.matmul(sp2[:, :], rk_tok[:, t * 128:(t + 1) * 128], xw_[:, :], start=True, stop=True),
                         [rk_tok, xw_], [sp2])
                    P.op("dve", lambda e, sp2=sp2, t=t, po=po, k=k: e.scalar_tensor_tensor(
                        Sr_f[po:po + 64, t, :], Sr_f[po:po + 64, t, :], g128[po:po + 64, k:k + 1], sp2[po:po + 64, :], ALU.mult, ALU.add),
                        [Sr_f, g128, sp2], [Sr_f])
                    act_copy(P, Sr_bf[po:po + 64, t, :], Sr_f[po:po + 64, t, :], [Sr_f], [Sr_bf])
                P.dma("pool", T["yd"].t[d, s:s + 128, :], ys[:, :], T["yd"], ys)
            cic[0] = ci

        run_dir(0)
        run_dir(1)


def store_yT(C, yt, s):
    P, cfg, T = C.P, C.cfg, C.T
    if s < 256:
        for half in range(2):
            r = (s // 64) + half
            P.dma("pool", T["yT_loc"].t[r * 1024:(r + 1) * 1024, 0:64].rearrange("(kc p) t -> p kc t", p=128), yt[:, :, half * 64:(half + 1) * 64],
                  T["yT_loc"], yt, scratch=True)
    else:
        lp = s - 256
        r, row0 = lp // cfg.NTL, 64 + lp % cfg.NTL
        P.dma("pool", T["yT_loc"].t[r * 1024:(r + 1) * 1024, row0:row0 + 128].rearrange("(kc p) t -> p kc t", p=128), yt[:, :, :], T["yT_loc"], yt, scratch=True)


def phase_even_epilogue(C):
    P, cfg, T = C.P, C.cfg, C.T
    with Scope(P):
        dskb = P.sbuf("dskb", [128, 512], F32)
        ssdn = P.sbuf("ssdn", [128, 512], F32)
        P.dma("sp", dskb[:, :], T["ev_vec"].t[64:576].partition_broadcast(128), dskb, T["ev_vec"])
        P.dma("sp", ssdn[:, :], T["ev_vec"].t[576:1088].partition_broadcast(128), ssdn, T["ev_vec"])
        NB = 2
        yf = [P.sbuf(f"yf{i}", [128, 1024], F32) for i in range(NB)]
        yr = [P.sbuf(f"yr{i}", [128, 1024], F32) for i in range(NB)]
        xk = [P.sbuf(f"xk{i}", [128, 512], BF16) for i in range(NB)]
        zg = [P.sbuf(f"zg{i}", [128, 1536], F32) for i in range(NB)]
        ysum = P.sbuf("ysum", [128, 1024], F32)
        t512 = P.sbuf("t512", [128, 512], F32)
        junk = P.sbuf("ejunk", [128, 512], F32)
        sz = P.sbuf("sz", [128, 1024], F32)
        ymix = P.sbuf("ymix", [128, 1024], BF16)
        ss = P.sbuf("ess", [128, 8], F32)
        rstd = P.sbuf("erstd", [128, 8], F32)
        mean = P.sbuf("emean", [128, 8], F32)
        yT = [P.sbuf(f"eyT{i}", [128, 8, 128], BF16) for i in range(2)]
        pst = [P.psum(f"etp{i}", [128, 4, 128], BF16) for i in range(2)]
        nch = cfg.TA // 128
        for ci in range(nch):
            s = ci * 128
            a, b, x, z = yf[ci % NB], yr[ci % NB], xk[ci % NB], zg[ci % NB]
            P.dma("sp", a[:, :], T["yd"].t[0, s:s + 128, :], a, T["yd"])
            P.dma("sp", b[:, :], T["yd"].t[1, s:s + 128, :], b, T["yd"])
            P.dma("sp", x[:, :], T["xs_keep"].t[s:s + 128, :], x, T["xs_keep"])
            P.dma("sp", z[:, 0:512], T["tm_pre"].t[s:s + 128, 0:512], z, T["tm_pre"])
            P.dma("sp", z[:, 512:1024], T["tm_pre"].t[s:s + 128, 1024:1536], z, T["tm_pre"])
            P.op("dve", lambda e, a=a, b=b: e.tensor_tensor(ysum[:, :], a[:, :], b[:, :], ALU.add), [a, b], [ysum])
            P.op("pool", lambda e, x=x: e.tensor_tensor(t512[:, :], x[:, :], dskb[:, :], ALU.mult), [x, dskb], [t512])
            P.op("dve", lambda e: e.tensor_tensor(ysum[:, 0:512], ysum[:, 0:512], t512[:, :], ALU.add), [ysum, t512], [ysum])
            P.op("act", lambda e, z=z: e.activation(sz[:, :], z[:, 0:1024], AF.Silu), [z], [sz])
            P.op("dve", lambda e: e.tensor_tensor(ysum[:, 0:512], ysum[:, 0:512], sz[:, 0:512], ALU.mult), [ysum, sz], [ysum])
            P.op("act", lambda e: e.activation(junk[:, :], ysum[:, 0:512], AF.Square, accum_out=ss[:, 0:1]), [ysum], [junk, ss])
            P.op("dve", lambda e: e.tensor_scalar(rstd[:, 0:1], ss[:, 0:1], 1.0 / 512.0, EPS, ALU.mult, ALU.add), [ss], [rstd])
            P.op("act", lambda e: e.activation(rstd[:, 0:1], rstd[:, 0:1], AF.Sqrt), [rstd], [rstd])
            P.op("dve", lambda e: e.reciprocal(rstd[:, 0:1], rstd[:, 0:1]), [rstd], [rstd])
            P.op("dve", lambda e: e.scalar_tensor_tensor(ymix[:, 0:512], ysum[:, 0:512], rstd[:, 0:1], ssdn[:, :], ALU.mult, ALU.mult),
                 [ysum, rstd, ssdn], [ymix])
            for r in range(4):
                sl = slice(512 + r * 128, 512 + (r + 1) * 128)
                P.op("act", lambda e, sl=sl, r=r: e.activation(junk[:, 0:128], ysum[:, sl], AF.Copy, accum_out=mean[:, r:r + 1]), [ysum], [junk, mean])
                P.op("dve", lambda e, r=r: e.tensor_scalar(mean[:, r:r + 1], mean[:, r:r + 1], 1.0 / 128.0, None, ALU.mult), [mean], [mean])
                P.op("dve", lambda e, sl=sl, r=r: e.tensor_scalar(ysum[:, sl], ysum[:, sl], mean[:, r:r + 1], None, ALU.subtract), [ysum, mean], [ysum])
                P.op("act", lambda e, sl=sl, r=r: e.activation(junk[:, 0:128], ysum[:, sl], AF.Square, accum_out=ss[:, 1 + r:2 + r]), [ysum], [junk, ss])
                P.op("dve", lambda e, r=r: e.tensor_scalar(rstd[:, 1 + r:2 + r], ss[:, 1 + r:2 + r], 1.0 / 128.0, EPS, ALU.mult, ALU.add), [ss], [rstd])
                P.op("act", lambda e, r=r: e.activation(rstd[:, 1 + r:2 + r], rstd[:, 1 + r:2 + r], AF.Sqrt), [rstd], [rstd])
                P.op("dve", lambda e, r=r: e.reciprocal(rstd[:, 1 + r:2 + r], rstd[:, 1 + r:2 + r]), [rstd], [rstd])
                P.op("dve", lambda e, sl=sl, r=r: e.scalar_tensor_tensor(ymix[:, sl], ysum[:, sl], rstd[:, 1 + r:2 + r], sz[:, sl], ALU.mult, ALU.mult),
                     [ysum, rstd, sz], [ymix])
            yt = yT[ci % 2]
            transpose_rows(C, ymix, 128, 1024, yt, pst)
            store_yT(C, yt, s)


MAXCC = 1 << 20


def pick_rn(rows, cols, esize=2):
    best = 1
    for rn in range(1, rows + 1):
        if rows % rn == 0 and rn * cols * esize <= MAXCC:
            best = rn
    return best


def ag_chunks(C, groups, nr, dst, src, rn):
    P = C.P
    R = src.t.shape[0]
    for c in range(R // rn):
        P.coll("AllGather", groups, dst.t[c * nr * rn:(c + 1) * nr * rn, :].opt(), src.t[c * rn:(c + 1) * rn, :].opt(), dst, src)


def prep_weights(C, l):
    P, cfg, T = C.P, C.cfg, C.T
    for nm, src in (("wo", "w_out"), ("w1", "ffn_w1"), ("w3", "ffn_w3"), ("w2", "ffn_w2")):
        sh, fu = T[f"{nm}_s{l}"], T[f"{nm}_f{l}"]
        rows, cols = sh.t.shape
        for r0 in range(0, rows, 256):
            rn = min(256, rows - r0)
            P.dma("pool", sh.t[r0:r0 + rn, :], T[src].t[l, r0:r0 + rn, :], sh, T[src], scratch=True)
        ag_chunks(C, BGROUPS, 4, fu, sh, pick_rn(rows, cols))


def phase_outproj(C, l, do_ctx):
    P, cfg, T = C.P, C.cfg, C.T
    D = cfg.D
    rk = C.rk
    with Scope(P):
        G = GemmRes(C, 8)
        xts = [P.sbuf(f"opx{i}", [128, 32, 512], BF16) for i in range(2)]
        stg = [P.sbuf(f"opst{i}", [128, 512], F32) for i in range(4)]
        si = [0]
        for bi, (r0, n) in enumerate(cfg.token_tiles()):
            if r0 == 0 and not do_ctx:
                continue
            xT = xts[bi % 2]
            for gq in range(4):
                yv = T["yallT"].t.ap().rearrange("(d kk g p) t -> d kk g p t", d=4, kk=8, g=4)
                P.dma("sp", xT[:, gq * 8:(gq + 1) * 8, :n], yv[rk, :, gq, :, r0:r0 + n].rearrange("kk p t -> p kk t"), xT, T["yallT"])

            def evac(a0, an, c0, nw, pt, r0=r0):
                st = stg[si[0] % 4]
                if si[0] % 2 == 0:
                    act_copy(P, st[:an, :nw], pt[:an, :nw], [pt], [st])
                else:
                    P.op("dve", lambda e: e.tensor_copy(st[:an, :nw], pt[:an, :nw]), [pt], [st])
                si[0] += 1
                P.dma("pool", T["o_loc"].t[r0 + a0:r0 + a0 + an, c0:c0 + nw], st[:an, :nw], T["o_loc"], st)

            gemm(C, G, "TM", xT, n, T[f"wo_f{l}"], 4096, 0, D, evac)


def row_tiles(cfg, do_ctx):
    t = [(0, 64, 1)] if do_ctx else []
    return t + [(r, 128, 0) for r in range(64, cfg.NT, 128)]


def phase_postmix(C, l, xin, do_ctx):
    P, cfg, T = C.P, C.cfg, C.T
    D, KC = cfg.D, cfg.KC
    with Scope(P):
        gm = P.sbuf("gm", [128, D], F32)
        g2 = P.sbuf("g2", [128, D], F32)
        sh2 = P.sbuf("sh2", [128, D], F32)
        ot = [P.sbuf(f"ot{i}", [128, D], F32) for i in range(2)]
        xt = [P.sbuf(f"pxt{i}", [128, D], F32) for i in range(2)]
        junk = P.sbuf("pjunk", [128, D], BF16)
        hb = [P.sbuf(f"phb{i}", [128, D], BF16) for i in range(2)]
        hT = [P.sbuf(f"phT{i}", [128, KC, 128], BF16) for i in range(2)]
        ss = P.sbuf("pss", [128, 1], F32)
        rstd = P.sbuf("prstd", [128, 1], F32)
        pst = [P.psum(f"ptp{i}", [128, 4, 128], BF16) for i in range(2)]
        cur = None
        for i, (r0, n, var) in enumerate(row_tiles(cfg, do_ctx)):
            if var != cur:
                for (tile_, vi) in ((gm, 2), (g2, 3), (sh2, 4)):
                    P.dma("sp", tile_[:, :], T["vecs"].t[l, vi, var, :].partition_broadcast(128), tile_, T["vecs"])
                cur = var
            o, x, h, ht = ot[i % 2], xt[i % 2], hb[i % 2], hT[i % 2]
            P.dma("sp", o[:n, :], T["o_loc"].t[r0:r0 + n, :], o, T["o_loc"])
            P.dma("sp", x[:n, :], xin.t[r0:r0 + n, :], x, xin)
            P.op("act", lambda e, o=o, n=n: e.activation(junk[:n, :], o[:n, :], AF.Square, accum_out=ss[:n, :]), [o], [junk, ss])
            rstd_from_ss(P, rstd, ss, n, 1.0 / D)
            P.op("dve", lambda e, o=o, n=n: e.scalar_tensor_tensor(o[:n, :], o[:n, :], rstd[:n, 0:1], gm[:n, :], ALU.mult, ALU.mult), [o, rstd, gm], [o])
            P.op("pool", lambda e, o=o, x=x, n=n: e.tensor_tensor(x[:n, :], x[:n, :], o[:n, :], ALU.add), [x, o], [x])
            P.dma("pool", T["s_loc"].t[r0:r0 + n, :], x[:n, :], T["s_loc"], x, scratch=True)
            P.op("act", lambda e, x=x, n=n: e.activation(junk[:n, :], x[:n, :], AF.Square, accum_out=ss[:n, :]), [x], [junk, ss])
            rstd_from_ss(P, rstd, ss, n, 1.0 / D)
            P.op("dve", lambda e, o=o, x=x, n=n: e.scalar_tensor_tensor(o[:n, :], x[:n, :], rstd[:n, 0:1], g2[:n, :], ALU.mult, ALU.mult), [x, rstd, g2], [o])
            P.op("pool", lambda e, o=o, h=h, n=n: e.tensor_tensor(h[:n, :], o[:n, :], sh2[:n, :], ALU.add), [o, sh2], [h])
            transpose_rows(C, h, n, D, ht, pst)
            for k8 in range(0, KC, 8):
                ke = min(KC, k8 + 8)
                P.dma("pool", T["h2T_loc"].t[k8 * 128:ke * 128, r0:r0 + n].rearrange("(kc p) t -> p kc t", p=128), ht[:, k8:ke, :n],
                      T["h2T_loc"], ht, scratch=True)


def phase_ffn(C, l, do_ctx):
    P, cfg, T = C.P, C.cfg, C.T
    D, KC, DFF = cfg.D, cfg.KC, cfg.DFF
    FC = DFF // 128
    with Scope(P):
        G = GemmRes(C, 8)
        xts = [P.sbuf(f"fx{i}", [128, KC, 512], BF16) for i in range(1)]
        aT = P.sbuf("aT", [128, FC, 512], BF16)
        su = [P.sbuf(f"su{i}", [128, 512], F32) for i in range(4)]
        stg = [P.sbuf(f"fst{i}", [128, 512], F32) for i in range(4)]
        si = [0]
        for bi, (r0, n) in enumerate(cfg.token_tiles()):
            if r0 == 0 and not do_ctx:
                continue
            xT = xts[0]
            for k8 in range(0, KC, 8):
                ke = min(KC, k8 + 8)
                P.dma("sp", xT[:, k8:ke, :n], T["h2T_loc"].t[k8 * 128:ke * 128, r0:r0 + n].rearrange("(kc p) t -> p kc t", p=128), xT, T["h2T_loc"])
            for c0 in range(0, DFF, 512):
                nw = min(512, DFF - c0)

                def ev1(a0, pt, nf, c0=c0, n=n):
                    q = (a0 - c0) // 128
                    P.op("act", lambda e: e.activation(su[q][:nf, :n], pt[:nf, :n], AF.Silu), [pt], [su[q]])

                def ev3(a0, pt, nf, c0=c0, n=n):
                    q = (a0 - c0) // 128
                    P.op("dve", lambda e: e.tensor_tensor(aT[:nf, a0 // 128, :n], su[q][:nf, :n], pt[:nf, :n], ALU.mult), [su[q], pt], [aT])

                gemm(C, G, "FM", xT, n, T[f"w1_f{l}"], D, c0, nw, ev1)
                gemm(C, G, "FM", xT, n, T[f"w3_f{l}"], D, c0, nw, ev3)

            def evac(a0, an, c0, nw, pt, r0=r0):
                st = stg[si[0] % 4]
                if si[0] % 2 == 0:
                    act_copy(P, st[:an, :nw], pt[:an, :nw], [pt], [st])
                else:
                    P.op("dve", lambda e: e.tensor_copy(st[:an, :nw], pt[:an, :nw]), [pt], [st])
                si[0] += 1
                P.dma("pool", T["f_loc"].t[r0 + a0:r0 + a0 + an, c0:c0 + nw], st[:an, :nw], T["f_loc"], st)

            gemm(C, G, "TM", aT, n, T[f"w2_f{l}"], DFF, 0, D, evac)


def phase_postffn(C, l, dst, do_ctx, dst_row_off=0):
    P, cfg, T = C.P, C.cfg, C.T
    D = cfg.D
    with Scope(P):
        gf = P.sbuf("gf", [128, D], F32)
        ft = [P.sbuf(f"fft{i}", [128, D], F32) for i in range(2)]
        st_ = [P.sbuf(f"fst_{i}", [128, D], F32) for i in range(2)]
        junk = P.sbuf("fjunk", [128, D], BF16)
        ss = P.sbuf("fss", [128, 1], F32)
        rstd = P.sbuf("frstd", [128, 1], F32)
        cur = None
        for i, (r0, n, var) in enumerate(row_tiles(cfg, do_ctx)):
            if var != cur:
                P.dma("sp", gf[:, :], T["vecs"].t[l, 5, var, :].partition_broadcast(128), gf, T["vecs"])
                cur = var
            f, s_ = ft[i % 2], st_[i % 2]
            P.dma("sp", f[:n, :], T["f_loc"].t[r0:r0 + n, :], f, T["f_loc"])
            P.dma("sp", s_[:n, :], T["s_loc"].t[r0:r0 + n, :], s_, T["s_loc"])
            P.op("act", lambda e, f=f, n=n: e.activation(junk[:n, :], f[:n, :], AF.Square, accum_out=ss[:n, :]), [f], [junk, ss])
            rstd_from_ss(P, rstd, ss, n, 1.0 / D)
            P.op("dve", lambda e, f=f, n=n: e.scalar_tensor_tensor(f[:n, :], f[:n, :], rstd[:n, 0:1], gf[:n, :], ALU.mult, ALU.mult), [f, rstd, gf], [f])
            P.op("pool", lambda e, f=f, s_=s_, n=n: e.tensor_tensor(s_[:n, :], s_[:n, :], f[:n, :], ALU.add), [s_, f], [s_])
            P.dma("pool", dst.t[r0 - dst_row_off:r0 - dst_row_off + n, :], s_[:n, :], dst, s_, scratch=True)


def phase_mixer_odd(C):
    P, cfg, T = C.P, C.cfg, C.T
    fm, tm = T["fm_pre"], T["tm_pre"]
    import os
    CUT = float(os.environ.get("ODDCUT", "9"))
    with Scope(P):
        cons = P.sbuf("cons", [128, C.ncons], F32)
        cslice = make_cslice(cons)
        P.dma("sp", cons[:, :], T["consts"].t.ap(), cons, T["consts"])
        cw = P.sbuf("ocw", [128, 16, 5], F32)
        P.dma("sp", cw[:, :, :], T["od_cw"].t.ap(), cw, T["od_cw"])
        vec = P.sbuf("odvec", [128, 32], F32)
        P.dma("sp", vec[:, :], T["od_vec"].t[0:32].partition_broadcast(128), vec, T["od_vec"])
        negA = P.sbuf("onegA", [128, 16], F32)
        P.op("act", lambda e: e.activation(negA[:, :], vec[:, 16:32], AF.Exp), [vec], [negA])
        P.op("dve", lambda e: e.tensor_scalar(negA[:, :], negA[:, :], -1.0, None, ALU.mult), [negA], [negA])
        identb = P.sbuf("identb", [128, 128], BF16)
        P.op("pool", lambda e: e.tensor_copy(identb[:, :], cslice(C, "ident")), [cons], [identb])

        NB = 2
        win = [P.sbuf(f"owin{i}", [128, 16, 132], F32) for i in range(NB)]
        ba = [P.sbuf(f"oba{i}", [128, 32], F32) for i in range(NB)]
        acc = [P.sbuf(f"oacc{i}", [128, 128], F32) for i in range(2)]
        cv = P.sbuf("ocv", [128, 8, 128], F32)
        vT = P.sbuf("ovT", [128, 8, 128], BF16)
        sq = [P.sbuf(f"osq{i}", [128, 128], F32) for i in range(2)]
        rn = [P.sbuf(f"orn{i}", [128, 128], F32) for i in range(2)]
        qkT = P.sbuf("oqkT", [128, 8, 128], BF16)
        k_tok = P.sbuf("ok_tok", [128, 4, 128], BF16)
        v_tok = P.sbuf("ov_tok", [128, 8, 128], BF16)
        x16 = P.sbuf("ox16", [128, 32], F32)
        g16 = P.sbuf("og16", [128, 16], F32)
        be16 = P.sbuf("obe16", [128, 16], F32)
        lnb16 = P.sbuf("olnb16", [128, 16], F32)
        csT = P.sbuf("ocsT", [8, 128], F32)
        lbT = P.sbuf("olbT", [8, 128], F32)
        cscol = P.sbuf("ocscol", [128, 8], F32)
        clb = P.sbuf("oclb", [128, 8], F32)
        ecs = P.sbuf("oecs", [128, 8], F32)
        decb = P.sbuf("odecb", [128, 8], F32)
        wend = P.sbuf("owend", [128, 8], F32)
        wkb = P.sbuf("owkb", [128, 8], F32)
        tmp = [P.sbuf(f"otmp{i}", [128, 128], F32) for i in range(3)]
        Ei = [P.sbuf(f"oEi{i}", [128, 128], F32) for i in range(3)]
        attT = P.sbuf("oattT", [128, 128], BF16)
        Mk = [P.sbuf(f"oM{i}", [128, 128], F32) for i in range(2)]
        Nk = [P.sbuf(f"oN{i}", [128, 128], F32) for i in range(2)]
        Uk = [P.sbuf(f"oU{i}", [128, 128], F32) for i in range(2)]
        kn32 = P.sbuf("okn32", [128, 4, 128], F32)
        u_sb = P.sbuf("ou_sb", [128, 128], F32)
        Esb = P.sbuf("oEsb", [128, 128], F32)
        Fsb = P.sbuf("oFsb", [128, 128], F32)
        Gsb = P.sbuf("oGsb", [128, 128], F32)
        F2sb = P.sbuf("oF2sb", [128, 128], F32)
        W1 = P.sbuf("oW1", [128, 128], F32)
        Tbd = P.sbuf("oTbd", [128, 128], F32)
        wT = P.sbuf("owT", [128, 128], BF16)
        ecsb = P.sbuf("oecsb", [128, 128], F32)
        qeT = P.sbuf("oqeT", [128, 128], BF16)
        kb = P.sbuf("okb", [128, 128], F32)
        kend = P.sbuf("okend", [128, 128], BF16)
        vb = P.sbuf("ovb", [128, 128], F32)
        vnew = P.sbuf("ovnew", [128, 128], BF16)
        y_sb = [P.sbuf(f"oy_sb{i}", [128, 1024], F32) for i in range(2)]
        S_f = P.sbuf("oS_f", [128, 8, 128], F32)
        S_bf = P.sbuf("oS_bf", [128, 8, 128], BF16)
        tpb = P.psum_views("otpb", 4, 128, BF16)
        bkA = P.psum_views("obkA", 4)
        bkB = P.psum_views("obkB", 4)
        bkC = P.psum_views("obkC", 4)
        bkD = P.psum_views("obkD", 4)
        bkE = P.psum_views("obkE", 4)
        ssq_ps = bkA[0:2]
        QK_ps, KK_ps = bkA[2], bkA[3]
        small_a, small_b, csb_ps = bkB[1], bkB[2], bkB[3]
        csT_ps = bkE[2]
        lbb_ps, M_ps, N_ps, U_ps = bkC[0], bkC[1], bkC[2], bkC[3]
        u_ps, wT_ps, vn_ps, o_ps = bkD[0], bkD[1], bkD[2], bkD[3]
        S_ps, lbT_ps = bkE[0], bkE[1]

        cic = [0]

        def run_dir(d):
            ci = cic[0]
            Md = cslice(C, "mf" if d == 0 else "mr")
            n_incl = cslice(C, "nf" if d == 0 else "nr")
            n_N0 = cslice(C, "nfs" if d == 0 else "nrs")
            n_M0 = cslice(C, "nrs" if d == 0 else "nfs")
            P.op("pool", lambda e: e.memset(S_f[:, :, :], 0.0), [], [S_f])
            P.op("pool", lambda e: e.memset(S_bf[:, :, :], 0.0), [], [S_bf])
            for (s, is_ctx, first, last, lp0) in chunk_list(cfg, d):
                w_ = win[ci % NB]
                ba_ = ba[ci % NB]
                ys = y_sb[ci % 2]
                ci += 1
                lo = 0 if not first else 2
                hi = 132 if not last else 130
                if first:
                    P.op("pool", lambda e, w_=w_: e.memset(w_[:, :, 0:2], 0.0), [], [w_])
                if last:
                    P.op("pool", lambda e, w_=w_: e.memset(w_[:, :, 130:132], 0.0), [], [w_])
                for q4 in range(4):
                    P.dma("sp", w_[:, q4 * 4:(q4 + 1) * 4, lo:hi],
                          fm.t[q4 * 512:(q4 + 1) * 512, s - 2 + lo:s - 2 + hi].rearrange("(ft p) t -> p ft t", p=128), w_, fm)
                P.dma("sp", ba_[:, :], tm.t[s:s + 128, 1024:1056], ba_, tm)
                for ft in range(16):
                    a_ = acc[ft % 2]
                    P.op("dve", lambda e, a_=a_, ft=ft, w_=w_: e.tensor_scalar(a_[:, :], w_[:, ft, 0:128], cw[:, ft, 0:1], None, ALU.mult), [w_, cw], [a_])
                    for k in range(1, 5):
                        P.op("dve", lambda e, a_=a_, ft=ft, w_=w_, k=k: e.scalar_tensor_tensor(
                            a_[:, :], w_[:, ft, k:k + 128], cw[:, ft, k:k + 1], a_[:, :], ALU.mult, ALU.add), [w_, cw, a_], [a_])
                    if ft < 8:
                        P.op("act", lambda e, a_=a_, ft=ft: e.activation(cv[:, ft, :], a_[:, :], AF.Silu), [a_], [cv])
                    else:
                        P.op("act", lambda e, a_=a_, ft=ft: e.activation(vT[:, ft - 8, :], a_[:, :], AF.Silu), [a_], [vT])
                for t in range(8 if CUT >= 2 else 0):
                    sq_, rn_, sp_ = sq[t % 2], rn[t % 2], ssq_ps[t % 2]
                    P.op("act", lambda e, sq_=sq_, t=t: e.activation(sq_[:, :], cv[:, t, :], AF.Square), [cv], [sq_])
                    P.op("pe", lambda e, sp_=sp_, sq_=sq_: e.matmul(sp_[:, :], cslice(C, "ones"), sq_[:, :], start=True, stop=True), [cons, sq_], [sp_])
                    P.op("dve", lambda e, rn_=rn_, sp_=sp_: e.tensor_scalar(rn_[:, :], sp_[:, :], EPS, None, ALU.add), [sp_], [rn_])
                    P.op("act", lambda e, rn_=rn_: e.activation(rn_[:, :], rn_[:, :], AF.Sqrt), [rn_], [rn_])
                    P.op("dve", lambda e, rn_=rn_: e.reciprocal(rn_[:, :], rn_[:, :]), [rn_], [rn_])
                    sc = (128.0 ** -0.5) if t < 4 else 1.0
                    P.op("dve", lambda e, rn_=rn_, t=t, sc=sc: e.scalar_tensor_tensor(qkT[:, t, :], cv[:, t, :], sc, rn_[:, :], ALU.mult, ALU.mult),
                         [cv, rn_], [qkT])
                    if t >= 4:
                        P.op("pool", lambda e, rn_=rn_, t=t: e.tensor_tensor(kn32[:, t - 4, :], cv[:, t, :], rn_[:, :], ALU.mult), [cv, rn_], [kn32])
                if CUT < 3:
                    continue
                for t in range(4):
                    P.op("pe", lambda e, t=t: e.transpose(tpb[t][:, :], qkT[:, 4 + t, :], identb[:, :]), [qkT, identb], [tpb[t]])
                    P.op("dve", lambda e, t=t: e.tensor_copy(k_tok[:, t, :], tpb[t][:, :]), [tpb[t]], [k_tok])
                for t in range(8):
                    P.op("pe", lambda e, t=t: e.transpose(tpb[t % 4][:, :], vT[:, t, :], identb[:, :]), [vT, identb], [tpb[t % 4]])
                    act_copy(P, v_tok[:, t, :], tpb[t % 4][:, :], [tpb[t % 4]], [v_tok])
                if CUT < 3.1:
                    continue
                P.op("dve", lambda e, ba_=ba_: e.tensor_tensor(x16[:, 16:32], ba_[:, 16:32], vec[:, 0:16], ALU.add), [ba_, vec], [x16])
                P.op("act", lambda e: e.activation(x16[:, 16:32], x16[:, 16:32], AF.Exp), [x16], [x16])
                P.op("act", lambda e: e.activation(g16[:, :], x16[:, 16:32], AF.Ln, bias=1.0), [x16], [g16])
                P.op("dve", lambda e: e.tensor_tensor(g16[:, :], g16[:, :], negA[:, :], ALU.mult), [g16, negA], [g16])
                P.op("act", lambda e, ba_=ba_: e.activation(x16[:, 0:16], ba_[:, 0:16], AF.Exp, scale=-1.0), [ba_], [x16])
                P.op("act", lambda e: e.activation(lnb16[:, :], x16[:, 0:16], AF.Ln, bias=1.0), [x16], [lnb16])
                P.op("dve", lambda e: e.tensor_scalar(lnb16[:, :], lnb16[:, :], -1.0, None, ALU.mult), [lnb16], [lnb16])
                P.op("act", lambda e: e.activation(be16[:, :], lnb16[:, :], AF.Exp), [lnb16], [be16])
                gd = g16[:, d * 8:(d + 1) * 8]
                lnbd = lnb16[:, d * 8:(d + 1) * 8]
                if CUT < 3.3:
                    continue
                P.op("pe", lambda e: e.matmul(csT_ps[0:8, :], gd, Md, start=True, stop=True), [g16, cons], [csT_ps])
                if CUT < 3.6:
                    continue
                P.op("pe", lambda e: e.matmul(lbT_ps[0:8, :], gd, Md, start=True, stop=False), [g16, cons], [lbT_ps])
                P.op("pe", lambda e: e.matmul(lbT_ps[0:8, :], lnbd, cslice(C, "ident"), start=False, stop=True), [lnb16, cons], [lbT_ps])
                if CUT < 3.8:
                    continue
                P.op("pe", lambda e: e.matmul(small_a[:, 0:8], Md, gd, start=True, stop=True), [g16, cons], [small_a])
                P.op("pe", lambda e: e.matmul(small_b[:, 0:8], cslice(C, "ones"), gd, start=True, stop=True), [g16, cons], [small_b])
                if CUT < 3.85:
                    continue
                act_copy(P, csT[:, :], csT_ps[0:8, :], [csT_ps], [csT])
                act_copy(P, lbT[:, :], lbT_ps[0:8, :], [lbT_ps], [lbT])
                if CUT < 3.9:
                    continue
                P.op("dve", lambda e: e.tensor_copy(cscol[:, :], small_a[:, 0:8]), [small_a], [cscol])
                P.op("dve", lambda e: e.tensor_copy(clb[:, :], small_b[:, 0:8]), [small_b], [clb])
                if CUT < 3.95:
                    continue
                P.op("act", lambda e: e.activation(ecs[:, :], cscol[:, :], AF.Exp), [cscol], [ecs])
                P.op("act", lambda e: e.activation(decb[:, :], clb[:, :], AF.Exp), [clb], [decb])
                P.op("dve", lambda e: e.tensor_tensor(wend[:, :], clb[:, :], cscol[:, :], ALU.subtract), [clb, cscol], [wend])
                P.op("act", lambda e: e.activation(wend[:, :], wend[:, :], AF.Exp), [wend], [wend])
                P.op("dve", lambda e: e.tensor_tensor(wkb[:, :], ecs[:, :], be16[:, d * 8:(d + 1) * 8], ALU.mult), [ecs, be16], [wkb])
                for h in range(8 if CUT >= 5 else 0):
                    kh = h // 2
                    if h % 2 == 0:
                        P.op("pe", lambda e, kh=kh: e.matmul(QK_ps[:, :], qkT[:, 4 + kh, :], qkT[:, kh, :], start=True, stop=True), [qkT], [QK_ps])
                        P.op("pe", lambda e, kh=kh: e.matmul(KK_ps[:, :], kn32[:, kh, :], kn32[:, kh, :], start=True, stop=True), [kn32], [KK_ps])
                    selh = cslice(C, "sel8", rows=8, lo=h * 128, n=128)
                    P.op("pe", lambda e, selh=selh: e.matmul(csb_ps[:, :], selh, csT[:, :], start=True, stop=True), [cons, csT], [csb_ps])
                    P.op("pe", lambda e, selh=selh: e.matmul(lbb_ps[:, :], selh, lbT[:, :], start=True, stop=True), [cons, lbT], [lbb_ps])
                    csc = cscol[:, h:h + 1]
                    P.op("dve", lambda e, csc=csc: e.scalar_tensor_tensor(tmp[0][:, :], csb_ps[:, :], csc, n_incl, ALU.subtract, ALU.add), [csb_ps, cscol, cons], [tmp[0]])
                    P.op("act", lambda e: e.activation(Ei[0][:, :], tmp[0][:, :], AF.Exp), [tmp[0]], [Ei[0]])
                    P.op("dve", lambda e: e.tensor_tensor(attT[:, :], QK_ps[:, :], Ei[0][:, :], ALU.mult), [QK_ps, Ei[0]], [attT])
                    P.op("dve", lambda e, csc=csc: e.scalar_tensor_tensor(tmp[1][:, :], lbb_ps[:, :], csc, n_N0, ALU.subtract, ALU.add), [lbb_ps, cscol, cons], [tmp[1]])
                    P.op("act", lambda e: e.activation(Ei[1][:, :], tmp[1][:, :], AF.Exp), [tmp[1]], [Ei[1]])
                    P.op("dve", lambda e: e.tensor_tensor(Nk[0][:, :], KK_ps[:, :], Ei[1][:, :], ALU.mult), [KK_ps, Ei[1]], [Nk[0]])
                    P.op("dve", lambda e: e.scalar_tensor_tensor(tmp[2][:, :], csb_ps[:, :], -1.0, n_M0, ALU.mult, ALU.add), [csb_ps, cons], [tmp[2]])
                    P.op("act", lambda e, csc=csc: e.activation(Ei[2][:, :], tmp[2][:, :], AF.Exp, bias=csc), [tmp[2], cscol], [Ei[2]])
                    P.op("dve", lambda e, h=h: e.scalar_tensor_tensor(Mk[0][:, :], KK_ps[:, :], be16[:, d * 8 + h:d * 8 + h + 1], Ei[2][:, :], ALU.mult, ALU.mult),
                         [KK_ps, be16, Ei[2]], [Mk[0]])
                    P.op("act", lambda e: e.activation(ecsb[:, :], csb_ps[:, :], AF.Exp), [csb_ps], [ecsb])
                    P.op("pool", lambda e, kh=kh: e.tensor_tensor(qeT[:, :], qkT[:, kh, :], ecsb[:, :], ALU.mult), [qkT, ecsb], [qeT])
                    P.op("dve", lambda e: e.tensor_tensor(Esb[:, :], Mk[0][:, :], cslice(C, "off"), ALU.mult), [Mk[0], cons], [Esb])
                    P.op("dve", lambda e: e.tensor_tensor(Mk[0][:, :], Mk[0][:, :], cslice(C, "bd"), ALU.mult), [Mk[0], cons], [Mk[0]])
                    P.op("pool", lambda e: e.tensor_tensor(Nk[0][:, :], Nk[0][:, :], cslice(C, "bd"), ALU.mult), [Nk[0], cons], [Nk[0]])
                    P.op("pool", lambda e: e.tensor_tensor(Uk[0][:, :], cslice(C, "ident"), Nk[0][:, :], ALU.subtract), [cons, Nk[0]], [Uk[0]])
                    cm_, cn_, cu_ = 0, 0, 0
                    for lv in range(1, 6):
                        Mo, No, Uo = Mk[cm_], Nk[cn_], Uk[cu_]
                        Mn, Nn, Un = Mk[1 - cm_], Nk[1 - cn_], Uk[1 - cu_]
                        P.op("pe", lambda e, Mo=Mo, No=No: e.matmul(M_ps[:, :], No[:, :], Mo[:, :], start=True, stop=True), [Mo, No], [M_ps])
                        if lv < 5:
                            P.op("pe", lambda e, Mo=Mo, No=No: e.matmul(N_ps[:, :], Mo[:, :], No[:, :], start=True, stop=True), [Mo, No], [N_ps])
                        act_copy(P, Mn[:, :], M_ps[:, :], [M_ps], [Mn])
                        if lv < 5:
                            P.op("dve", lambda e, Nn=Nn: e.tensor_copy(Nn[:, :], N_ps[:, :]), [N_ps], [Nn])
                        P.op("pe", lambda e, Mn=Mn, Uo=Uo: e.matmul(U_ps[:, :], Mn[:, :], Uo[:, :], start=True, stop=True), [Mn, Uo], [U_ps])
                        P.op("dve", lambda e, Un=Un, Uo=Uo: e.tensor_tensor(Un[:, :], U_ps[:, :], Uo[:, :], ALU.add), [U_ps, Uo], [Un])
                        cm_, cn_, cu_ = 1 - cm_, 1 - cn_, 1 - cu_
                    Ubd = Uk[cu_]
                    Ufin = Uk[1 - cu_]
                    P.op("pe", lambda e, Ubd=Ubd: e.matmul(M_ps[:, :], Ubd[:, :], Esb[:, :], start=True, stop=True), [Ubd, Esb], [M_ps])
                    P.op("pe", lambda e, Ubd=Ubd: e.matmul(N_ps[:, :], Esb[:, :], Ubd[:, :], start=True, stop=True), [Ubd, Esb], [N_ps])
                    act_copy(P, Fsb[:, :], M_ps[:, :], [M_ps], [Fsb])
                    P.op("dve", lambda e: e.tensor_copy(Gsb[:, :], N_ps[:, :]), [N_ps], [Gsb])
                    P.op("pe", lambda e: e.matmul(M_ps[:, :], Fsb[:, :], Gsb[:, :], start=True, stop=True), [Fsb, Gsb], [M_ps])
                    P.op("pe", lambda e: e.matmul(N_ps[:, :], Gsb[:, :], Fsb[:, :], start=True, stop=True), [Fsb, Gsb], [N_ps])
                    act_copy(P, F2sb[:, :], N_ps[:, :], [N_ps], [F2sb])
                    P.op("pool", lambda e: e.tensor_tensor(W1[:, :], cslice(C, "ident"), Gsb[:, :], ALU.subtract), [cons, Gsb], [W1])
                    P.op("dve", lambda e: e.tensor_tensor(W1[:, :], W1[:, :], M_ps[:, :], ALU.add), [W1, M_ps], [W1])
                    P.op("pe", lambda e: e.matmul(U_ps[:, :], F2sb[:, :], Gsb[:, :], start=True, stop=True), [F2sb, Gsb], [U_ps])
                    P.op("dve", lambda e: e.tensor_tensor(W1[:, :], W1[:, :], U_ps[:, :], ALU.subtract), [W1, U_ps], [W1])
                    P.op("pe", lambda e, Ubd=Ubd: e.matmul(M_ps[:, :], Ubd[:, :], cslice(C, "ident"), start=True, stop=True), [Ubd, cons], [M_ps])
                    act_copy(P, Tbd[:, :], M_ps[:, :], [M_ps], [Tbd])
                    P.op("pe", lambda e: e.matmul(U_ps[:, :], Tbd[:, :], W1[:, :], start=True, stop=True), [Tbd, W1], [U_ps])
                    P.op("dve", lambda e, Ufin=Ufin: e.tensor_copy(Ufin[:, :], U_ps[:, :]), [U_ps], [Ufin])
                    cu_ = 1 - cu_
                    U = Uk[cu_]
                    if CUT < 7:
                        continue
                    P.op("pool", lambda e, h=h: e.tensor_scalar(vb[:, :], v_tok[:, h, :], be16[:, d * 8 + h:d * 8 + h + 1], None, ALU.mult), [v_tok, be16], [vb])
                    P.op("pool", lambda e, h=h, kh=kh: e.tensor_scalar(kb[:, :], k_tok[:, kh, :], wkb[:, h:h + 1], None, ALU.mult), [k_tok, wkb], [kb])
                    P.op("pool", lambda e, h=h, kh=kh: e.tensor_scalar(kend[:, :], k_tok[:, kh, :], wend[:, h:h + 1], None, ALU.mult), [k_tok, wend], [kend])
                    P.op("pe", lambda e, U=U: e.matmul(u_ps[:, :], U[:, :], vb[:, :], start=True, stop=True), [U, vb], [u_ps])
                    P.op("pe", lambda e, U=U: e.matmul(wT_ps[:, :], kb[:, :], U[:, :], start=True, stop=True), [U, kb], [wT_ps])
                    act_copy(P, u_sb[:, :], u_ps[:, :], [u_ps], [u_sb])
                    act_copy(P, wT[:, :], wT_ps[:, :], [wT_ps], [wT])
                    P.op("pe", lambda e, h=h: e.matmul(vn_ps[:, :], wT[:, :], S_bf[:, h, :], start=True, stop=True), [wT, S_bf], [vn_ps])
                    P.op("dve", lambda e: e.tensor_tensor(vnew[:, :], u_sb[:, :], vn_ps[:, :], ALU.subtract), [u_sb, vn_ps], [vnew])
                    P.op("pe", lambda e, h=h: e.matmul(o_ps[:, :], qeT[:, :], S_bf[:, h, :], start=True, stop=False), [qeT, S_bf], [o_ps])
                    P.op("pe", lambda e: e.matmul(o_ps[:, :], attT[:, :], vnew[:, :], start=False, stop=True), [attT, vnew], [o_ps])
                    act_copy(P, ys[:, h * 128:(h + 1) * 128], o_ps[:, :], [o_ps], [ys])
                    P.op("pe", lambda e: e.matmul(S_ps[:, :], kend[:, :], vnew[:, :], start=True, stop=True), [kend, vnew], [S_ps])
                    P.op("dve", lambda e, h=h: e.scalar_tensor_tensor(S_f[:, h, :], S_f[:, h, :], decb[:, h:h + 1], S_ps[:, :], ALU.mult, ALU.add),
                         [S_f, decb, S_ps], [S_f])
                    act_copy(P, S_bf[:, h, :], S_f[:, h, :], [S_f], [S_bf])
                P.dma("pool", T["yd"].t[d, s:s + 128, :], ys[:, :], T["yd"], ys)
            cic[0] = ci

        run_dir(0)
        run_dir(1)


def phase_odd_epilogue(C):
    P, cfg, T = C.P, C.cfg, C.T
    with Scope(P):
        nw = P.sbuf("onw", [128, 128], F32)
        P.dma("sp", nw[:, :], T["od_vec"].t[32:160].partition_broadcast(128), nw, T["od_vec"])
        NB = 2
        yf = [P.sbuf(f"oyf{i}", [128, 1024], F32) for i in range(NB)]
        yr = [P.sbuf(f"oyr{i}", [128, 1024], F32) for i in range(NB)]
        zg = [P.sbuf(f"ozg{i}", [128, 1024], F32) for i in range(NB)]
        sz = P.sbuf("osz", [128, 1024], F32)
        junk = P.sbuf("ojunk", [128, 128], F32)
        ymix = P.sbuf("oymix", [128, 1024], BF16)
        ss = P.sbuf("oss", [128, 8], F32)
        rstd = P.sbuf("orstd", [128, 8], F32)
        yT = [P.sbuf(f"oyT{i}", [128, 8, 128], BF16) for i in range(2)]
        pst = [P.psum(f"oetp{i}", [128, 4, 128], BF16) for i in range(2)]
        for ci in range(cfg.TA // 128):
            s = ci * 128
            a, b, z = yf[ci % NB], yr[ci % NB], zg[ci % NB]
            P.dma("sp", a[:, :], T["yd"].t[0, s:s + 128, :], a, T["yd"])
            P.dma("sp", b[:, :], T["yd"].t[1, s:s + 128, :], b, T["yd"])
            P.dma("sp", z[:, :], T["tm_pre"].t[s:s + 128, 0:1024], z, T["tm_pre"])
            P.op("dve", lambda e, a=a, b=b: e.tensor_tensor(a[:, :], a[:, :], b[:, :], ALU.add), [a, b], [a])
            P.op("act", lambda e, z=z: e.activation(sz[:, :], z[:, :], AF.Silu), [z], [sz])
            for h in range(8):
                sl = slice(h * 128, (h + 1) * 128)
                P.op("act", lambda e, a=a, sl=sl, h=h: e.activation(junk[:, :], a[:, sl], AF.Square, accum_out=ss[:, h:h + 1]), [a], [junk, ss])
            P.op("dve", lambda e: e.tensor_scalar(rstd[:, :], ss[:, :], 1.0 / 128.0, EPS, ALU.mult, ALU.add), [ss], [rstd])
            P.op("act", lambda e: e.activation(rstd[:, :], rstd[:, :], AF.Sqrt), [rstd], [rstd])
            P.op("dve", lambda e: e.reciprocal(rstd[:, :], rstd[:, :]), [rstd], [rstd])
            for h in range(8):
                sl = slice(h * 128, (h + 1) * 128)
                P.op("dve", lambda e, a=a, sl=sl, h=h: e.scalar_tensor_tensor(a[:, sl], a[:, sl], rstd[:, h:h + 1], nw[:, :], ALU.mult, ALU.mult), [a, rstd, nw], [a])
            P.op("pool", lambda e, a=a: e.tensor_tensor(ymix[:, :], a[:, :], sz[:, :], ALU.mult), [a, sz], [ymix])
            yt = yT[ci % 2]
            transpose_rows(C, ymix, 128, 1024, yt, pst)
            store_yT(C, yt, s)


ALL_STAGES = ("ada", "vecs", "prenorm0", "inproj0", "mixer0", "tail0", "prenorm1", "inproj1", "mixer1", "tail1")


def kernel(**inputs):
    cfg = Cfg()
    inp = {k: np.asarray(v) for k, v in inputs.items()}
    nc, C = build_program(cfg, stages=ALL_STAGES, dumps=())
    in_maps = make_in_maps(cfg, inp)
    res = run_bass_kernel_spmd(nc, in_maps, core_ids=list(range(NCORES)))
    out = np.stack([np.concatenate([np.asarray(res.results[b * 4 + g]["out"]).reshape(cfg.NTL, cfg.D) for g in range(4)], 0)
                    for b in range(2)])
    return np.ascontiguousarray(out.astype(np.float32))
```
